# Optimizing a Trainium2 kernel written in Bass

```python
import math, functools
import jax, jax.numpy as jnp
from jax import lax
import numpy as np

D_MODEL = 2048
BATCH = 8
SEQ = 4096
DEPTH = 4

GRID_W = 64
CTX_LEN = 256
CHUNK = 128
VW = 2048
H_R = 8
R_DK = 128
R_DV = VW // H_R
H_M = 8
M_DK = 128
M_DV = VW // H_M
R_QK = H_R * R_DK
M_QK = H_M * M_DK
N_GATES = 4 * H_M
IN_SIZES = (R_QK, R_QK, VW, VW, M_QK, M_QK, VW, VW, N_GATES, 2 * D_MODEL)
IN_COLS = 2 * R_QK + 2 * VW + 2 * M_QK + 2 * VW + N_GATES + 2 * D_MODEL
D_FF = 5632
CONV_W = 3
N_MOD = 6
ROPE_BASE = 10000.0
EPS = 1e-6

kernel_name = 'hybrid_retention_mlstm_convffn_dit'


def rms_norm(x, g):
    xf = x.astype(jnp.float32)
    y = xf * lax.rsqrt(jnp.mean(xf * xf, axis=-1, keepdims=True) + EPS)
    return (y * g.astype(jnp.float32)).astype(x.dtype)


def head_rms(y, g):
    H, d = y.shape[1], y.shape[3]
    y = y * lax.rsqrt(jnp.mean(y * y, axis=-1, keepdims=True) + EPS)
    return y * g.astype(jnp.float32).reshape(H, 1, d)


def modulate(h, shift, scale):
    return h * (1 + scale) + shift


def dwconv(x, w, b):
    T = x.shape[1]
    half = w.shape[0] // 2
    xp = jnp.pad(x, ((0, 0), (half, half), (0, 0)))
    out = b
    for k in range(w.shape[0]):
        out = out + xp[:, k:k + T] * w[k]
    return out


def split_cols(p, sizes):
    offs, o = [], 0
    for s in sizes[:-1]:
        o += s
        offs.append(o)
    return jnp.split(p, offs, axis=-1)


def split_heads(x, h):
    B, T, _ = x.shape
    return x.reshape(B, T, h, -1).transpose(0, 2, 1, 3)


def merge_heads(x):
    B, H, T, d = x.shape
    return x.transpose(0, 2, 1, 3).reshape(B, T, H * d)


def rope_1d(x, pos):
    half = x.shape[-1] // 2
    freqs = ROPE_BASE ** (-jnp.arange(half, dtype=jnp.float32) / half)
    ang = pos[:, None] * freqs[None, :]
    cos, sin = jnp.cos(ang).astype(x.dtype), jnp.sin(ang).astype(x.dtype)
    x1, x2 = x[..., :half], x[..., half:]
    return jnp.concatenate([x1 * cos - x2 * sin, x1 * sin + x2 * cos], axis=-1)


def axial_rope(x, rows, cols):
    half = x.shape[-1] // 2
    return jnp.concatenate([rope_1d(x[..., :half], rows), rope_1d(x[..., half:], cols)], axis=-1)


def to_chunks(a):
    B, H, T = a.shape[:3]
    a = a.reshape((B, H, T // CHUNK, CHUNK) + a.shape[3:])
    return jnp.moveaxis(a, 2, 0)


def from_chunks(a):
    a = jnp.moveaxis(a, 0, 2)
    B, H, N, L = a.shape[:4]
    return a.reshape((B, H, N * L) + a.shape[4:])


def retention_scan(q, k, v, state0, log_gamma):
    q, k, v = (a.astype(jnp.float32) for a in (q, k, v))
    idx = jnp.arange(CHUNK, dtype=jnp.float32)
    lg = log_gamma.astype(jnp.float32)[:, None, None]
    rel = idx[:, None] - idx[None, :]
    intra = jnp.where(rel >= 0, jnp.exp(jnp.maximum(rel, 0.0) * lg), 0.0)
    q_dec = jnp.exp((idx + 1.0)[:, None] * lg)
    k_dec = jnp.exp((CHUNK - 1.0 - idx)[:, None] * lg)
    chunk_dec = jnp.exp(CHUNK * lg)

    def step(S, inp):
        qc, kc, vc = inp
        s = jnp.einsum('bhid,bhjd->bhij', qc, kc) * intra
        y = jnp.einsum('bhij,bhjv->bhiv', s, vc) + jnp.einsum('bhid,bhdv->bhiv', qc * q_dec, S)
        S = chunk_dec * S + jnp.einsum('bhjd,bhjv->bhdv', kc * k_dec, vc)
        return S, y

    S, ys = lax.scan(step, state0, (to_chunks(q), to_chunks(k), to_chunks(v)))
    return from_chunks(ys), S


def mlstm_scan(q, k, v, ig, lf, state0):
    q, k, v = (a.astype(jnp.float32) for a in (q, k, v))
    causal = jnp.tril(jnp.ones((CHUNK, CHUNK), dtype=bool))

    def step(carry, inp):
        C, n, m = carry
        qc, kc, vc, ic, fc = inp
        b = jnp.cumsum(fc, axis=-1)
        d = jnp.where(causal, b[..., :, None] - b[..., None, :] + ic[..., None, :], -jnp.inf)
        inter = b + m[..., None]
        m_t = jnp.maximum(d.max(axis=-1), inter)
        s = jnp.einsum('bhid,bhjd->bhij', qc, kc) * jnp.exp(d - m_t[..., None])
        a = jnp.exp(inter - m_t)
        num = jnp.einsum('bhij,bhjv->bhiv', s, vc) + a[..., None] * jnp.einsum('bhid,bhdv->bhiv', qc, C)
        den = s.sum(axis=-1) + a * jnp.einsum('bhid,bhd->bhi', qc, n)
        h = num / jnp.maximum(jnp.abs(den), jnp.exp(-m_t))[..., None]
        b_end = b[..., -1]
        loc = b_end[..., None] - b + ic
        m_new = jnp.maximum(b_end + m, loc.max(axis=-1))
        w = jnp.exp(loc - m_new[..., None])
        decay = jnp.exp(b_end + m - m_new)
        C = decay[..., None, None] * C + jnp.einsum('bhj,bhjd,bhjv->bhdv', w, kc, vc)
        n = decay[..., None] * n + jnp.einsum('bhj,bhjd->bhd', w, kc)
        return (C, n, m_new), h

    state, hs = lax.scan(step, state0, tuple(to_chunks(a) for a in (q, k, v, ig, lf)))
    return from_chunks(hs), state


def run_bidirectional(scan_f, scan_b, init, ctx_f, lat_f, ctx_b, lat_b):
    rev = lambda xs: tuple(jnp.flip(a, axis=2) for a in xs)
    yc_f, st_f = scan_f(*ctx_f, init)
    yl_f, _ = scan_f(*lat_f, st_f)
    yc_b, st_b = scan_b(*rev(ctx_b), init)
    yl_b, _ = scan_b(*rev(lat_b), st_b)
    return yc_f + jnp.flip(yc_b, axis=2), yl_f + jnp.flip(yl_b, axis=2)


def project_stream(h, w_in, conv_w, conv_b, gate_b, rope_pos):
    B, T, _ = h.shape
    rq, rk, rv, rg, mq, mk, mv, mo, mg, merge = split_cols(h @ w_in, IN_SIZES)
    mqk = jax.nn.silu(dwconv(jnp.concatenate([mq, mk], axis=-1), conv_w, conv_b))
    mq, mk = mqk[..., :M_QK], mqk[..., M_QK:]
    rq = split_heads(rq, H_R) * (R_DK ** -0.5)
    rk = split_heads(rk, H_R)
    if rope_pos is not None:
        rq = axial_rope(rq, *rope_pos)
        rk = axial_rope(rk, *rope_pos)
    ret = (rq, rk, split_heads(rv, H_R))
    gates = (mg.astype(jnp.float32) + gate_b.astype(jnp.float32).reshape(-1))
    gates = gates.reshape(B, T, 4, H_M).transpose(2, 0, 3, 1)
    mq_h = split_heads(mq, H_M) * (M_DK ** -0.5)
    mk_h = split_heads(mk, H_M)
    mv_h = split_heads(mv, H_M)
    ml_f = (mq_h, mk_h, mv_h, gates[0], jax.nn.log_sigmoid(gates[1]))
    ml_b = (mq_h, mk_h, mv_h, gates[2], jax.nn.log_sigmoid(gates[3]))
    return ret, ml_f, ml_b, (rg, mo, merge)


def branch_merge(yr, ym, gates, head_norm_w, w_ret_out, w_mlstm_out, w_o, dtype):
    rg, mo, merge = gates
    yr = merge_heads(head_rms(yr, head_norm_w[0])).astype(dtype) * jax.nn.silu(rg)
    ym = merge_heads(head_rms(ym, head_norm_w[1])).astype(dtype) * jax.nn.sigmoid(mo)
    gr, gm = jnp.split(merge, 2, axis=-1)
    y = jax.nn.sigmoid(gr) * (yr @ w_ret_out) + jax.nn.sigmoid(gm) * (ym @ w_mlstm_out)
    return y @ w_o


def token_mixer(hc, hl, w_in, conv_w, conv_b, gate_b, decay_exp, head_norm_w,
                w_ret_out, w_mlstm_out, w_o, rope_pos, with_ctx_out):
    ret_c, mf_c, mb_c, g_c = project_stream(hc, w_in, conv_w, conv_b, gate_b, None)
    ret_l, mf_l, mb_l, g_l = project_stream(hl, w_in, conv_w, conv_b, gate_b, rope_pos)
    B = hl.shape[0]
    log_gamma = jnp.log1p(-jnp.exp2(-decay_exp.astype(jnp.float32)))
    r_init = jnp.zeros((B, H_R, R_DK, R_DV), jnp.float32)
    yr_c, yr_l = run_bidirectional(functools.partial(retention_scan, log_gamma=log_gamma[0]),
                                   functools.partial(retention_scan, log_gamma=log_gamma[1]),
                                   r_init, ret_c, ret_l, ret_c, ret_l)
    m_init = (jnp.zeros((B, H_M, M_DK, M_DV), jnp.float32),
              jnp.zeros((B, H_M, M_DK), jnp.float32),
              jnp.zeros((B, H_M), jnp.float32))
    ym_c, ym_l = run_bidirectional(mlstm_scan, mlstm_scan, m_init, mf_c, mf_l, mb_c, mb_l)
    out_l = branch_merge(yr_l, ym_l, g_l, head_norm_w, w_ret_out, w_mlstm_out, w_o, hl.dtype)
    out_c = None
    if with_ctx_out:
        out_c = branch_merge(yr_c, ym_c, g_c, head_norm_w, w_ret_out, w_mlstm_out, w_o, hc.dtype)
    return out_c, out_l


def conv_ffn(h, w_up, conv_w, conv_b, w_down):
    u = dwconv(h @ w_up, conv_w, conv_b)
    a, g = jnp.split(u, 2, axis=-1)
    return (jax.nn.silu(a) * g) @ w_down


def setup_inputs(seed: int = 0) -> dict:
    key = jax.random.key(seed)
    ks = jax.random.split(key, 20)
    f32 = jnp.float32
    nrm = lambda k, shape, s: jax.random.normal(k, shape, f32) * s
    gate_base = jnp.stack([jnp.zeros((H_M,), f32), jnp.linspace(3.0, 6.0, H_M, dtype=f32),
                           jnp.zeros((H_M,), f32), jnp.linspace(3.0, 6.0, H_M, dtype=f32)])
    decay_base = 5.0 + jnp.arange(H_R, dtype=f32)
    return {
        'x': nrm(ks[0], (BATCH, SEQ, D_MODEL), 1.0),
        'c': nrm(ks[1], (BATCH, D_MODEL), 1.0),
        'ctx': nrm(ks[2], (BATCH, CTX_LEN, D_MODEL), 1.0),
        'c_ctx': nrm(ks[3], (D_MODEL,), 1.0),
        'w_ada': nrm(ks[4], (DEPTH, D_MODEL, N_MOD * D_MODEL), 0.5 * D_MODEL ** -0.5),
        'b_ada': nrm(ks[5], (DEPTH, N_MOD * D_MODEL), 0.02),
        'norm_w': 1.0 + nrm(ks[6], (DEPTH, 4, D_MODEL), 0.02),
        'w_in': nrm(ks[7], (DEPTH, D_MODEL, IN_COLS), D_MODEL ** -0.5),
        'mlstm_conv_w': nrm(ks[8], (DEPTH, CONV_W, 2 * M_QK), CONV_W ** -0.5),
        'mlstm_conv_b': nrm(ks[9], (DEPTH, 2 * M_QK), 0.02),
        'mlstm_gate_b': gate_base[None] + nrm(ks[10], (DEPTH, 4, H_M), 0.1),
        'ret_decay_exp': decay_base[None, None] + nrm(ks[11], (DEPTH, 2, H_R), 0.1),
        'head_norm_w': 1.0 + nrm(ks[12], (DEPTH, 2, VW), 0.02),
        'w_ret_out': nrm(ks[13], (DEPTH, VW, D_MODEL), VW ** -0.5),
        'w_mlstm_out': nrm(ks[14], (DEPTH, VW, D_MODEL), VW ** -0.5),
        'w_o': nrm(ks[15], (DEPTH, D_MODEL, D_MODEL), D_MODEL ** -0.5),
        'w_up': nrm(ks[16], (DEPTH, D_MODEL, 2 * D_FF), D_MODEL ** -0.5),
        'ffn_conv_w': nrm(ks[17], (DEPTH, CONV_W, 2 * D_FF), CONV_W ** -0.5),
        'ffn_conv_b': nrm(ks[18], (DEPTH, 2 * D_FF), 0.02),
        'w_down': nrm(ks[19], (DEPTH, D_FF, D_MODEL), D_FF ** -0.5),
    }


def reference(x, c, ctx, c_ctx, w_ada, b_ada, norm_w, w_in, mlstm_conv_w, mlstm_conv_b,
              mlstm_gate_b, ret_decay_exp, head_norm_w, w_ret_out, w_mlstm_out, w_o,
              w_up, ffn_conv_w, ffn_conv_b, w_down):
    T = x.shape[1]
    ROWS = T // GRID_W
    rows = jnp.repeat(jnp.arange(ROWS, dtype=jnp.float32), GRID_W)
    cols = jnp.tile(jnp.arange(GRID_W, dtype=jnp.float32), ROWS)
    s_c = jax.nn.silu(c)
    s_cc = jax.nn.silu(c_ctx)
    xl, xc = x, ctx
    for l in range(DEPTH):
        last = l == DEPTH - 1
        sh1, sc1, g1, sh2, sc2, g2 = jnp.split((s_c @ w_ada[l] + b_ada[l])[:, None, :], N_MOD, axis=-1)
        csh1, csc1, cg1, csh2, csc2, cg2 = jnp.split(s_cc @ w_ada[l] + b_ada[l], N_MOD, axis=-1)
        hl = modulate(rms_norm(xl, norm_w[l, 0]), sh1, sc1)
        hc = modulate(rms_norm(xc, norm_w[l, 0]), csh1, csc1)
        out_c, out_l = token_mixer(hc, hl, w_in[l], mlstm_conv_w[l], mlstm_conv_b[l], mlstm_gate_b[l],
                                   ret_decay_exp[l], head_norm_w[l], w_ret_out[l], w_mlstm_out[l], w_o[l],
                                   (rows, cols), not last)
        xl = xl + g1 * rms_norm(out_l, norm_w[l, 1])
        hl = modulate(rms_norm(xl, norm_w[l, 2]), sh2, sc2)
        xl = xl + g2 * rms_norm(conv_ffn(hl, w_up[l], ffn_conv_w[l], ffn_conv_b[l], w_down[l]), norm_w[l, 3])
        if not last:
            xc = xc + cg1 * rms_norm(out_c, norm_w[l, 1])
            hc = modulate(rms_norm(xc, norm_w[l, 2]), csh2, csc2)
            xc = xc + cg2 * rms_norm(conv_ffn(hc, w_up[l], ffn_conv_w[l], ffn_conv_b[l], w_down[l]), norm_w[l, 3])
    return xl
```

```python
import numpy as np
import ml_dtypes
import concourse.bass as bass
import concourse.mybir as mybir
from concourse.bass_utils import run_bass_kernel_spmd

F32 = mybir.dt.float32
BF16 = mybir.dt.bfloat16
ALU = mybir.AluOpType
AF = mybir.ActivationFunctionType

D = 2048
KC = 16
HD = 8
EPS = 1e-6
NEG = -30000.0
CFG = dict(TC=256, TL=4096, L=4, FF=5632, NCORES=8, dbg=False, stop=None)


class Buf:
    __slots__ = ("w", "r")

    def __init__(self):
        self.w = None
        self.r = {}


class Eng:
    def __init__(self, name, sem):
        self.name = name
        self.sem = sem
        self.cnt = 0
        self.q = []
        self.known = {}


class K:
    def __init__(self, nc):
        self.nc = nc
        self.E = {n: Eng(n, nc.alloc_semaphore(name="p_" + n)) for n in ("pe", "act", "dve", "pool", "sp")}
        self.dsem = {}
        self.freel = []
        self.allsem = []

    def dma_sem(self, name):
        if name not in self.dsem:
            if self.freel:
                self.dsem[name] = self.freel.pop()
            else:
                self.dsem[name] = [self.nc.alloc_semaphore(name="d_%d" % len(self.allsem)), 0]
                self.allsem.append(self.dsem[name])
        return name

    def _waits(self, eng, reads, writes):
        deps = {}

        def add(t):
            if t is None:
                return
            s, v = t
            if deps.get(s, 0) < v:
                deps[s] = v

        for b in reads:
            add(b.w)
        for b in writes:
            add(b.w)
            for s, v in b.r.items():
                add((s, v))
        for s, v in deps.items():
            if s is eng.sem:
                if eng.name == "pe" or v > eng.cnt:
                    continue
            if eng.known.get(s, 0) >= v:
                continue
            eng.known[s] = v
            eng.q.append(("w", s, v))

    def _mark(self, tag, reads, writes):
        s, v = tag
        for b in reads:
            if b.r.get(s, 0) < v:
                b.r[s] = v
        for b in writes:
            b.w = tag
            b.r = {}

    def op(self, en, fn, reads=(), writes=(), inc=True):
        eng = self.E[en]
        self._waits(eng, reads, writes)
        if inc:
            eng.cnt += 1
            eng.q.append(("i", fn, eng.sem))
            tag = (eng.sem, eng.cnt)
        else:
            eng.q.append(("n", fn))
            tag = (eng.sem, eng.cnt + 1)
        self._mark(tag, reads, writes)

    def dma(self, en, out, in_, reads=(), writes=(), sem=None, **kw):
        eng = self.E[en]
        self.dma_sem(sem)
        self._waits(eng, reads, writes)
        d = self.dsem[sem]
        d[1] += 16
        eng.q.append(("d", out, in_, d[0], kw))
        self._mark((d[0], d[1]), reads, writes)

    def barrier(self):
        for en, eng in self.E.items():
            for en2, e2 in self.E.items():
                if e2 is eng or e2.cnt == 0:
                    continue
                if eng.known.get(e2.sem, 0) < e2.cnt:
                    eng.known[e2.sem] = e2.cnt
                    eng.q.append(("w", e2.sem, e2.cnt))
            for d in self.allsem:
                if d[1] > 0 and eng.known.get(d[0], 0) < d[1]:
                    eng.known[d[0]] = d[1]
                    eng.q.append(("w", d[0], d[1]))
        self.freel = list(self.allsem)
        self.dsem = {}

    def emit(self):
        nc = self.nc
        with nc.Block() as block:
            def run(eng):
                def f(e):
                    for it in eng.q:
                        kk = it[0]
                        if kk == "w":
                            e.wait_ge(it[1], it[2])
                        elif kk == "i":
                            it[1](e).then_inc(it[2], 1)
                        elif kk == "n":
                            it[1](e)
                        else:
                            e.dma_start(out=it[1], in_=it[2], **it[4]).then_inc(it[3], 16)
                return f
            block.tensor(run(self.E["pe"]))
            block.scalar(run(self.E["act"]))
            block.vector(run(self.E["dve"]))
            block.gpsimd(run(self.E["pool"]))
            block.sync(run(self.E["sp"]))


class Ring:
    def __init__(self, name, aps):
        self.name = name
        self.aps = aps
        self.bufs = [Buf() for _ in aps]
        self.i = -1

    def next(self):
        self.i = (self.i + 1) % len(self.aps)
        return self.aps[self.i], self.bufs[self.i], "%s%d" % (self.name, self.i)


def host_consts(TL):
    ident = np.eye(128, dtype=np.float32)
    perm = np.zeros((128, 128), np.float32)
    for m in range(128):
        partner = m + 32 if (m % 64) < 32 else m - 32
        perm[partner, m] = 1.0
    j = np.arange(128, dtype=np.float32)[:, None]
    i = np.arange(128, dtype=np.float32)[None, :]
    one = np.ones((128, 128), np.float32)
    blocks = [
        ident,
        np.maximum(i - j, 0.0),
        np.maximum(j - i, 0.0),
        (i >= j).astype(np.float32),
        (j >= i).astype(np.float32),
        (i + 1.0) * one,
        (128.0 - i) * one,
        np.where(j <= i, 0.0, NEG).astype(np.float32),
        np.where(j >= i, 0.0, NEG).astype(np.float32),
        (127.0 - j) * one,
        j * one,
    ]
    cF = np.concatenate(blocks, axis=1).astype(np.float32)
    t = np.arange(TL)
    rows = (t // 64).astype(np.float32)
    cols = (t % 64).astype(np.float32)
    half = 32
    freqs = (10000.0 ** (-np.arange(half, dtype=np.float32) / half)).astype(np.float32)
    rope = np.zeros((128, 2, TL), np.float32)
    for p in range(128):
        pos = rows if p < 64 else cols
        ang = (pos * freqs[p % 32]).astype(np.float32)
        rope[p, 0] = np.cos(ang)
        rope[p, 1] = np.sin(ang) * (-1.0 if (p % 64) < 32 else 1.0)
    return dict(cI=ident.astype(ml_dtypes.bfloat16), cP=perm.astype(ml_dtypes.bfloat16), cF=cF, cROPE=rope)


def build(cfg):
    TC, TL, L, FF, dbg, stop = cfg["TC"], cfg["TL"], cfg["L"], cfg["FF"], cfg["dbg"], cfg["stop"]
    TT = TC + TL
    NCH = TT // 128
    NCC = TC // 128
    FC = FF // 128
    INC = 16416
    nc = bass.Bass("TRN2", target_bir_lowering=False)
    k = K(nc)

    def din(name, shape, dt=F32):
        return nc.dram_tensor(name, list(shape), dt, kind="ExternalInput").ap()

    def dscr(name, shape, dt):
        return nc.dram_tensor(name, list(shape), dt, kind=("ExternalOutput" if dbg else "Internal")).ap()

    x_in = din("x", [TL, D])
    ctx_in = din("ctx", [TC, D])
    cvec = din("cvec", [2, D])
    w_ada = din("w_ada", [L, D, 6 * D])
    b_ada = din("b_ada", [L, 6 * D])
    norm_w = din("norm_w", [L, 4, D])
    w_in = din("w_in", [L, D, INC])
    mconv_w = din("mlstm_conv_w", [L, 3, 2048])
    mconv_b = din("mlstm_conv_b", [L, 2048])
    gate_b = din("mlstm_gate_b", [L, 4, 8])
    decay_e = din("ret_decay_exp", [L, 2, 8])
    hnorm_w = din("head_norm_w", [L, 2, 2048])
    w_ro = din("w_ret_out", [L, D, D])
    w_mo = din("w_mlstm_out", [L, D, D])
    w_o = din("w_o", [L, D, D])
    w_up = din("w_up", [L, D, 2 * FF])
    fconv_w = din("ffn_conv_w", [L, 3, 2 * FF])
    fconv_b = din("ffn_conv_b", [L, 2 * FF])
    w_dn = din("w_down", [L, FF, D])
    cI_d = din("cI", [128, 128], BF16)
    cP_d = din("cP", [128, 128], BF16)
    cF_d = din("cF", [128, 11 * 128])
    cR_d = din("cROPE", [128, 2, TL])
    y_out = nc.dram_tensor("y", [TL, D], F32, kind="ExternalOutput").ap()

    X = dscr("X", [TT, D], F32)
    QR = dscr("QR", [1024, TT], BF16)
    KR = dscr("KR", [1024, TT], BF16)
    QM = dscr("QM", [1024, TT], BF16)
    KM = dscr("KM", [1024, TT], BF16)
    VR = dscr("VR", [TT, 2048], BF16)
    VM = dscr("VM", [TT, 2048], BF16)
    RG = dscr("RG", [2048, TT], BF16)
    MO = dscr("MO", [2048, TT], BF16)
    GRM = dscr("GRM", [4096, TT], BF16)
    GATES = dscr("GATES", [4, 8, TT], F32)
    NEGMX = dscr("NEGMX", [16, TT], F32)
    CHT = dscr("CHT", [2, 16, NCH], F32)
    SBR = dscr("SBR", [NCH, 128, 8, 256], BF16)
    SBM = dscr("SBM", [NCH, 128, 8, 257], BF16)
    YRT = dscr("YRT", [2048, TT], BF16)
    YMT = dscr("YMT", [2048, TT], BF16)
    ZR = dscr("ZR", [2048, TT], F32)
    YT = dscr("YT", [2048, TT], BF16)
    ACTT = dscr("ACTT", [FF, TT], BF16)
    FO = dscr("FO", [TT, D], F32)
    GNROW = dscr("GNROW", [4, D], F32)

    cnt = [0]
    cur = [0]

    ARENA_BYTES = 206 * 1024
    arena = nc.alloc_sbuf_tensor("arena", [128, ARENA_BYTES], mybir.dt.uint8).ap()

    def sb(shape, dt):
        nel = int(np.prod(shape[1:]))
        nbytes = nel * (4 if dt == F32 else 2)
        off = cur[0]
        cur[0] += (nbytes + 63) // 64 * 64
        assert cur[0] <= ARENA_BYTES, ("sbuf overflow", cur[0])
        flat = arena[0:shape[0], off:off + nbytes].bitcast(dt)
        if len(shape) == 2:
            return flat
        names = ["a%d" % i for i in range(len(shape) - 1)]
        pat = "p (%s) -> p %s" % (" ".join(names), " ".join(names))
        return flat.rearrange(pat, **{n: int(v) for n, v in zip(names[1:], shape[2:])})

    psb = [nc.alloc_psum_tensor("ps%d" % i, [128, 512], F32).ap() for i in range(8)]
    psB = [Buf() for _ in range(8)]

    cI = sb([128, 128], BF16)
    cP = sb([128, 128], BF16)
    cF = sb([128, 11 * 128], F32)
    bI, bP, bF = Buf(), Buf(), Buf()
    k.dma("sp", cI, cI_d, writes=[bI], sem="c0")
    k.dma("sp", cP, cP_d, writes=[bP], sem="c1")
    k.dma("sp", cF, cF_d, writes=[bF], sem="c2")

    def CF(i):
        return cF[:, i * 128:(i + 1) * 128]

    ones_bf = sb([128, 128], BF16)
    b_ones = Buf()
    k.op("pool", lambda e: e.memset(ones_bf, 1.0), writes=[b_ones])
    sT = sb([128, KC, 2], BF16)
    sRep = sb([128, 2, KC, 128], BF16)
    b_sT, b_sRep = Buf(), Buf()
    MODS = sb([128, 8, KC], F32)
    b_MODS = Buf()
    persist_end = cur[0]

    k.dma("sp", X[0:TC, :], ctx_in, sem="c3")
    k.dma("sp", X[TC:TT, :], x_in, sem="c4")

    cur[0] = persist_end
    cv32 = sb([128, KC, 2], F32)
    b_cv = Buf()
    for m in range(2):
        k.dma("sp", cv32[:, :, m], cvec[m, :].rearrange("(kc p) -> p kc", p=128), writes=[b_cv], sem="c5",
              allow_slow_non_contiguous=True)
    k.op("act", lambda e: e.activation(out=sT, in_=cv32, func=AF.Silu), reads=[b_cv], writes=[b_sT])
    for m in range(2):
        k.op("dve", lambda e, m=m: e.tensor_copy(out=sRep[:, m], in_=sT[:, :, m:m + 1].to_broadcast([128, KC, 128])),
             reads=[b_sT], writes=[b_sRep])
    k.barrier()

    segs = [(0, TC)] + [(TC + i * 512, 512) for i in range(TL // 512)]

    def wslab_src(w2d, c0, ncols):
        return w2d[:, c0:c0 + ncols].rearrange("(kc p) n -> p kc n", p=128)

    def phase_adaln(l):
        cur[0] = persist_end
        wr = Ring("wa", [sb([128, KC, 1536], BF16) for _ in range(2)])
        brow = sb([1, 6 * D], BF16)
        b_brow = Buf()
        k.dma("pool", brow, b_ada[l:l + 1, :], writes=[b_brow], sem="ab")
        nwf = sb([128, 2, KC], F32)
        b_nwf = Buf()
        for i, r in enumerate((0, 2)):
            k.dma("sp", nwf[:, i, :], norm_w[l, r, :].rearrange("(kc p) -> p kc", p=128), writes=[b_nwf], sem="an",
                  allow_slow_non_contiguous=True)
        nwb = sb([128, 2, D], F32)
        b_nwb = Buf()
        for i, r in enumerate((1, 3)):
            k.dma("sp", nwb[:, i, :], norm_w[l, r:r + 1, :].partition_broadcast(128), writes=[b_nwb], sem="anb")
        modfm = sb([128, 96, 2], F32)
        b_modfm = Buf()
        gtile = sb([128, 4, D], F32)
        b_gt = Buf()
        pm = psb[0][:, 0:192].rearrange("p (j m) -> p j m", m=2)
        for s in range(8):
            ws, wb, wsem = wr.next()
            k.dma("pool", ws, wslab_src(w_ada[l], s * 1536, 1536), writes=[wb], sem=wsem)
            for jj in range(12):
                j = s * 12 + jj
                for kc in range(KC):
                    k.op("pe", lambda e, j=j, jj=jj, kc=kc, ws=ws: e.matmul(pm[:, j, :], ws[:, kc, jj * 128:(jj + 1) * 128],
                         sT[:, kc, :], start=(kc == 0), stop=False), reads=[wb, b_sT], writes=[psB[0]], inc=False)
                k.op("pe", lambda e, j=j: e.matmul(pm[:, j, :], brow[0:1, j * 128:(j + 1) * 128], ones_bf[0:1, 0:2],
                     start=False, stop=True), reads=[b_brow, b_ones], writes=[psB[0]])
            for g in range(3):
                c0 = s * 1536 + g * 512
                which = None
                if 2 * D <= c0 < 3 * D:
                    which = 0
                elif 5 * D <= c0 < 6 * D:
                    which = 1
                if which is None:
                    continue
                off = c0 - (2 * D if which == 0 else 5 * D)
                for m in range(2):
                    pb = 1 + m
                    for kc in range(KC):
                        k.op("pe", lambda e, pb=pb, m=m, kc=kc, ws=ws, g=g: e.matmul(psb[pb], sRep[:, m, kc, :],
                             ws[:, kc, g * 512:(g + 1) * 512], start=(kc == 0), stop=False),
                             reads=[wb, b_sRep], writes=[psB[pb]], inc=False)
                    k.op("pe", lambda e, pb=pb, c0=c0: e.matmul(psb[pb], ones_bf[0:1, :], brow[0:1, c0:c0 + 512],
                         start=False, stop=True), reads=[b_brow, b_ones], writes=[psB[pb]])
                    k.op("dve", lambda e, pb=pb, which=which, m=m, off=off: e.tensor_tensor(
                        out=gtile[:, which * 2 + m, off:off + 512], in0=psb[pb], in1=nwb[:, which, off:off + 512],
                        op=ALU.mult), reads=[psB[pb], b_nwb], writes=[b_gt])
        k.op("dve", lambda e: e.tensor_copy(out=modfm, in_=pm), reads=[psB[0]], writes=[b_modfm])
        for which in range(2):
            shj, scj = (0, 16) if which == 0 else (48, 64)
            for m in range(2):
                o = which * 4 + m * 2
                k.op("dve", lambda e, o=o, scj=scj, m=m, which=which: e.scalar_tensor_tensor(
                    out=MODS[:, o, :], in0=modfm[:, scj:scj + 16, m], scalar=1.0, in1=nwf[:, which, :],
                    op0=ALU.add, op1=ALU.mult), reads=[b_modfm, b_nwf], writes=[b_MODS])
                k.op("dve", lambda e, o=o, shj=shj, m=m: e.tensor_copy(out=MODS[:, o + 1, :], in_=modfm[:, shj:shj + 16, m]),
                     reads=[b_modfm], writes=[b_MODS])
        for i in range(4):
            k.dma("sp", GNROW[i:i + 1, :], gtile[0:1, i, :], reads=[b_gt], sem="ag")
        k.barrier()

    def norm_to_hT(which):
        hT = sb([128, KC, TT], BF16)
        hTb = [Buf() for _ in range(NCH)]
        save = cur[0]
        xr = Ring("xi", [sb([128, D], F32) for _ in range(2)])
        junk = sb([128, D], BF16)
        b_junk = Buf()
        xsr = Ring("xs", [sb([128, D], BF16) for _ in range(2)])
        ss = sb([128, NCH], F32)
        rs = sb([128, NCH], F32)
        rstd = sb([128, NCH], F32)
        b_ss = [Buf() for _ in range(NCH)]
        pT = [psb[i].bitcast(BF16) for i in range(4)]
        for t in range(NCH):
            o = which * 4 + (0 if t >= NCC else 2)
            xi, xb, xsem = xr.next()
            k.dma("sp", xi, X[t * 128:(t + 1) * 128, :], writes=[xb], sem=xsem)
            k.op("act", lambda e, xi=xi, t=t: e.activation(out=junk, in_=xi, func=AF.Square, accum_out=ss[:, t:t + 1]),
                 reads=[xb], writes=[b_junk, b_ss[t]])
            k.op("act", lambda e, t=t: e.activation(out=rs[:, t:t + 1], in_=ss[:, t:t + 1], func=AF.Sqrt, scale=1.0 / D,
                 bias=CF(3)[:, 0:1] if False else EPS), reads=[b_ss[t]], writes=[b_ss[t]])
            k.op("dve", lambda e, t=t: e.reciprocal(out=rstd[:, t:t + 1], in_=rs[:, t:t + 1]), reads=[b_ss[t]], writes=[b_ss[t]])
            xs, xsb, _ = xsr.next()
            k.op("act", lambda e, xs=xs, xi=xi, t=t: e.activation(out=xs, in_=xi, func=AF.Copy, scale=rstd[:, t:t + 1]),
                 reads=[xb, b_ss[t]], writes=[xsb])
            for half in range(2):
                pb = (t % 2) * 2 + half
                for q in range(8):
                    kc = half * 8 + q
                    k.op("pe", lambda e, pb=pb, q=q, kc=kc, xs=xs: e.transpose(pT[pb][:, q * 128:(q + 1) * 128],
                         xs[:, kc * 128:(kc + 1) * 128], cI), reads=[xsb, bI], writes=[psB[pb]], inc=(q == 7))
                for q in range(8):
                    kc = half * 8 + q
                    en = "dve" if q % 2 == 0 else "act"
                    if en == "dve":
                        k.op("dve", lambda e, pb=pb, q=q, kc=kc, t=t, o=o: e.tensor_scalar(
                            out=hT[:, kc, t * 128:(t + 1) * 128], in0=pT[pb][:, q * 128:(q + 1) * 128],
                            scalar1=MODS[:, o, kc:kc + 1], scalar2=MODS[:, o + 1, kc:kc + 1], op0=ALU.mult, op1=ALU.add),
                            reads=[psB[pb], b_MODS], writes=[hTb[t]])
                    else:
                        k.op("act", lambda e, pb=pb, q=q, kc=kc, t=t, o=o: e.activation(
                            out=hT[:, kc, t * 128:(t + 1) * 128], in_=pT[pb][:, q * 128:(q + 1) * 128], func=AF.Identity,
                            scale=MODS[:, o, kc:kc + 1], bias=MODS[:, o + 1, kc:kc + 1]),
                            reads=[psB[pb], b_MODS], writes=[hTb[t]])
        k.barrier()
        cur[0] = save
        return hT, hTb

    def load_resident(src, rows):
        n = rows // 128
        t = sb([128, n, TT], BF16)
        b = Buf()
        for kc in range(n):
            k.dma("sp", t[:, kc, :], src[kc * 128:(kc + 1) * 128, :], writes=[b], sem="lr%d" % (kc % 4))
        return t, [b] * NCH

    def proj_fm(hT, hTb, w2d, col_list, nk, evac_chunk, M=128, wring=None):
        pbi = [0]
        for ci, c0 in enumerate(col_list):
            ws, wb, wsem = wring.next()
            k.dma("pool", ws[:, :, 0:M], wslab_src(w2d, c0, M), writes=[wb], sem=wsem)
            outs = []
            for si, (g0, n) in enumerate(segs):
                pb = pbi[0] % 4
                pbi[0] += 1
                tl = [hTb[t] for t in range(g0 // 128, (g0 + n) // 128)]
                for kc in range(nk):
                    k.op("pe", lambda e, pb=pb, kc=kc, ws=ws, g0=g0, n=n: e.matmul(psb[pb][0:M, 0:n], ws[:, kc, 0:M],
                         hT[:, kc, g0:g0 + n], start=(kc == 0), stop=(kc == nk - 1)),
                         reads=[wb] + tl, writes=[psB[pb]], inc=(kc == nk - 1))
                evac_chunk(ci, c0, pb, si, g0, n)

    def phase_inproj(l, hT, hTb):
        save = cur[0]
        W = w_in[l]
        rings = {}

        def reset():
            k.barrier()
            cur[0] = save
            rings["w"] = Ring("wi", [sb([128, KC, 128], BF16) for _ in range(2)])
            rings["s"] = Ring("sg", [sb([128, 512], BF16) for _ in range(4)])
            return rings["w"], rings["s"]
        wring, stg = reset()
        def mk_simple(dst, func, base):
            def ev(ci, c0, pb, si, g0, n):
                st, sbf, ssem = stg.next()
                k.op("act", lambda e: e.activation(out=st[:, 0:n], in_=psb[pb][:, 0:n], func=func), reads=[psB[pb]], writes=[sbf])
                r0 = c0 - base
                k.dma("sp", dst[r0:r0 + 128, g0:g0 + n], st[:, 0:n], reads=[sbf], sem=ssem)
            return ev
        proj_fm(hT, hTb, W, [4096 + i * 128 for i in range(16)], KC, mk_simple(RG, AF.Silu, 4096), wring=wring)
        proj_fm(hT, hTb, W, [10240 + i * 128 for i in range(16)], KC, mk_simple(MO, AF.Sigmoid, 10240), wring=wring)
        proj_fm(hT, hTb, W, [12320 + i * 128 for i in range(32)], KC, mk_simple(GRM, AF.Sigmoid, 12320), wring=wring)
        wring, stg = reset()
        ropeT = Ring("rp", [sb([128, 2, 512], F32) for _ in range(2)])
        xbf = Ring("xb", [sb([128, 512], BF16) for _ in range(2)])
        t12 = Ring("t1", [sb([128, 2, 512], F32) for _ in range(2)])
        def mk_rope(dst, base, scale):
            def ev(ci, c0, pb, si, g0, n):
                r0 = c0 - base
                st, sbf, ssem = stg.next()
                if g0 < TC:
                    k.op("act", lambda e: e.activation(out=st[:, 0:n], in_=psb[pb][:, 0:n], func=AF.Copy, scale=scale),
                         reads=[psB[pb]], writes=[sbf])
                else:
                    rp, rpb, rsem = ropeT.next()
                    k.dma("sp", rp[:, :, 0:n], cR_d[:, :, g0 - TC:g0 - TC + n], writes=[rpb], sem=rsem)
                    xb_, xbb, _ = xbf.next()
                    k.op("act", lambda e: e.activation(out=xb_[:, 0:n], in_=psb[pb][:, 0:n], func=AF.Copy, scale=scale),
                         reads=[psB[pb]], writes=[xbb])
                    pp = 4 + (pb % 2)
                    k.op("pe", lambda e: e.matmul(psb[pp][:, 0:n], cP, xb_[:, 0:n], start=True, stop=True),
                         reads=[xbb, bP], writes=[psB[pp]])
                    tt, ttb, _ = t12.next()
                    k.op("pool", lambda e: e.tensor_tensor(out=tt[:, 0, 0:n], in0=xb_[:, 0:n], in1=rp[:, 0, 0:n], op=ALU.mult),
                         reads=[xbb, rpb], writes=[ttb])
                    k.op("dve", lambda e: e.tensor_tensor(out=tt[:, 1, 0:n], in0=psb[pp][:, 0:n], in1=rp[:, 1, 0:n], op=ALU.mult),
                         reads=[psB[pp], rpb], writes=[ttb])
                    k.op("dve", lambda e: e.tensor_tensor(out=st[:, 0:n], in0=tt[:, 0, 0:n], in1=tt[:, 1, 0:n], op=ALU.add),
                         reads=[ttb], writes=[sbf])
                k.dma("sp", dst[r0:r0 + 128, g0:g0 + n], st[:, 0:n], reads=[sbf], sem=ssem)
            return ev
        proj_fm(hT, hTb, W, [i * 128 for i in range(8)], KC, mk_rope(QR, 0, 128.0 ** -0.5), wring=wring)
        proj_fm(hT, hTb, W, [1024 + i * 128 for i in range(8)], KC, mk_rope(KR, 1024, 1.0), wring=wring)
        wring, stg = reset()
        cw = sb([128, 4, KC], F32)
        b_cw = Buf()
        for kk in range(3):
            k.dma("sp", cw[:, kk, :], mconv_w[l, kk, :].rearrange("(c p) -> p c", p=128), writes=[b_cw], sem="cw",
                  allow_slow_non_contiguous=True)
        k.dma("sp", cw[:, 3, :], mconv_b[l, :].rearrange("(c p) -> p c", p=128), writes=[b_cw], sem="cw",
              allow_slow_non_contiguous=True)
        conv_chunks(hT, hTb, W, [6144 + i * 128 for i in range(16)], cw, wring,
                    lambda ci: ((QM, ci * 128, 128.0 ** -0.5) if ci < 8 else (KM, (ci - 8) * 128, 1.0)), mode="silu", cwb=b_cw)
        wring, stg = reset()
        gb = sb([8, 4], F32)
        b_gb = Buf()
        for g in range(4):
            k.dma("sp", gb[:, g:g + 1], gate_b[l, g, :].rearrange("(h o) -> h o", o=1), writes=[b_gb], sem="gb")
        gst = Ring("gs", [sb([8, 512], F32) for _ in range(2)])
        def ev_g(ci, c0, pb, si, g0, n):
            st, sbf, ssem = gst.next()
            k.op("act", lambda e: e.activation(out=st[:, 0:n], in_=psb[pb][0:8, 0:n], func=AF.Identity, bias=gb[:, ci:ci + 1]),
                 reads=[psB[pb], b_gb], writes=[sbf])
            k.dma("sp", GATES[ci, :, g0:g0 + n], st[:, 0:n], reads=[sbf], sem=ssem)
        proj_fm(hT, hTb, W, [12288 + g * 8 for g in range(4)], KC, ev_g, M=8, wring=wring)
        wring, stg = reset()
        wv = Ring("wv", [sb([128, KC, 512], BF16) for _ in range(2)])
        vst = Ring("vs", [sb([128, 512], BF16) for _ in range(3)])
        pbi = 0
        for dst, base in ((VR, 2048), (VM, 8192)):
            for g in range(4):
                ws, wb, wsem = wv.next()
                k.dma("pool", ws, wslab_src(W, base + g * 512, 512), writes=[wb], sem=wsem)
                for t in range(NCH):
                    pb = pbi % 4
                    pbi += 1
                    for kc in range(KC):
                        k.op("pe", lambda e, pb=pb, kc=kc, ws=ws, t=t: e.matmul(psb[pb], hT[:, kc, t * 128:(t + 1) * 128],
                             ws[:, kc, :], start=(kc == 0), stop=(kc == KC - 1)), reads=[wb, hTb[t]], writes=[psB[pb]],
                             inc=(kc == KC - 1))
                    st, sbf, ssem = vst.next()
                    k.op("act", lambda e, pb=pb, st=st: e.activation(out=st, in_=psb[pb], func=AF.Copy), reads=[psB[pb]], writes=[sbf])
                    k.dma("sp", dst[t * 128:(t + 1) * 128, g * 512:(g + 1) * 512], st, reads=[sbf], sem=ssem)
        k.barrier()
        cur[0] = save

    def conv_chunks(hT, hTb, W, cols, cw, wring, dst_fn, mode, cwb=None):
        raw = Ring("rw", [sb([128, TT + 3], F32) for _ in range(1)])
        for r_ap in raw.aps:
            k.op("pool", lambda e, r_ap=r_ap: e.memset(r_ap, 0.0), writes=[raw.bufs[raw.aps.index(r_ap)]])
        acc = Ring("ac", [sb([128, TT], F32) for _ in range(1)])
        ost = Ring("oc", [sb([128, TT], BF16) for _ in range(1)])
        state = {}

        def colof(g):
            return g + 1 if g < TC else g + 2

        def ev(ci, c0, pb, si, g0, n):
            if si == 0:
                state["raw"] = raw.next()
            rw, rwb, _ = state["raw"]
            k.op("act", lambda e: e.activation(out=rw[:, colof(g0):colof(g0) + n], in_=psb[pb][:, 0:n], func=AF.Copy),
                 reads=[psB[pb]], writes=[rwb])
            if si != len(segs) - 1:
                return
            a, ab, _ = acc.next()
            def views(off):
                return [(rw[:, off:off + TC], a[:, 0:TC]), (rw[:, off + TC + 1:off + TC + 1 + TL], a[:, TC:TT])]
            for (src, dsta) in views(1):
                k.op("dve", lambda e, src=src, dsta=dsta: e.tensor_scalar(out=dsta, in0=src, scalar1=cw[:, 1, ci:ci + 1],
                     scalar2=cw[:, 3, ci:ci + 1], op0=ALU.mult, op1=ALU.add), reads=[rwb, state["cwb"]], writes=[ab])
            for kk, off in ((0, 0), (2, 2)):
                for (src, dsta) in views(off):
                    k.op("dve", lambda e, src=src, dsta=dsta, kk=kk: e.scalar_tensor_tensor(
                        out=dsta, in0=src, scalar=cw[:, kk, ci:ci + 1], in1=dsta, op0=ALU.mult, op1=ALU.add),
                        reads=[rwb, state["cwb"], ab], writes=[ab])
            if mode == "silu":
                dst, r0, scale = dst_fn(ci)
                o, ob, osem = ost.next()
                if scale == 1.0:
                    k.op("act", lambda e: e.activation(out=o, in_=a, func=AF.Silu), reads=[ab], writes=[ob])
                else:
                    k.op("act", lambda e: e.activation(out=a, in_=a, func=AF.Silu), reads=[ab], writes=[ab])
                    k.op("pool", lambda e: e.tensor_scalar(out=o, in0=a, scalar1=scale, scalar2=None, op0=ALU.mult),
                         reads=[ab], writes=[ob])
                k.dma("sp", dst[r0:r0 + 128, :], o, reads=[ob], sem=osem)
            else:
                if ci % 2 == 0:
                    o, ob, osem = ost.next()
                    k.op("act", lambda e: e.activation(out=o, in_=a, func=AF.Silu), reads=[ab], writes=[ob])
                    state["a"] = (o, ob, osem)
                else:
                    o, ob, osem = state["a"]
                    k.op("dve", lambda e: e.tensor_tensor(out=o, in0=o, in1=a, op=ALU.mult), reads=[ob, ab], writes=[ob])
                    r0 = (ci // 2) * 128
                    k.dma("sp", ACTT[r0:r0 + 128, :], o, reads=[ob], sem=osem)
        state["cwb"] = cwb
        proj_fm(hT, hTb, W, cols, KC, ev, wring=wring)

    build.conv_chunks = conv_chunks
    def phase_gates(l):
        cur[0] = persist_end
        TM = sb([128, 6, NCH * 8], F32)
        tm_end = cur[0]
        G = [sb([8, TT], F32) for _ in range(4)]
        bG = [Buf() for _ in range(4)]
        for g in range(4):
            k.dma("sp", G[g], GATES[g], writes=[bG[g]], sem="pg%d" % g)
        one8 = sb([8, 1], F32)
        b_o8 = Buf()
        k.op("pool", lambda e: e.memset(one8, 1.0), writes=[b_o8])
        b_TM = Buf()
        cht = sb([8, 2, 2, NCH], F32)
        b_cht = Buf()
        tmp = sb([8, TT], F32)
        b_tmp = Buf()
        for d in range(2):
            IG, FG, bIG, bFG = G[2 * d], G[2 * d + 1], bG[2 * d], bG[2 * d + 1]
            k.op("act", lambda e, FG=FG: e.activation(out=FG, in_=FG, func=AF.Exp, scale=-1.0), reads=[bFG], writes=[bFG])
            k.op("act", lambda e, FG=FG: e.activation(out=FG, in_=FG, func=AF.Ln, bias=1.0), reads=[bFG], writes=[bFG])
            if d == 0:
                sl = [(slice(0, TT), 0.0)]
            else:
                sl = [("rc", None), ("rl", None)]
            def scan(out, dat, op1, init0, d=d):
                if d == 0:
                    k.op("dve", lambda e: e.tensor_tensor_scan(out=out[:, 0:TT], data0=one8[:, 0:1].to_broadcast([8, TT]),
                         data1=dat[:, 0:TT], initial=init0, op0=ALU.mult, op1=op1), reads=[b_o8, bIG, bFG, b_tmp], writes=[b_tmp, bIG, bFG])
                else:
                    k.op("dve", lambda e: e.tensor_tensor_scan(out=out[:, TC - 1::-1] if True else None,
                         data0=one8[:, 0:1].to_broadcast([8, TC]), data1=dat[:, TC - 1::-1], initial=init0, op0=ALU.mult, op1=op1),
                         reads=[b_o8, bIG, bFG, b_tmp], writes=[b_tmp, bIG, bFG])
                    k.op("dve", lambda e: e.tensor_tensor_scan(out=out[:, TT - 1:TC - 1:-1],
                         data0=one8[:, 0:1].to_broadcast([8, TL]), data1=dat[:, TT - 1:TC - 1:-1], initial=out[:, 0:1],
                         op0=ALU.mult, op1=op1), reads=[b_o8, bIG, bFG, b_tmp], writes=[b_tmp, bIG, bFG])
            scan(tmp, FG, ALU.add, 0.0)
            k.op("dve", lambda e, IG=IG: e.tensor_tensor(out=IG, in0=IG, in1=tmp, op=ALU.add), reads=[bIG, b_tmp], writes=[bIG])
            k.op("pool", lambda e, FG=FG: e.tensor_copy(out=FG, in_=tmp), reads=[b_tmp], writes=[bFG])
            scan(tmp, IG, ALU.max, 0.0)
            k.op("dve", lambda e, FG=FG: e.tensor_tensor(out=FG, in0=FG, in1=tmp, op=ALU.subtract), reads=[bFG, b_tmp], writes=[bFG])
            k.op("act", lambda e, FG=FG: e.activation(out=FG, in_=FG, func=AF.Exp), reads=[bFG], writes=[bFG])
            Mx3 = tmp.rearrange("p (c j) -> p c j", j=128)
            endj = 127 if d == 0 else 0
            Mend = Mx3[:, :, endj]
            mp = cht[:, d, 0, :]
            k.op("pool", lambda e, mp=mp: e.memset(mp, 0.0), writes=[b_cht])
            if d == 0:
                k.op("dve", lambda e, mp=mp, Mend=Mend: e.tensor_copy(out=mp[:, 1:NCH], in_=Mend[:, 0:NCH - 1]), reads=[b_tmp], writes=[b_cht])
            else:
                if NCC > 1:
                    k.op("dve", lambda e, mp=mp, Mend=Mend: e.tensor_copy(out=mp[:, 0:NCC - 1], in_=Mend[:, 1:NCC]), reads=[b_tmp], writes=[b_cht])
                k.op("dve", lambda e, mp=mp, Mend=Mend: e.tensor_copy(out=mp[:, NCC:NCH - 1], in_=Mend[:, NCC + 1:NCH]), reads=[b_tmp], writes=[b_cht])
                k.op("dve", lambda e, mp=mp, Mend=Mend: e.tensor_copy(out=mp[:, NCH - 1:NCH], in_=Mend[:, 0:1]), reads=[b_tmp], writes=[b_cht])
            k.op("dve", lambda e, mp=mp, Mend=Mend, d=d: e.tensor_tensor(out=cht[:, d, 1, :], in0=mp, in1=Mend, op=ALU.subtract),
                 reads=[b_tmp, b_cht], writes=[b_cht])
            k.op("act", lambda e, d=d: e.activation(out=cht[:, d, 1, :], in_=cht[:, d, 1, :], func=AF.Exp), reads=[b_cht], writes=[b_cht])
            wt = sb([8, TT], F32) if d == 0 else state_wt[0]
            if d == 0:
                state_wt.append(wt)
            b_wt = Buf()
            k.op("dve", lambda e, IG=IG, Mend=Mend, wt=wt: e.tensor_tensor(out=wt.rearrange("p (c j) -> p c j", j=128),
                 in0=IG.rearrange("p (c j) -> p c j", j=128), in1=Mend.unsqueeze(2).to_broadcast([8, NCH, 128]), op=ALU.subtract),
                 reads=[bIG, b_tmp], writes=[b_wt])
            k.op("act", lambda e, wt=wt: e.activation(out=wt, in_=wt, func=AF.Exp), reads=[b_wt], writes=[b_wt])
            for qi, (src, sbuf_) in enumerate(((IG, bIG), (wt, b_wt), (FG, bFG))):
                pb = d * 3 + qi
                for c in range(NCH):
                    k.op("pe", lambda e, pb=pb, c=c, src=src: e.matmul(psb[pb][:, c * 8:(c + 1) * 8], src[:, c * 128:(c + 1) * 128],
                         CF(0)[0:8, 0:8], start=True, stop=True), reads=[sbuf_, bF], writes=[psB[pb]], inc=(c == NCH - 1))
                k.op("dve", lambda e, pb=pb: e.tensor_copy(out=TM[:, pb, :], in_=psb[pb][:, 0:NCH * 8]), reads=[psB[pb]], writes=[b_TM])
            k.op("pool", lambda e: e.tensor_scalar(out=tmp, in0=tmp, scalar1=-1.0, scalar2=None, op0=ALU.mult), reads=[b_tmp], writes=[b_tmp])
            k.dma("sp", NEGMX[d * 8:(d + 1) * 8, :], tmp, reads=[b_tmp], sem="pn%d" % d)
            for q in range(2):
                k.dma("sp", CHT[q, d * 8:(d + 1) * 8, :], cht[:, d, q, :], reads=[b_cht], sem="pc%d" % q)
        k.barrier()
        cur[0] = tm_end
        return TM, b_TM

    state_wt = []
    def layer_consts(l):
        de = sb([128, 16], F32)
        b_de = Buf()
        k.dma("sp", de, decay_e[l:l + 1].rearrange("o a b -> o (a b)").partition_broadcast(128), writes=[b_de], sem="ld")
        lg = sb([128, 16], F32)
        b_lg = Buf()
        k.op("act", lambda e: e.activation(out=lg, in_=de, func=AF.Exp, scale=-float(np.log(2.0))), reads=[b_de], writes=[b_lg])
        k.op("act", lambda e: e.activation(out=lg, in_=lg, func=AF.Ln, scale=-1.0, bias=1.0), reads=[b_lg], writes=[b_lg])
        mask = sb([128, 8, 128], BF16)
        qf = sb([128, 8, 128], BF16)
        qb = sb([128, 8, 128], BF16)
        kd = sb([128, 2, 8], F32)
        cd = sb([128, 2, 8], F32)
        t1 = sb([128, 128], F32)
        t2 = sb([128, 128], F32)
        b_c, b_t1, b_t2 = Buf(), Buf(), Buf()
        for h in range(8):
            k.op("act", lambda e, h=h: e.activation(out=t1, in_=CF(1), func=AF.Exp, scale=lg[:, h:h + 1]), reads=[bF, b_lg], writes=[b_t1])
            k.op("act", lambda e, h=h: e.activation(out=t2, in_=CF(2), func=AF.Exp, scale=lg[:, 8 + h:9 + h]), reads=[bF, b_lg], writes=[b_t2])
            k.op("dve", lambda e: e.tensor_tensor(out=t1, in0=t1, in1=CF(3), op=ALU.mult), reads=[b_t1, bF], writes=[b_t1])
            k.op("dve", lambda e: e.tensor_tensor(out=t2, in0=t2, in1=CF(4), op=ALU.mult), reads=[b_t2, bF], writes=[b_t2])
            k.op("dve", lambda e, h=h: e.tensor_tensor(out=mask[:, h, :], in0=t1, in1=t2, op=ALU.add), reads=[b_t1, b_t2], writes=[b_c])
            k.op("act", lambda e, h=h: e.activation(out=qf[:, h, :], in_=CF(5), func=AF.Exp, scale=lg[:, h:h + 1]), reads=[bF, b_lg], writes=[b_c])
            k.op("act", lambda e, h=h: e.activation(out=qb[:, h, :], in_=CF(6), func=AF.Exp, scale=lg[:, 8 + h:9 + h]), reads=[bF, b_lg], writes=[b_c])
            k.op("act", lambda e, h=h: e.activation(out=kd[:, 0, h:h + 1], in_=CF(9)[:, 0:1], func=AF.Exp, scale=lg[:, h:h + 1]), reads=[bF, b_lg], writes=[b_c])
            k.op("act", lambda e, h=h: e.activation(out=kd[:, 1, h:h + 1], in_=CF(10)[:, 0:1], func=AF.Exp, scale=lg[:, 8 + h:9 + h]), reads=[bF, b_lg], writes=[b_c])
        k.op("act", lambda e: e.activation(out=cd.rearrange("p a b -> p (a b)"), in_=lg, func=AF.Exp, scale=128.0), reads=[b_lg], writes=[b_c])
        return dict(mask=mask, qf=qf, qb=qb, kd=kd, cd=cd, b=b_c)

    def phase_scan(l, TM, b_TM):
        RC = layer_consts(l)
        bRC = RC["b"]
        chb = sb([128, 2, 16, NCH], F32)
        b_chb = Buf()
        k.dma("sp", chb, CHT.rearrange("q r c -> (q r c)").rearrange("(o n) -> o n", o=1).partition_broadcast(128)
              if False else CHT.rearrange("q r c -> (q r c)").partition_broadcast(128), writes=[b_chb], sem="chb")
        hn = sb([128, 2, KC], F32)
        b_hn = Buf()
        for i in range(2):
            k.dma("sp", hn[:, i, :], hnorm_w[l, i, :].rearrange("(c p) -> p c", p=128), writes=[b_hn], sem="hn",
                  allow_slow_non_contiguous=True)
        S32 = sb([128, 8, 256], F32)
        Sbf = sb([128, 8, 256], BF16)
        C32 = sb([128, 8, 257], F32)
        Cbf = sb([128, 8, 257], BF16)
        bS = [Buf() for _ in range(8)]
        bSb = [Buf() for _ in range(8)]
        bC = [Buf() for _ in range(8)]
        bCb = [Buf() for _ in range(8)]
        kin = Ring("ki", [sb([128, 4, 8, 128], BF16) for _ in range(2)])
        vin = Ring("vi", [sb([128, 2, 8, 257], BF16) for _ in range(2)])
        for v_ap, vb in zip(vin.aps, vin.bufs):
            k.op("pool", lambda e, v_ap=v_ap: e.memset(v_ap[:, 1, :, 256:257], 1.0), writes=[vb])
        kt = Ring("kt", [sb([128, 128], BF16) for _ in range(4)])
        pT = [psb[6].bitcast(BF16), psb[7].bitcast(BF16)]
        tcount = [0]

        def zero_states():
            k.op("pool", lambda e: e.memset(S32, 0.0), writes=bS)
            k.op("pool", lambda e: e.memset(Sbf, 0.0), writes=bSb)
            k.op("pool", lambda e: e.memset(C32, 0.0), writes=bC)
            k.op("pool", lambda e: e.memset(Cbf, 0.0), writes=bCb)

        def load_chunk(c, need_q):
            ki, kb, ksem = kin.next()
            for i, src in enumerate((QR, KR, QM, KM)):
                if not need_q and i in (0, 2):
                    continue
                k.dma("sp", ki[:, i], src[:, c * 128:(c + 1) * 128].rearrange("(h d) t -> d h t", d=128), writes=[kb], sem=ksem)
            vi, vb, vsem = vin.next()
            k.dma("sp", vi[:, 0, :, 0:256], VR[c * 128:(c + 1) * 128, :].rearrange("t (h v) -> t h v", v=256), writes=[vb], sem=vsem)
            k.dma("sp", vi[:, 1, :, 0:256], VM[c * 128:(c + 1) * 128, :].rearrange("t (h v) -> t h v", v=256), writes=[vb], sem=vsem)
            return ki, kb, vi, vb

        def state_update(c, h, ki, kb, vi, vb, d):
            for br in range(2):
                ti = tcount[0]
                tcount[0] += 1
                slot = ti % 8
                pt = pT[0][:, slot * 128:(slot + 1) * 128]
                k.op("pe", lambda e, pt=pt, br=br: e.transpose(pt, ki[:, 1 + 2 * br, h, :], cI), reads=[kb, bI], writes=[psB[6]])
                kk_, kkb, _ = kt.next()
                if br == 0:
                    sc = RC["kd"][:, d, h:h + 1]
                    rd = [bRC]
                else:
                    sc = TM[:, d * 3 + 1, c * 8 + h:c * 8 + h + 1]
                    rd = [b_TM]
                k.op("act", lambda e, kk_=kk_, pt=pt, sc=sc: e.activation(out=kk_, in_=pt, func=AF.Copy, scale=sc),
                     reads=[psB[6]] + rd, writes=[kkb])
                pb = 4 + (ti % 2)
                if br == 0:
                    k.op("pe", lambda e, pb=pb, kk_=kk_: e.matmul(psb[pb][:, 0:256], kk_, vi[:, 0, h, 0:256], start=True, stop=True),
                         reads=[kkb, vb], writes=[psB[pb]])
                    k.op("dve", lambda e, pb=pb: e.scalar_tensor_tensor(out=S32[:, h, :], in0=S32[:, h, :], scalar=RC["cd"][:, d, h:h + 1],
                         in1=psb[pb][:, 0:256], op0=ALU.mult, op1=ALU.add), reads=[psB[pb], bS[h], bRC], writes=[bS[h]])
                    k.op("pool", lambda e: e.tensor_copy(out=Sbf[:, h, :], in_=S32[:, h, :]), reads=[bS[h]], writes=[bSb[h]])
                else:
                    k.op("pe", lambda e, pb=pb, kk_=kk_: e.matmul(psb[pb][:, 0:257], kk_, vi[:, 1, h, :], start=True, stop=True),
                         reads=[kkb, vb], writes=[psB[pb]])
                    k.op("dve", lambda e, pb=pb: e.scalar_tensor_tensor(out=C32[:, h, :], in0=C32[:, h, :],
                         scalar=chb[:, 1, d * 8 + h, c:c + 1], in1=psb[pb][:, 0:257], op0=ALU.mult, op1=ALU.add),
                         reads=[psB[pb], bC[h], b_chb], writes=[bC[h]])
                    k.op("pool", lambda e: e.tensor_copy(out=Cbf[:, h, :], in_=C32[:, h, :]), reads=[bC[h]], writes=[bCb[h]])

        zero_states()
        BWD = list(range(NCC - 1, -1, -1)) + list(range(NCH - 1, NCC - 1, -1))
        for c in BWD:
            ki, kb, vi, vb = load_chunk(c, False)
            k.dma("sp", SBR[c], Sbf, reads=bSb, sem="s1r")
            k.dma("sp", SBM[c], Cbf, reads=bCb, sem="s1m")
            for h in range(8):
                state_update(c, h, ki, kb, vi, vb, 1)
        k.barrier()
        zero_states()
        sbin = Ring("sn", [sb([128, 2, 8, 257], BF16) for _ in range(2)])
        uin = Ring("ui", [sb([128, 16, 128], F32) for _ in range(2)])
        gin = Ring("gi", [sb([128, 2, KC, 128], BF16) for _ in range(2)])
        wk = Ring("wk", [sb([128, 128], BF16) for _ in range(6)])
        wf = Ring("wf", [sb([128, 128], F32) for _ in range(4)])
        ytm = Ring("yt", [sb([128, 2, 8, 256], F32) for _ in range(1)])
        ynb = Ring("yn", [sb([128, 2, 2048], BF16) for _ in range(1)])
        oT = Ring("ot", [sb([128, 2, KC, 128], BF16) for _ in range(2)])
        small = Ring("sm", [sb([128, 8], F32) for _ in range(8)])
        hst = Ring("hs", [sb([128, 2, 16], F32) for _ in range(2)])
        junk = sb([128, 256], BF16)
        b_junk = Buf()
        def post_body(c, yt, ytb, hs, hsb, gi, gb_):
            k.op("act", lambda e: e.activation(out=hs[:, :, 0:8], in_=hs[:, :, 0:8], func=AF.Sqrt, scale=1.0 / 256, bias=EPS),
                 reads=[hsb], writes=[hsb])
            k.op("dve", lambda e: e.reciprocal(out=hs[:, :, 8:16], in_=hs[:, :, 0:8]), reads=[hsb], writes=[hsb])
            yn, ynb_, _ = ynb.next()
            for br in range(2):
                k.op("dve" if br == 0 else "pool", lambda e, br=br: e.tensor_tensor(out=yn[:, br, :].rearrange("p (h v) -> p h v", v=256),
                     in0=yt[:, br], in1=hs[:, br, 8:16].unsqueeze(2).to_broadcast([128, 8, 256]), op=ALU.mult),
                     reads=[ytb, hsb], writes=[ynb_])
            o_, ob, osem = oT.next()
            for br in range(2):
                for half in range(2):
                    for q in range(8):
                        kc = half * 8 + q
                        k.op("pe", lambda e, q=q, kc=kc, br=br: e.transpose(pT[1][:, q * 128:(q + 1) * 128], yn[:, br, kc * 128:(kc + 1) * 128], cI),
                             reads=[ynb_, bI], writes=[psB[7]], inc=(q == 7))
                    for q in range(8):
                        kc = half * 8 + q
                        k.op("dve", lambda e, q=q, kc=kc, br=br: e.scalar_tensor_tensor(out=o_[:, br, kc, :], in0=pT[1][:, q * 128:(q + 1) * 128],
                             scalar=hn[:, br, kc:kc + 1], in1=gi[:, br, kc, :], op0=ALU.mult, op1=ALU.mult),
                             reads=[psB[7], b_hn, gb_], writes=[ob])
            k.dma("sp", YRT[:, c * 128:(c + 1) * 128].rearrange("(kc p) t -> p kc t", p=128), o_[:, 0], reads=[ob], sem=osem)
            k.dma("sp", YMT[:, c * 128:(c + 1) * 128].rearrange("(kc p) t -> p kc t", p=128), o_[:, 1], reads=[ob], sem=osem)

        mc = [0]
        for c in range(NCH):
            ki, kb, vi, vb = load_chunk(c, True)
            sn, snb, snsem = sbin.next()
            k.dma("sp", sn[:, 0, :, 0:256], SBR[c], writes=[snb], sem=snsem)
            k.dma("sp", sn[:, 1], SBM[c], writes=[snb], sem=snsem)
            ui, ub, usem = uin.next()
            k.dma("sp", ui, NEGMX[:, c * 128:(c + 1) * 128].partition_broadcast(128), writes=[ub], sem=usem)
            gi, gb_, gsem = gin.next()
            k.dma("sp", gi[:, 0], RG[:, c * 128:(c + 1) * 128].rearrange("(kc p) t -> p kc t", p=128), writes=[gb_], sem=gsem)
            k.dma("sp", gi[:, 1], MO[:, c * 128:(c + 1) * 128].rearrange("(kc p) t -> p kc t", p=128), writes=[gb_], sem=gsem)
            yt, ytb, _ = ytm.next()
            hs, hsb, _ = hst.next()
            def head_body(h, c=c, ki=ki, kb=kb, vi=vi, vb=vb, sn=sn, snb=snb, ui=ui, ub=ub, yt=yt, ytb=ytb, hs=hs, hsb=hsb):
                ps_s = mc[0] % 2
                mc[0] += 1
                sps = psb[ps_s][:, 0:128]
                k.op("pe", lambda e, sps=sps: e.matmul(sps, ki[:, 1, h, :], ki[:, 0, h, :], start=True, stop=True),
                     reads=[kb], writes=[psB[ps_s]])
                sm_, smb, _ = wk.next()
                k.op("dve", lambda e, sm_=sm_, sps=sps: e.tensor_tensor(out=sm_, in0=sps, in1=RC["mask"][:, h, :], op=ALU.mult),
                     reads=[psB[ps_s], bRC], writes=[smb])
                qf_, qfb, _ = wk.next()
                k.op("pool", lambda e, qf_=qf_: e.tensor_tensor(out=qf_, in0=ki[:, 0, h, :], in1=RC["qf"][:, h, :], op=ALU.mult),
                     reads=[kb, bRC], writes=[qfb])
                qb_, qbb, _ = wk.next()
                k.op("pool", lambda e, qb_=qb_: e.tensor_tensor(out=qb_, in0=ki[:, 0, h, :], in1=RC["qb"][:, h, :], op=ALU.mult),
                     reads=[kb, bRC], writes=[qbb])
                yp = 2 + (h % 2)
                ypa = psb[yp][:, 0:256]
                k.op("pe", lambda e, ypa=ypa, sm_=sm_: e.matmul(ypa, sm_, vi[:, 0, h, 0:256], start=True, stop=False),
                     reads=[smb, vb], writes=[psB[yp]], inc=False)
                k.op("pe", lambda e, ypa=ypa, qf_=qf_: e.matmul(ypa, qf_, Sbf[:, h, :], start=False, stop=False),
                     reads=[qfb, bSb[h]], writes=[psB[yp]], inc=False)
                k.op("pe", lambda e, ypa=ypa, qb_=qb_: e.matmul(ypa, qb_, sn[:, 0, h, 0:256], start=False, stop=True),
                     reads=[qbb, snb], writes=[psB[yp]])
                k.op("act", lambda e, ypa=ypa: e.activation(out=yt[:, 0, h, :], in_=ypa, func=AF.Copy), reads=[psB[yp]], writes=[ytb])
                k.op("act", lambda e: e.activation(out=junk, in_=yt[:, 0, h, :], func=AF.Square, accum_out=hs[:, 0, h:h + 1]),
                     reads=[ytb], writes=[b_junk, hsb])
                sps2 = psb[ps_s][:, 128:256]
                k.op("pe", lambda e, sps2=sps2: e.matmul(sps2, ki[:, 3, h, :], ki[:, 2, h, :], start=True, stop=True),
                     reads=[kb], writes=[psB[ps_s]])
                for d in range(2):
                    r = d * 8 + h
                    f1, f1b, _ = wf.next()
                    k.op("dve", lambda e, f1=f1, d=d, r=r: e.scalar_tensor_tensor(out=f1, in0=ui[:, r, :],
                         scalar=TM[:, d * 3 + 0, c * 8 + h:c * 8 + h + 1], in1=CF(7 + d), op0=ALU.add, op1=ALU.min),
                         reads=[ub, b_TM, bF], writes=[f1b])
                    k.op("act", lambda e, f1=f1: e.activation(out=f1, in_=f1, func=AF.Exp), reads=[f1b], writes=[f1b])
                    sd, sdb, _ = wk.next()
                    k.op("dve", lambda e, sd=sd, f1=f1, sps2=sps2: e.tensor_tensor(out=sd, in0=sps2, in1=f1, op=ALU.mult),
                         reads=[psB[ps_s], f1b], writes=[sdb])
                    f2, f2b, _ = wf.next()
                    k.op("act", lambda e, f2=f2, r=r: e.activation(out=f2, in_=ui[:, r, :], func=AF.Exp, bias=chb[:, 0, r, c:c + 1]),
                         reads=[ub, b_chb], writes=[f2b])
                    qa, qab, _ = wk.next()
                    k.op("pool", lambda e, qa=qa, f2=f2: e.tensor_tensor(out=qa, in0=ki[:, 2, h, :], in1=f2, op=ALU.mult),
                         reads=[kb, f2b], writes=[qab])
                    npb = 4 + d
                    npa = psb[npb][:, 0:257]
                    k.op("pe", lambda e, npa=npa, sd=sd: e.matmul(npa, sd, vi[:, 1, h, :], start=True, stop=False),
                         reads=[sdb, vb], writes=[psB[npb]], inc=False)
                    st_rhs = Cbf[:, h, :] if d == 0 else sn[:, 1, h, :]
                    st_b = bCb[h] if d == 0 else snb
                    k.op("pe", lambda e, npa=npa, qa=qa, st_rhs=st_rhs: e.matmul(npa, qa, st_rhs, start=False, stop=True),
                         reads=[qab, st_b], writes=[psB[npb]])
                    s8, s8b, _ = small.next()
                    k.op("act", lambda e, s8=s8, npb=npb: e.activation(out=s8[:, 0:1], in_=psb[npb][:, 256:257], func=AF.Abs),
                         reads=[psB[npb]], writes=[s8b])
                    k.op("dve", lambda e, s8=s8, d=d: e.tensor_tensor(out=s8[:, 0:1], in0=s8[:, 0:1],
                         in1=TM[:, d * 3 + 2, c * 8 + h:c * 8 + h + 1], op=ALU.max), reads=[s8b, b_TM], writes=[s8b])
                    k.op("dve", lambda e, s8=s8: e.reciprocal(out=s8[:, 1:2], in_=s8[:, 0:1]), reads=[s8b], writes=[s8b])
                    if d == 0:
                        k.op("act", lambda e, s8=s8, npb=npb: e.activation(out=yt[:, 1, h, :], in_=psb[npb][:, 0:256], func=AF.Copy,
                             scale=s8[:, 1:2]), reads=[psB[npb], s8b], writes=[ytb])
                    else:
                        k.op("dve", lambda e, s8=s8, npb=npb: e.scalar_tensor_tensor(out=yt[:, 1, h, :], in0=psb[npb][:, 0:256],
                             scalar=s8[:, 1:2], in1=yt[:, 1, h, :], op0=ALU.mult, op1=ALU.add), reads=[psB[npb], s8b, ytb], writes=[ytb])
                k.op("act", lambda e: e.activation(out=junk, in_=yt[:, 1, h, :], func=AF.Square, accum_out=hs[:, 1, h:h + 1]),
                     reads=[ytb], writes=[b_junk, hsb])
                state_update(c, h, ki, kb, vi, vb, 0)
            for h in range(8):
                head_body(h)
            post_body(c, yt, ytb, hs, hsb, gi, gb_)
        k.barrier()

    def resid_update(t, ps_list, gi_row, xr, gnt, b_gnt, last_layer, stg2, src_sb=None, rstd_ap=None, rstd_b=None):
        pass

    def phase_outproj(l, last):
        cur[0] = persist_end
        for step in range(2):
            save = cur[0]
            aT, aTb = load_resident(YRT if step == 0 else YMT, 2048)
            wring = Ring("wo", [sb([128, KC, 128], BF16) for _ in range(3)])
            gt = Ring("og", [sb([128, 512], BF16) for _ in range(3)])
            zt = Ring("oz", [sb([128, 512], F32) for _ in range(3)])
            yo = Ring("oy", [sb([128, 512], BF16) for _ in range(3)])
            Wm = w_ro[l] if step == 0 else w_mo[l]

            def ev(ci, c0, pb, si, g0, n, step=step):
                g_, gb_, gsem = gt.next()
                k.dma("sp", g_[:, 0:n], GRM[step * 2048 + c0:step * 2048 + c0 + 128, g0:g0 + n], writes=[gb_], sem=gsem)
                z_, zb, zsem = zt.next()
                if step == 0:
                    k.op("dve", lambda e: e.tensor_tensor(out=z_[:, 0:n], in0=psb[pb][:, 0:n], in1=g_[:, 0:n], op=ALU.mult),
                         reads=[psB[pb], gb_], writes=[zb])
                    k.dma("sp", ZR[c0:c0 + 128, g0:g0 + n], z_[:, 0:n], reads=[zb], sem=zsem)
                else:
                    k.dma("sp", z_[:, 0:n], ZR[c0:c0 + 128, g0:g0 + n], writes=[zb], sem=zsem)
                    y_, yb, ysem = yo.next()
                    k.op("dve", lambda e: e.tensor_tensor(out=g_[:, 0:n], in0=psb[pb][:, 0:n], in1=g_[:, 0:n], op=ALU.mult),
                         reads=[psB[pb], gb_], writes=[gb_])
                    k.op("pool", lambda e: e.tensor_tensor(out=y_[:, 0:n], in0=g_[:, 0:n], in1=z_[:, 0:n], op=ALU.add),
                         reads=[gb_, zb], writes=[yb])
                    k.dma("sp", YT[c0:c0 + 128, g0:g0 + n], y_[:, 0:n], reads=[yb], sem=ysem)
            proj_fm(aT, aTb, Wm, [i * 128 for i in range(16)], KC, ev, wring=wring)
            k.barrier()
            cur[0] = save
        wres = sb([128, KC, D], BF16)
        b_wres = Buf()
        for g in range(4):
            k.dma("pool", wres[:, :, g * 512:(g + 1) * 512], wslab_src(w_o[l], g * 512, 512), writes=[b_wres], sem="wr%d" % g)
        gn = sb([128, 2, D], F32)
        b_gn = Buf()
        for m in range(2):
            k.dma("sp", gn[:, m, :], GNROW[m:m + 1, :].partition_broadcast(128), writes=[b_gn], sem="gn")
        yin = Ring("pyi", [sb([128, KC, 128], BF16) for _ in range(2)])
        xin = Ring("pxi", [sb([128, D], F32) for _ in range(2)])
        ot = Ring("pot", [sb([128, D], F32) for _ in range(2)])
        junk = sb([128, D], BF16)
        b_junk = Buf()
        st4 = Ring("ps4", [sb([128, 4], F32) for _ in range(4)])
        for t in range(NCH):
            yi, yib, ysem = yin.next()
            k.dma("sp", yi, YT[:, t * 128:(t + 1) * 128].rearrange("(kc p) t -> p kc t", p=128), writes=[yib], sem=ysem)
            xi, xib, xsem = xin.next()
            k.dma("sp", xi, X[t * 128:(t + 1) * 128, :], writes=[xib], sem=xsem)
            o_, ob, osem = ot.next()
            s4, s4b, _ = st4.next()
            base = (t % 2) * 4
            for g in range(4):
                pb = base + g
                for kc in range(KC):
                    k.op("pe", lambda e, pb=pb, kc=kc, g=g, yi=yi: e.matmul(psb[pb], yi[:, kc, :], wres[:, kc, g * 512:(g + 1) * 512],
                         start=(kc == 0), stop=(kc == KC - 1)), reads=[yib, b_wres], writes=[psB[pb]], inc=(kc == KC - 1))
                k.op("act", lambda e, pb=pb, g=g, o_=o_: e.activation(out=o_[:, g * 512:(g + 1) * 512], in_=psb[pb], func=AF.Copy),
                     reads=[psB[pb]], writes=[ob])
            finish_resid(t, o_, ob, xi, xib, gn, b_gn, s4, s4b, junk, b_junk, osem, last=False)
        k.barrier()

    def finish_resid(t, o_, ob, xi, xib, gn, b_gn, s4, s4b, junk, b_junk, osem, last):
        m = 0 if t >= NCC else 1
        k.op("act", lambda e: e.activation(out=junk, in_=o_, func=AF.Square, accum_out=s4[:, 0:1]), reads=[ob], writes=[b_junk, s4b])
        k.op("act", lambda e: e.activation(out=s4[:, 1:2], in_=s4[:, 0:1], func=AF.Sqrt, scale=1.0 / D, bias=EPS), reads=[s4b], writes=[s4b])
        k.op("dve", lambda e: e.reciprocal(out=s4[:, 2:3], in_=s4[:, 1:2]), reads=[s4b], writes=[s4b])
        k.op("dve", lambda e: e.scalar_tensor_tensor(out=o_, in0=o_, scalar=s4[:, 2:3], in1=gn[:, m, :], op0=ALU.mult, op1=ALU.mult),
             reads=[ob, s4b, b_gn], writes=[ob])
        k.op("pool", lambda e: e.tensor_tensor(out=o_, in0=o_, in1=xi, op=ALU.add), reads=[ob, xib], writes=[ob])
        if last and t >= NCC:
            k.dma("sp", y_out[(t - NCC) * 128:(t - NCC + 1) * 128, :], o_, reads=[ob], sem=osem)
        else:
            k.dma("sp", X[t * 128:(t + 1) * 128, :], o_, reads=[ob], sem=osem)

    def phase_ffn_up(l, hT, hTb):
        save = cur[0]
        wring = Ring("wu", [sb([128, KC, 128], BF16) for _ in range(2)])
        cw = sb([128, 4, 2 * FC], F32)
        b_cw = Buf()
        for kk in range(4):
            for half in range(2):
                src = (fconv_w[l, kk, half * FF:(half + 1) * FF] if kk < 3 else fconv_b[l, half * FF:(half + 1) * FF])
                k.dma("sp", cw[:, kk, :].rearrange("p (c two) -> p c two", two=2)[:, :, half], src.rearrange("(c p) -> p c", p=128),
                      writes=[b_cw], sem="fcw", allow_slow_non_contiguous=True)
        cols = []
        for c in range(FC):
            cols += [c * 128, FF + c * 128]
        build.conv_chunks(hT, hTb, w_up[l], cols, cw, wring, None, mode="ffn", cwb=b_cw)
        k.barrier()
        cur[0] = save

    def phase_ffn_down(l, last):
        cur[0] = persist_end
        HALF = 1024
        ssq = sb([128, NCH, 2], F32)
        ssq2 = sb([128, NCH, 2], F32)
        b_ssq = Buf()
        ssq_end = [cur[0]]
        wres = sb([128, FC, HALF], BF16)
        junk = sb([128, 512], BF16)
        b_junk = Buf()
        ain = Ring("dai", [sb([128, FC, 128], BF16) for _ in range(2)])
        ost = Ring("dos", [sb([128, HALF], F32) for _ in range(2)])
        b_wres = Buf()
        for hcol in range(2):
            for g in range(2):
                k.dma("pool", wres[:, :, g * 512:(g + 1) * 512],
                      w_dn[l][:, hcol * HALF + g * 512:hcol * HALF + (g + 1) * 512].rearrange("(kc p) n -> p kc n", p=128),
                      writes=[b_wres], sem="dw%d" % g)
            for t in range(NCH):
                ai, aib, asem = ain.next()
                k.dma("sp", ai, ACTT[:, t * 128:(t + 1) * 128].rearrange("(kc p) t -> p kc t", p=128), writes=[aib], sem=asem)
                o_, ob, osem = ost.next()
                for g in range(2):
                    pb = (t % 2) * 2 + g
                    for kc in range(FC):
                        k.op("pe", lambda e, pb=pb, kc=kc, g=g, ai=ai: e.matmul(psb[pb], ai[:, kc, :], wres[:, kc, g * 512:(g + 1) * 512],
                             start=(kc == 0), stop=(kc == FC - 1)), reads=[aib, b_wres], writes=[psB[pb]], inc=(kc == FC - 1))
                    k.op("act", lambda e, pb=pb, g=g, o_=o_: e.activation(out=o_[:, g * 512:(g + 1) * 512], in_=psb[pb], func=AF.Copy),
                         reads=[psB[pb]], writes=[ob])
                k.op("act", lambda e, o_=o_, t=t, hcol=hcol: e.activation(out=junk, in_=o_[:, 0:512], func=AF.Square,
                     accum_out=ssq[:, t, hcol:hcol + 1]), reads=[ob], writes=[b_junk, b_ssq])
                k.op("act", lambda e, o_=o_, t=t, hcol=hcol: e.activation(out=junk, in_=o_[:, 512:1024], func=AF.Square,
                     accum_out=ssq2[:, t, hcol:hcol + 1]), reads=[ob], writes=[b_junk, b_ssq])
                k.dma("sp", FO[t * 128:(t + 1) * 128, hcol * HALF:(hcol + 1) * HALF], o_, reads=[ob], sem=osem)
        k.barrier()
        cur[0] = ssq_end[0]
        gn = sb([128, 2, D], F32)
        b_gn = Buf()
        for m in range(2):
            k.dma("sp", gn[:, m, :], GNROW[2 + m:3 + m, :].partition_broadcast(128), writes=[b_gn], sem="gn")
        fin = Ring("dfi", [sb([128, D], F32) for _ in range(2)])
        xin = Ring("dxi", [sb([128, D], F32) for _ in range(2)])
        st4 = Ring("ds4", [sb([128, 4], F32) for _ in range(4)])
        for t in range(NCH):
            o_, ob, osem = fin.next()
            k.dma("sp", o_, FO[t * 128:(t + 1) * 128, :], writes=[ob], sem=osem + "l")
            xi, xib, xsem = xin.next()
            k.dma("sp", xi, X[t * 128:(t + 1) * 128, :], writes=[xib], sem=xsem)
            s4, s4b, _ = st4.next()
            m = 0 if t >= NCC else 1
            k.op("dve", lambda e, s4=s4, t=t: e.tensor_tensor(out=s4[:, 0:2], in0=ssq[:, t, :], in1=ssq2[:, t, :], op=ALU.add), reads=[b_ssq], writes=[s4b])
            k.op("dve", lambda e, s4=s4: e.tensor_tensor(out=s4[:, 0:1], in0=s4[:, 0:1], in1=s4[:, 1:2], op=ALU.add), reads=[s4b], writes=[s4b])
            k.op("act", lambda e, s4=s4: e.activation(out=s4[:, 1:2], in_=s4[:, 0:1], func=AF.Sqrt, scale=1.0 / D, bias=EPS), reads=[s4b], writes=[s4b])
            k.op("dve", lambda e, s4=s4: e.reciprocal(out=s4[:, 2:3], in_=s4[:, 1:2]), reads=[s4b], writes=[s4b])
            k.op("dve", lambda e, s4=s4, o_=o_, m=m: e.scalar_tensor_tensor(out=o_, in0=o_, scalar=s4[:, 2:3], in1=gn[:, m, :], op0=ALU.mult, op1=ALU.mult),
                 reads=[ob, s4b, b_gn], writes=[ob])
            k.op("pool", lambda e, o_=o_, xi=xi: e.tensor_tensor(out=o_, in0=o_, in1=xi, op=ALU.add), reads=[ob, xib], writes=[ob])
            if last and t >= NCC:
                k.dma("sp", y_out[(t - NCC) * 128:(t - NCC + 1) * 128, :], o_, reads=[ob], sem=osem)
            else:
                k.dma("sp", X[t * 128:(t + 1) * 128, :], o_, reads=[ob], sem=osem)
        k.barrier()


    for l in range(L):
        last = l == L - 1
        phase_adaln(l)
        if stop == "adaln":
            break
        cur[0] = persist_end
        hT, hTb = norm_to_hT(0)
        if stop == "norm":
            break
        phase_inproj(l, hT, hTb)
        if stop == "inproj":
            break
        TM, b_TM = phase_gates(l)
        if stop == "gates":
            break
        phase_scan(l, TM, b_TM)
        if stop == "scan":
            break
        phase_outproj(l, last)
        if stop == "outproj":
            break
        cur[0] = persist_end
        hT, hTb = norm_to_hT(1)
        phase_ffn_up(l, hT, hTb)
        if stop == "ffnup":
            break
        phase_ffn_down(l, last)
    k.barrier()
    k.emit()
    return nc


def make_in_maps(inputs, cfg):
    n = cfg["NCORES"]
    hc = host_consts(cfg["TL"])
    maps = []
    shared = {kk: np.ascontiguousarray(inputs[kk]) for kk in (
        "w_ada", "b_ada", "norm_w", "w_in", "mlstm_conv_w", "mlstm_conv_b", "mlstm_gate_b", "ret_decay_exp",
        "head_norm_w", "w_ret_out", "w_mlstm_out", "w_o", "w_up", "ffn_conv_w", "ffn_conv_b", "w_down")}
    for b in range(n):
        m = dict(shared)
        m["x"] = np.ascontiguousarray(inputs["x"][b])
        m["ctx"] = np.ascontiguousarray(inputs["ctx"][b])
        m["cvec"] = np.ascontiguousarray(np.stack([inputs["c"][b], inputs["c_ctx"]], axis=0))
        m.update(hc)
        maps.append(m)
    return maps


def kernel(**inputs):
    cfg = dict(CFG)
    nc = build(cfg)
    maps = make_in_maps(inputs, cfg)
    res = run_bass_kernel_spmd(nc, maps, core_ids=list(range(cfg["NCORES"])))
    return np.stack([res.results[b]["y"] for b in range(cfg["NCORES"])], axis=0).astype(np.float32)
```

```python
import numpy as np
import ml_dtypes
import concourse.bass as bass
import concourse.mybir as mybir
from concourse.bass_utils import run_bass_kernel_spmd

F32 = mybir.dt.float32
BF16 = mybir.dt.bfloat16
ALU = mybir.AluOpType
AF = mybir.ActivationFunctionType

D = 2048
KC = 16
HD = 8
EPS = 1e-6
NEG = -30000.0
CFG = dict(TC=256, TL=4096, L=4, FF=5632, NCORES=8, dbg=False, stop=None)


class Buf:
    __slots__ = ("w", "r")

    def __init__(self):
        self.w = None
        self.r = {}


class Eng:
    def __init__(self, name, sem):
        self.name = name
        self.sem = sem
        self.cnt = 0
        self.q = []
        self.known = {}


class K:
    def __init__(self, nc):
        self.nc = nc
        self.E = {n: Eng(n, nc.alloc_semaphore(name="p_" + n)) for n in ("pe", "act", "dve", "pool", "sp")}
        self.dsem = {}
        self.freel = {}
        self.allsem = []

    def dma_sem(self, name, kind="sp"):
        if name not in self.dsem:
            fl = self.freel.setdefault(kind, [])
            if fl:
                self.dsem[name] = fl.pop()
            else:
                self.dsem[name] = [self.nc.alloc_semaphore(name="d_%d" % len(self.allsem)), 0, kind]
                self.allsem.append(self.dsem[name])
        return name

    def _waits(self, eng, reads, writes):
        deps = {}

        def add(t):
            if t is None:
                return
            s, v = t
            if deps.get(s, 0) < v:
                deps[s] = v

        for b in reads:
            add(b.w)
        for b in writes:
            add(b.w)
            for s, v in b.r.items():
                add((s, v))
        for s, v in deps.items():
            if s is eng.sem:
                if eng.name == "pe" or v > eng.cnt:
                    continue
            if eng.known.get(s, 0) >= v:
                continue
            eng.known[s] = v
            eng.q.append(("w", s, v))

    def _mark(self, tag, reads, writes):
        s, v = tag
        for b in reads:
            if b.r.get(s, 0) < v:
                b.r[s] = v
        for b in writes:
            b.w = tag
            b.r = {}

    def op(self, en, fn, reads=(), writes=(), inc=True):
        eng = self.E[en]
        self._waits(eng, reads, writes)
        if inc:
            eng.cnt += 1
            eng.q.append(("i", fn, eng.sem))
            tag = (eng.sem, eng.cnt)
        else:
            eng.q.append(("n", fn))
            tag = (eng.sem, eng.cnt + 1)
        self._mark(tag, reads, writes)

    def dma(self, en, out, in_, reads=(), writes=(), sem=None, **kw):
        eng = self.E[en]
        self.dma_sem(sem, en)
        self._waits(eng, reads, writes)
        d = self.dsem[sem]
        d[1] += 16
        eng.q.append(("d", out, in_, d[0], kw))
        self._mark((d[0], d[1]), reads, writes)

    def barrier(self):
        for en, eng in self.E.items():
            for en2, e2 in self.E.items():
                if e2 is eng or e2.cnt == 0:
                    continue
                if eng.known.get(e2.sem, 0) < e2.cnt:
                    eng.known[e2.sem] = e2.cnt
                    eng.q.append(("w", e2.sem, e2.cnt))
            for d in self.allsem:
                if d[1] > 0 and eng.known.get(d[0], 0) < d[1]:
                    eng.known[d[0]] = d[1]
                    eng.q.append(("w", d[0], d[1]))
        self.freel = {}
        for d in self.allsem:
            self.freel.setdefault(d[2], []).append(d)
        self.dsem = {}

    def emit(self):
        nc = self.nc
        with nc.Block() as block:
            def run(eng):
                def f(e):
                    for it in eng.q:
                        kk = it[0]
                        if kk == "w":
                            e.wait_ge(it[1], it[2])
                        elif kk == "i":
                            it[1](e).then_inc(it[2], 1)
                        elif kk == "n":
                            it[1](e)
                        else:
                            e.dma_start(out=it[1], in_=it[2], **it[4]).then_inc(it[3], 16)
                return f
            block.tensor(run(self.E["pe"]))
            block.scalar(run(self.E["act"]))
            block.vector(run(self.E["dve"]))
            block.gpsimd(run(self.E["pool"]))
            block.sync(run(self.E["sp"]))


class Ring:
    def __init__(self, name, aps):
        self.name = name
        self.aps = aps
        self.bufs = [Buf() for _ in aps]
        self.i = -1

    def next(self):
        self.i = (self.i + 1) % len(self.aps)
        return self.aps[self.i], self.bufs[self.i], "%s%d" % (self.name, self.i)


def host_consts(TL):
    ident = np.eye(128, dtype=np.float32)
    perm = np.zeros((128, 128), np.float32)
    for m in range(128):
        partner = m + 32 if (m % 64) < 32 else m - 32
        perm[partner, m] = 1.0
    j = np.arange(128, dtype=np.float32)[:, None]
    i = np.arange(128, dtype=np.float32)[None, :]
    one = np.ones((128, 128), np.float32)
    blocks = [
        ident,
        np.maximum(i - j, 0.0),
        np.maximum(j - i, 0.0),
        (i >= j).astype(np.float32),
        (j >= i).astype(np.float32),
        (i + 1.0) * one,
        (128.0 - i) * one,
        np.where(j <= i, 0.0, NEG).astype(np.float32),
        np.where(j >= i, 0.0, NEG).astype(np.float32),
        (127.0 - j) * one,
        j * one,
    ]
    cF = np.concatenate(blocks, axis=1).astype(np.float32)
    t = np.arange(TL)
    rows = (t // 64).astype(np.float32)
    cols = (t % 64).astype(np.float32)
    half = 32
    freqs = (10000.0 ** (-np.arange(half, dtype=np.float32) / half)).astype(np.float32)
    rope = np.zeros((128, 2, TL), np.float32)
    for p in range(128):
        pos = rows if p < 64 else cols
        ang = (pos * freqs[p % 32]).astype(np.float32)
        rope[p, 0] = np.cos(ang)
        rope[p, 1] = np.sin(ang) * (-1.0 if (p % 64) < 32 else 1.0)
    return dict(cI=ident.astype(ml_dtypes.bfloat16), cP=perm.astype(ml_dtypes.bfloat16), cF=cF, cROPE=rope)


def build(cfg):
    TC, TL, L, FF, dbg, stop = cfg["TC"], cfg["TL"], cfg["L"], cfg["FF"], cfg["dbg"], cfg["stop"]
    TT = TC + TL
    NCH = TT // 128
    NCC = TC // 128
    FC = FF // 128
    INC = 16416
    nc = bass.Bass("TRN2", target_bir_lowering=False)
    k = K(nc)

    def din(name, shape, dt=F32):
        return nc.dram_tensor(name, list(shape), dt, kind="ExternalInput").ap()

    def dscr(name, shape, dt):
        return nc.dram_tensor(name, list(shape), dt, kind=("ExternalOutput" if dbg else "Internal")).ap()

    x_in = din("x", [TL, D])
    ctx_in = din("ctx", [TC, D])
    cvec = din("cvec", [2, D])
    w_ada = din("w_ada", [L, D, 6 * D])
    b_ada = din("b_ada", [L, 6 * D])
    norm_w = din("norm_w", [L, 4, D])
    w_in = din("w_in", [L, D, INC])
    mconv_w = din("mlstm_conv_w", [L, 3, 2048])
    mconv_b = din("mlstm_conv_b", [L, 2048])
    gate_b = din("mlstm_gate_b", [L, 4, 8])
    decay_e = din("ret_decay_exp", [L, 2, 8])
    hnorm_w = din("head_norm_w", [L, 2, 2048])
    w_ro = din("w_ret_out", [L, D, D])
    w_mo = din("w_mlstm_out", [L, D, D])
    w_o = din("w_o", [L, D, D])
    w_up = din("w_up", [L, D, 2 * FF])
    fconv_w = din("ffn_conv_w", [L, 3, 2 * FF])
    fconv_b = din("ffn_conv_b", [L, 2 * FF])
    w_dn = din("w_down", [L, FF, D])
    cI_d = din("cI", [128, 128], BF16)
    cP_d = din("cP", [128, 128], BF16)
    cF_d = din("cF", [128, 11 * 128])
    cR_d = din("cROPE", [128, 2, TL])
    y_out = nc.dram_tensor("y", [TL, D], F32, kind="ExternalOutput").ap()

    X = dscr("X", [TT, D], F32)
    QR = dscr("QR", [1024, TT], BF16)
    KR = dscr("KR", [1024, TT], BF16)
    QM = dscr("QM", [1024, TT], BF16)
    KM = dscr("KM", [1024, TT], BF16)
    VR = dscr("VR", [TT, 2048], BF16)
    VM = dscr("VM", [TT, 2048], BF16)
    RG = dscr("RG", [2048, TT], BF16)
    MO = dscr("MO", [2048, TT], BF16)
    GRM = dscr("GRM", [4096, TT], BF16)
    GATES = dscr("GATES", [4, 8, TT], F32)
    NEGMX = dscr("NEGMX", [16, TT], F32)
    CHT = dscr("CHT", [2, 16, NCH], F32)
    SBR = dscr("SBR", [NCH, 128, 8, 256], BF16)
    SBM = dscr("SBM", [NCH, 128, 8, 257], BF16)
    YRT = dscr("YRT", [2048, TT], BF16)
    YMT = dscr("YMT", [2048, TT], BF16)
    ZR = dscr("ZR", [2048, TT], F32)
    YT = dscr("YT", [2048, TT], BF16)
    ACTT = dscr("ACTT", [FF, TT], BF16)
    FO = dscr("FO", [TT, D], F32)
    GNROW = dscr("GNROW", [4, D], F32)

    cnt = [0]
    cur = [0]

    ARENA_BYTES = 206 * 1024
    arena = nc.alloc_sbuf_tensor("arena", [128, ARENA_BYTES], mybir.dt.uint8).ap()

    def sb(shape, dt):
        nel = int(np.prod(shape[1:]))
        nbytes = nel * (4 if dt == F32 else 2)
        off = cur[0]
        cur[0] += (nbytes + 63) // 64 * 64
        assert cur[0] <= ARENA_BYTES, ("sbuf overflow", cur[0])
        flat = arena[0:shape[0], off:off + nbytes].bitcast(dt)
        if len(shape) == 2:
            return flat
        names = ["a%d" % i for i in range(len(shape) - 1)]
        pat = "p (%s) -> p %s" % (" ".join(names), " ".join(names))
        return flat.rearrange(pat, **{n: int(v) for n, v in zip(names[1:], shape[2:])})

    psb = [nc.alloc_psum_tensor("ps%d" % i, [128, 512], F32).ap() for i in range(8)]
    psB = [Buf() for _ in range(8)]

    cI = sb([128, 128], BF16)
    cP = sb([128, 128], BF16)
    cF = sb([128, 11 * 128], F32)
    bI, bP, bF = Buf(), Buf(), Buf()
    k.dma("sp", cI, cI_d, writes=[bI], sem="c0")
    k.dma("sp", cP, cP_d, writes=[bP], sem="c1")
    k.dma("sp", cF, cF_d, writes=[bF], sem="c2")

    def CF(i):
        return cF[:, i * 128:(i + 1) * 128]

    ones_bf = sb([128, 128], BF16)
    b_ones = Buf()
    k.op("pool", lambda e: e.memset(ones_bf, 1.0), writes=[b_ones])
    sT = sb([128, KC, 2], BF16)
    b_sT = Buf()
    MODS = sb([128, 8, KC], F32)
    b_MODS = Buf()
    persist_end = cur[0]

    k.dma("sp", X[0:TC, :], ctx_in, sem="c3")
    k.dma("sp", X[TC:TT, :], x_in, sem="c4")

    cur[0] = persist_end
    cv32 = sb([128, KC, 2], F32)
    b_cv = Buf()
    for m in range(2):
        k.dma("sp", cv32[:, :, m], cvec[m, :].rearrange("(kc p) -> p kc", p=128), writes=[b_cv], sem="c5",
              allow_slow_non_contiguous=True)
    k.op("act", lambda e: e.activation(out=sT, in_=cv32, func=AF.Silu), reads=[b_cv], writes=[b_sT])
    k.barrier()

    segs = [(0, TC)] + [(TC + i * 512, 512) for i in range(TL // 512)]

    def wslab_src(w2d, c0, ncols):
        return w2d[:, c0:c0 + ncols].rearrange("(kc p) n -> p kc n", p=128)

    def phase_adaln(l):
        cur[0] = persist_end
        sRep = sb([128, 2, KC, 128], BF16)
        b_sRep = Buf()
        for m in range(2):
            k.op("dve", lambda e, m=m: e.tensor_copy(out=sRep[:, m], in_=sT[:, :, m:m + 1].to_broadcast([128, KC, 128])),
                 reads=[b_sT], writes=[b_sRep])
        wr = Ring("wa", [sb([128, KC, 1536], BF16) for _ in range(2)])
        brow = sb([1, 6 * D], BF16)
        b_brow = Buf()
        k.dma("pool", brow, b_ada[l:l + 1, :], writes=[b_brow], sem="ab")
        nwf = sb([128, 2, KC], F32)
        b_nwf = Buf()
        for i, r in enumerate((0, 2)):
            k.dma("sp", nwf[:, i, :], norm_w[l, r, :].rearrange("(kc p) -> p kc", p=128), writes=[b_nwf], sem="an",
                  allow_slow_non_contiguous=True)
        nwb = sb([128, 2, D], F32)
        b_nwb = Buf()
        for i, r in enumerate((1, 3)):
            k.dma("sp", nwb[:, i, :], norm_w[l, r:r + 1, :].partition_broadcast(128), writes=[b_nwb], sem="anb")
        modfm = sb([128, 96, 2], F32)
        b_modfm = Buf()
        gtile = sb([128, 4, D], F32)
        b_gt = Buf()
        pm = psb[0][:, 0:192].rearrange("p (j m) -> p j m", m=2)
        for s in range(8):
            ws, wb, wsem = wr.next()
            k.dma("pool", ws, wslab_src(w_ada[l], s * 1536, 1536), writes=[wb], sem=wsem)
            for jj in range(12):
                j = s * 12 + jj
                for kc in range(KC):
                    k.op("pe", lambda e, j=j, jj=jj, kc=kc, ws=ws: e.matmul(pm[:, j, :], ws[:, kc, jj * 128:(jj + 1) * 128],
                         sT[:, kc, :], start=(kc == 0), stop=False), reads=[wb, b_sT], writes=[psB[0]], inc=False)
                k.op("pe", lambda e, j=j: e.matmul(pm[:, j, :], brow[0:1, j * 128:(j + 1) * 128], ones_bf[0:1, 0:2],
                     start=False, stop=True), reads=[b_brow, b_ones], writes=[psB[0]])
            for g in range(3):
                c0 = s * 1536 + g * 512
                which = None
                if 2 * D <= c0 < 3 * D:
                    which = 0
                elif 5 * D <= c0 < 6 * D:
                    which = 1
                if which is None:
                    continue
                off = c0 - (2 * D if which == 0 else 5 * D)
                for m in range(2):
                    pb = 1 + m
                    for kc in range(KC):
                        k.op("pe", lambda e, pb=pb, m=m, kc=kc, ws=ws, g=g: e.matmul(psb[pb], sRep[:, m, kc, :],
                             ws[:, kc, g * 512:(g + 1) * 512], start=(kc == 0), stop=False),
                             reads=[wb, b_sRep], writes=[psB[pb]], inc=False)
                    k.op("pe", lambda e, pb=pb, c0=c0: e.matmul(psb[pb], ones_bf[0:1, :], brow[0:1, c0:c0 + 512],
                         start=False, stop=True), reads=[b_brow, b_ones], writes=[psB[pb]])
                    k.op("dve", lambda e, pb=pb, which=which, m=m, off=off: e.tensor_tensor(
                        out=gtile[:, which * 2 + m, off:off + 512], in0=psb[pb], in1=nwb[:, which, off:off + 512],
                        op=ALU.mult), reads=[psB[pb], b_nwb], writes=[b_gt])
        k.op("dve", lambda e: e.tensor_copy(out=modfm, in_=pm), reads=[psB[0]], writes=[b_modfm])
        for which in range(2):
            shj, scj = (0, 16) if which == 0 else (48, 64)
            for m in range(2):
                o = which * 4 + m * 2
                k.op("dve", lambda e, o=o, scj=scj, m=m, which=which: e.scalar_tensor_tensor(
                    out=MODS[:, o, :], in0=modfm[:, scj:scj + 16, m], scalar=1.0, in1=nwf[:, which, :],
                    op0=ALU.add, op1=ALU.mult), reads=[b_modfm, b_nwf], writes=[b_MODS])
                k.op("dve", lambda e, o=o, shj=shj, m=m: e.tensor_copy(out=MODS[:, o + 1, :], in_=modfm[:, shj:shj + 16, m]),
                     reads=[b_modfm], writes=[b_MODS])
        for i in range(4):
            k.dma("sp", GNROW[i:i + 1, :], gtile[0:1, i, :], reads=[b_gt], sem="ag")
        k.barrier()

    def norm_to_hT(which):
        hT = sb([128, KC, TT], BF16)
        hTb = [Buf() for _ in range(NCH)]
        save = cur[0]
        xr = Ring("xi", [sb([128, D], F32) for _ in range(2)])
        junk = sb([128, D], BF16)
        b_junk = Buf()
        xsr = Ring("xs", [sb([128, D], BF16) for _ in range(2)])
        ss = sb([128, NCH], F32)
        rs = sb([128, NCH], F32)
        rstd = sb([128, NCH], F32)
        b_ss = [Buf() for _ in range(NCH)]
        pT = [psb[i].bitcast(BF16) for i in range(4)]
        for t in range(NCH):
            o = which * 4 + (0 if t >= NCC else 2)
            xi, xb, xsem = xr.next()
            k.dma("sp", xi, X[t * 128:(t + 1) * 128, :], writes=[xb], sem=xsem)
            k.op("act", lambda e, xi=xi, t=t: e.activation(out=junk, in_=xi, func=AF.Square, accum_out=ss[:, t:t + 1]),
                 reads=[xb], writes=[b_junk, b_ss[t]])
            k.op("act", lambda e, t=t: e.activation(out=rs[:, t:t + 1], in_=ss[:, t:t + 1], func=AF.Sqrt, scale=1.0 / D,
                 bias=CF(3)[:, 0:1] if False else EPS), reads=[b_ss[t]], writes=[b_ss[t]])
            k.op("dve", lambda e, t=t: e.reciprocal(out=rstd[:, t:t + 1], in_=rs[:, t:t + 1]), reads=[b_ss[t]], writes=[b_ss[t]])
            xs, xsb, _ = xsr.next()
            k.op("act", lambda e, xs=xs, xi=xi, t=t: e.activation(out=xs, in_=xi, func=AF.Copy, scale=rstd[:, t:t + 1]),
                 reads=[xb, b_ss[t]], writes=[xsb])
            for half in range(2):
                pb = (t % 2) * 2 + half
                for q in range(8):
                    kc = half * 8 + q
                    k.op("pe", lambda e, pb=pb, q=q, kc=kc, xs=xs: e.transpose(pT[pb][:, q * 128:(q + 1) * 128],
                         xs[:, kc * 128:(kc + 1) * 128], cI), reads=[xsb, bI], writes=[psB[pb]], inc=(q == 7))
                for q in range(8):
                    kc = half * 8 + q
                    en = "dve" if q % 2 == 0 else "act"
                    if en == "dve":
                        k.op("dve", lambda e, pb=pb, q=q, kc=kc, t=t, o=o: e.tensor_scalar(
                            out=hT[:, kc, t * 128:(t + 1) * 128], in0=pT[pb][:, q * 128:(q + 1) * 128],
                            scalar1=MODS[:, o, kc:kc + 1], scalar2=MODS[:, o + 1, kc:kc + 1], op0=ALU.mult, op1=ALU.add),
                            reads=[psB[pb], b_MODS], writes=[hTb[t]])
                    else:
                        k.op("act", lambda e, pb=pb, q=q, kc=kc, t=t, o=o: e.activation(
                            out=hT[:, kc, t * 128:(t + 1) * 128], in_=pT[pb][:, q * 128:(q + 1) * 128], func=AF.Identity,
                            scale=MODS[:, o, kc:kc + 1], bias=MODS[:, o + 1, kc:kc + 1]),
                            reads=[psB[pb], b_MODS], writes=[hTb[t]])
        k.barrier()
        cur[0] = save
        return hT, hTb

    def load_resident(src, rows):
        n = rows // 128
        t = sb([128, n, TT], BF16)
        b = Buf()
        for kc in range(n):
            k.dma("sp", t[:, kc, :], src[kc * 128:(kc + 1) * 128, :], writes=[b], sem="lr%d" % (kc % 4))
        return t, [b] * NCH

    def proj_fm(hT, hTb, w2d, col_list, nk, evac_chunk, M=128, wring=None):
        pbi = [0]
        for ci, c0 in enumerate(col_list):
            ws, wb, wsem = wring.next()
            k.dma("pool", ws[:, :, 0:M], wslab_src(w2d, c0, M), writes=[wb], sem=wsem)
            outs = []
            for si, (g0, n) in enumerate(segs):
                pb = pbi[0] % 4
                pbi[0] += 1
                tl = [hTb[t] for t in range(g0 // 128, (g0 + n) // 128)]
                for kc in range(nk):
                    k.op("pe", lambda e, pb=pb, kc=kc, ws=ws, g0=g0, n=n: e.matmul(psb[pb][0:M, 0:n], ws[:, kc, 0:M],
                         hT[:, kc, g0:g0 + n], start=(kc == 0), stop=(kc == nk - 1)),
                         reads=[wb] + tl, writes=[psB[pb]], inc=(kc == nk - 1))
                evac_chunk(ci, c0, pb, si, g0, n)

    def phase_inproj(l, hT, hTb):
        save = cur[0]
        W = w_in[l]
        rings = {}

        def reset():
            k.barrier()
            cur[0] = save
            rings["w"] = Ring("wi", [sb([128, KC, 128], BF16) for _ in range(2)])
            rings["s"] = Ring("sg", [sb([128, 512], BF16) for _ in range(4)])
            return rings["w"], rings["s"]
        wring, stg = reset()
        def mk_simple(dst, func, base):
            def ev(ci, c0, pb, si, g0, n):
                st, sbf, ssem = stg.next()
                k.op("act", lambda e: e.activation(out=st[:, 0:n], in_=psb[pb][:, 0:n], func=func), reads=[psB[pb]], writes=[sbf])
                r0 = c0 - base
                k.dma("sp", dst[r0:r0 + 128, g0:g0 + n], st[:, 0:n], reads=[sbf], sem=ssem)
            return ev
        proj_fm(hT, hTb, W, [4096 + i * 128 for i in range(16)], KC, mk_simple(RG, AF.Silu, 4096), wring=wring)
        proj_fm(hT, hTb, W, [10240 + i * 128 for i in range(16)], KC, mk_simple(MO, AF.Sigmoid, 10240), wring=wring)
        proj_fm(hT, hTb, W, [12320 + i * 128 for i in range(32)], KC, mk_simple(GRM, AF.Sigmoid, 12320), wring=wring)
        wring, stg = reset()
        ropeT = Ring("rp", [sb([128, 2, 512], F32) for _ in range(3)])
        xbf = Ring("xb", [sb([128, 512], BF16) for _ in range(4)])
        t12 = Ring("t1", [sb([128, 2, 512], F32) for _ in range(3)])
        def mk_rope(dst, base, scale):
            def ev(ci, c0, pb, si, g0, n):
                r0 = c0 - base
                st, sbf, ssem = stg.next()
                if g0 < TC:
                    k.op("act", lambda e: e.activation(out=st[:, 0:n], in_=psb[pb][:, 0:n], func=AF.Copy, scale=scale),
                         reads=[psB[pb]], writes=[sbf])
                else:
                    rp, rpb, rsem = ropeT.next()
                    k.dma("sp", rp[:, :, 0:n], cR_d[:, :, g0 - TC:g0 - TC + n], writes=[rpb], sem=rsem)
                    xb_, xbb, _ = xbf.next()
                    k.op("act", lambda e: e.activation(out=xb_[:, 0:n], in_=psb[pb][:, 0:n], func=AF.Copy, scale=scale),
                         reads=[psB[pb]], writes=[xbb])
                    pp = 4 + (pb % 2)
                    k.op("pe", lambda e: e.matmul(psb[pp][:, 0:n], cP, xb_[:, 0:n], start=True, stop=True),
                         reads=[xbb, bP], writes=[psB[pp]])
                    tt, ttb, _ = t12.next()
                    k.op("pool", lambda e: e.tensor_tensor(out=tt[:, 0, 0:n], in0=xb_[:, 0:n], in1=rp[:, 0, 0:n], op=ALU.mult),
                         reads=[xbb, rpb], writes=[ttb])
                    k.op("dve", lambda e: e.tensor_tensor(out=tt[:, 1, 0:n], in0=psb[pp][:, 0:n], in1=rp[:, 1, 0:n], op=ALU.mult),
                         reads=[psB[pp], rpb], writes=[ttb])
                    k.op("dve", lambda e: e.tensor_tensor(out=st[:, 0:n], in0=tt[:, 0, 0:n], in1=tt[:, 1, 0:n], op=ALU.add),
                         reads=[ttb], writes=[sbf])
                k.dma("sp", dst[r0:r0 + 128, g0:g0 + n], st[:, 0:n], reads=[sbf], sem=ssem)
            return ev
        proj_fm(hT, hTb, W, [i * 128 for i in range(8)], KC, mk_rope(QR, 0, 128.0 ** -0.5), wring=wring)
        proj_fm(hT, hTb, W, [1024 + i * 128 for i in range(8)], KC, mk_rope(KR, 1024, 1.0), wring=wring)
        wring, stg = reset()
        cw = sb([128, 4, KC], F32)
        b_cw = Buf()
        for kk in range(3):
            k.dma("sp", cw[:, kk, :], mconv_w[l, kk, :].rearrange("(c p) -> p c", p=128), writes=[b_cw], sem="cw",
                  allow_slow_non_contiguous=True)
        k.dma("sp", cw[:, 3, :], mconv_b[l, :].rearrange("(c p) -> p c", p=128), writes=[b_cw], sem="cw",
              allow_slow_non_contiguous=True)
        conv_chunks(hT, hTb, W, [6144 + i * 128 for i in range(16)], cw, wring,
                    lambda ci: ((QM, ci * 128, 128.0 ** -0.5) if ci < 8 else (KM, (ci - 8) * 128, 1.0)), mode="silu", cwb=b_cw)
        wring, stg = reset()
        gb = sb([8, 4], F32)
        b_gb = Buf()
        for g in range(4):
            k.dma("sp", gb[:, g:g + 1], gate_b[l, g, :].rearrange("(h o) -> h o", o=1), writes=[b_gb], sem="gb")
        gst = Ring("gs", [sb([8, 512], F32) for _ in range(2)])
        def ev_g(ci, c0, pb, si, g0, n):
            st, sbf, ssem = gst.next()
            k.op("act", lambda e: e.activation(out=st[:, 0:n], in_=psb[pb][0:8, 0:n], func=AF.Identity, bias=gb[:, ci:ci + 1]),
                 reads=[psB[pb], b_gb], writes=[sbf])
            k.dma("sp", GATES[ci, :, g0:g0 + n], st[:, 0:n], reads=[sbf], sem=ssem)
        proj_fm(hT, hTb, W, [12288 + g * 8 for g in range(4)], KC, ev_g, M=8, wring=wring)
        wring, stg = reset()
        wv = Ring("wv", [sb([128, KC, 512], BF16) for _ in range(2)])
        vst = Ring("vs", [sb([128, 512], BF16) for _ in range(3)])
        pbi = 0
        for dst, base in ((VR, 2048), (VM, 8192)):
            for g in range(4):
                ws, wb, wsem = wv.next()
                k.dma("pool", ws, wslab_src(W, base + g * 512, 512), writes=[wb], sem=wsem)
                for t in range(NCH):
                    pb = pbi % 4
                    pbi += 1
                    for kc in range(KC):
                        k.op("pe", lambda e, pb=pb, kc=kc, ws=ws, t=t: e.matmul(psb[pb], hT[:, kc, t * 128:(t + 1) * 128],
                             ws[:, kc, :], start=(kc == 0), stop=(kc == KC - 1)), reads=[wb, hTb[t]], writes=[psB[pb]],
                             inc=(kc == KC - 1))
                    st, sbf, ssem = vst.next()
                    k.op("act", lambda e, pb=pb, st=st: e.activation(out=st, in_=psb[pb], func=AF.Copy), reads=[psB[pb]], writes=[sbf])
                    k.dma("sp", dst[t * 128:(t + 1) * 128, g * 512:(g + 1) * 512], st, reads=[sbf], sem=ssem)
        k.barrier()
        cur[0] = save

    def conv_chunks(hT, hTb, W, cols, cw, wring, dst_fn, mode, cwb=None):
        raw = Ring("rw", [sb([128, TT + 3], F32) for _ in range(1)])
        for r_ap in raw.aps:
            k.op("pool", lambda e, r_ap=r_ap: e.memset(r_ap, 0.0), writes=[raw.bufs[raw.aps.index(r_ap)]])
        acc = Ring("ac", [sb([128, TT], F32) for _ in range(1)])
        ost = Ring("oc", [sb([128, TT], BF16) for _ in range(1)])
        state = {}

        def colof(g):
            return g + 1 if g < TC else g + 2

        def ev(ci, c0, pb, si, g0, n):
            if si == 0:
                state["raw"] = raw.next()
            rw, rwb, _ = state["raw"]
            k.op("act", lambda e: e.activation(out=rw[:, colof(g0):colof(g0) + n], in_=psb[pb][:, 0:n], func=AF.Copy),
                 reads=[psB[pb]], writes=[rwb])
            if si != len(segs) - 1:
                return
            a, ab, _ = acc.next()
            def views(off):
                return [(rw[:, off:off + TC], a[:, 0:TC]), (rw[:, off + TC + 1:off + TC + 1 + TL], a[:, TC:TT])]
            for (src, dsta) in views(1):
                k.op("dve", lambda e, src=src, dsta=dsta: e.tensor_scalar(out=dsta, in0=src, scalar1=cw[:, 1, ci:ci + 1],
                     scalar2=cw[:, 3, ci:ci + 1], op0=ALU.mult, op1=ALU.add), reads=[rwb, state["cwb"]], writes=[ab])
            for kk, off in ((0, 0), (2, 2)):
                for (src, dsta) in views(off):
                    k.op("dve", lambda e, src=src, dsta=dsta, kk=kk: e.scalar_tensor_tensor(
                        out=dsta, in0=src, scalar=cw[:, kk, ci:ci + 1], in1=dsta, op0=ALU.mult, op1=ALU.add),
                        reads=[rwb, state["cwb"], ab], writes=[ab])
            if mode == "silu":
                dst, r0, scale = dst_fn(ci)
                o, ob, osem = ost.next()
                if scale == 1.0:
                    k.op("act", lambda e: e.activation(out=o, in_=a, func=AF.Silu), reads=[ab], writes=[ob])
                else:
                    k.op("act", lambda e: e.activation(out=a, in_=a, func=AF.Silu), reads=[ab], writes=[ab])
                    k.op("act", lambda e: e.activation(out=o, in_=a, func=AF.Copy, scale=scale), reads=[ab], writes=[ob])
                k.dma("sp", dst[r0:r0 + 128, :], o, reads=[ob], sem=osem)
            else:
                if ci % 2 == 0:
                    o, ob, osem = ost.next()
                    k.op("act", lambda e: e.activation(out=o, in_=a, func=AF.Silu), reads=[ab], writes=[ob])
                    state["a"] = (o, ob, osem)
                else:
                    o, ob, osem = state["a"]
                    k.op("dve", lambda e: e.tensor_tensor(out=o, in0=o, in1=a, op=ALU.mult), reads=[ob, ab], writes=[ob])
                    r0 = (ci // 2) * 128
                    k.dma("sp", ACTT[r0:r0 + 128, :], o, reads=[ob], sem=osem)
        state["cwb"] = cwb
        proj_fm(hT, hTb, W, cols, KC, ev, wring=wring)

    build.conv_chunks = conv_chunks
    def phase_gates(l):
        cur[0] = persist_end
        TM = sb([128, 6, NCH * 8], F32)
        tm_end = cur[0]
        G = [sb([8, TT], F32) for _ in range(4)]
        bG = [Buf() for _ in range(4)]
        for g in range(4):
            k.dma("sp", G[g], GATES[g], writes=[bG[g]], sem="pg%d" % g)
        one8 = sb([8, 1], F32)
        b_o8 = Buf()
        k.op("pool", lambda e: e.memset(one8, 1.0), writes=[b_o8])
        b_TM = Buf()
        cht = sb([8, 2, 2, NCH], F32)
        b_cht = Buf()
        tmp = sb([8, TT], F32)
        b_tmp = Buf()
        for d in range(2):
            IG, FG, bIG, bFG = G[2 * d], G[2 * d + 1], bG[2 * d], bG[2 * d + 1]
            k.op("act", lambda e, FG=FG: e.activation(out=FG, in_=FG, func=AF.Exp, scale=-1.0), reads=[bFG], writes=[bFG])
            k.op("act", lambda e, FG=FG: e.activation(out=FG, in_=FG, func=AF.Ln, bias=1.0), reads=[bFG], writes=[bFG])
            if d == 0:
                sl = [(slice(0, TT), 0.0)]
            else:
                sl = [("rc", None), ("rl", None)]
            def scan(out, dat, op1, init0, d=d):
                if d == 0:
                    k.op("dve", lambda e: e.tensor_tensor_scan(out=out[:, 0:TT], data0=one8[:, 0:1].to_broadcast([8, TT]),
                         data1=dat[:, 0:TT], initial=init0, op0=ALU.mult, op1=op1), reads=[b_o8, bIG, bFG, b_tmp], writes=[b_tmp, bIG, bFG])
                else:
                    k.op("dve", lambda e: e.tensor_tensor_scan(out=out[:, TC - 1::-1] if True else None,
                         data0=one8[:, 0:1].to_broadcast([8, TC]), data1=dat[:, TC - 1::-1], initial=init0, op0=ALU.mult, op1=op1),
                         reads=[b_o8, bIG, bFG, b_tmp], writes=[b_tmp, bIG, bFG])
                    k.op("dve", lambda e: e.tensor_tensor_scan(out=out[:, TT - 1:TC - 1:-1],
                         data0=one8[:, 0:1].to_broadcast([8, TL]), data1=dat[:, TT - 1:TC - 1:-1], initial=out[:, 0:1],
                         op0=ALU.mult, op1=op1), reads=[b_o8, bIG, bFG, b_tmp], writes=[b_tmp, bIG, bFG])
            scan(tmp, FG, ALU.add, 0.0)
            k.op("dve", lambda e, IG=IG: e.tensor_tensor(out=IG, in0=IG, in1=tmp, op=ALU.add), reads=[bIG, b_tmp], writes=[bIG])
            k.op("pool", lambda e, FG=FG: e.tensor_copy(out=FG, in_=tmp), reads=[b_tmp], writes=[bFG])
            scan(tmp, IG, ALU.max, 0.0)
            k.op("dve", lambda e, FG=FG: e.tensor_tensor(out=FG, in0=FG, in1=tmp, op=ALU.subtract), reads=[bFG, b_tmp], writes=[bFG])
            k.op("act", lambda e, FG=FG: e.activation(out=FG, in_=FG, func=AF.Exp), reads=[bFG], writes=[bFG])
            Mx3 = tmp.rearrange("p (c j) -> p c j", j=128)
            endj = 127 if d == 0 else 0
            Mend = Mx3[:, :, endj]
            mp = cht[:, d, 0, :]
            k.op("pool", lambda e, mp=mp: e.memset(mp, 0.0), writes=[b_cht])
            if d == 0:
                k.op("dve", lambda e, mp=mp, Mend=Mend: e.tensor_copy(out=mp[:, 1:NCH], in_=Mend[:, 0:NCH - 1]), reads=[b_tmp], writes=[b_cht])
            else:
                if NCC > 1:
                    k.op("dve", lambda e, mp=mp, Mend=Mend: e.tensor_copy(out=mp[:, 0:NCC - 1], in_=Mend[:, 1:NCC]), reads=[b_tmp], writes=[b_cht])
                k.op("dve", lambda e, mp=mp, Mend=Mend: e.tensor_copy(out=mp[:, NCC:NCH - 1], in_=Mend[:, NCC + 1:NCH]), reads=[b_tmp], writes=[b_cht])
                k.op("dve", lambda e, mp=mp, Mend=Mend: e.tensor_copy(out=mp[:, NCH - 1:NCH], in_=Mend[:, 0:1]), reads=[b_tmp], writes=[b_cht])
            k.op("dve", lambda e, mp=mp, Mend=Mend, d=d: e.tensor_tensor(out=cht[:, d, 1, :], in0=mp, in1=Mend, op=ALU.subtract),
                 reads=[b_tmp, b_cht], writes=[b_cht])
            k.op("act", lambda e, d=d: e.activation(out=cht[:, d, 1, :], in_=cht[:, d, 1, :], func=AF.Exp), reads=[b_cht], writes=[b_cht])
            wt = sb([8, TT], F32) if d == 0 else state_wt[0]
            if d == 0:
                state_wt.append(wt)
            b_wt = Buf()
            k.op("dve", lambda e, IG=IG, Mend=Mend, wt=wt: e.tensor_tensor(out=wt.rearrange("p (c j) -> p c j", j=128),
                 in0=IG.rearrange("p (c j) -> p c j", j=128), in1=Mend.unsqueeze(2).to_broadcast([8, NCH, 128]), op=ALU.subtract),
                 reads=[bIG, b_tmp], writes=[b_wt])
            k.op("act", lambda e, wt=wt: e.activation(out=wt, in_=wt, func=AF.Exp), reads=[b_wt], writes=[b_wt])
            for qi, (src, sbuf_) in enumerate(((IG, bIG), (wt, b_wt), (FG, bFG))):
                pb = d * 3 + qi
                for c in range(NCH):
                    k.op("pe", lambda e, pb=pb, c=c, src=src: e.matmul(psb[pb][:, c * 8:(c + 1) * 8], src[:, c * 128:(c + 1) * 128],
                         CF(0)[0:8, 0:8], start=True, stop=True), reads=[sbuf_, bF], writes=[psB[pb]], inc=(c == NCH - 1))
                k.op("dve", lambda e, pb=pb: e.tensor_copy(out=TM[:, pb, :], in_=psb[pb][:, 0:NCH * 8]), reads=[psB[pb]], writes=[b_TM])
            k.op("pool", lambda e: e.tensor_scalar(out=tmp, in0=tmp, scalar1=-1.0, scalar2=None, op0=ALU.mult), reads=[b_tmp], writes=[b_tmp])
            k.dma("sp", NEGMX[d * 8:(d + 1) * 8, :], tmp, reads=[b_tmp], sem="pn%d" % d)
            for q in range(2):
                k.dma("sp", CHT[q, d * 8:(d + 1) * 8, :], cht[:, d, q, :], reads=[b_cht], sem="pc%d" % q)
        k.barrier()
        cur[0] = tm_end
        return TM, b_TM

    state_wt = []
    def layer_consts(l):
        de = sb([128, 16], F32)
        b_de = Buf()
        k.dma("sp", de, decay_e[l:l + 1].rearrange("o a b -> o (a b)").partition_broadcast(128), writes=[b_de], sem="ld")
        lg = sb([128, 16], F32)
        b_lg = Buf()
        k.op("act", lambda e: e.activation(out=lg, in_=de, func=AF.Exp, scale=-float(np.log(2.0))), reads=[b_de], writes=[b_lg])
        k.op("act", lambda e: e.activation(out=lg, in_=lg, func=AF.Ln, scale=-1.0, bias=1.0), reads=[b_lg], writes=[b_lg])
        mask = sb([128, 8, 128], BF16)
        qf = sb([128, 8, 128], BF16)
        qb = sb([128, 8, 128], BF16)
        kd = sb([128, 2, 8], F32)
        cd = sb([128, 2, 8], F32)
        t1 = sb([128, 128], F32)
        t2 = sb([128, 128], F32)
        b_c, b_t1, b_t2 = Buf(), Buf(), Buf()
        for h in range(8):
            k.op("act", lambda e, h=h: e.activation(out=t1, in_=CF(1), func=AF.Exp, scale=lg[:, h:h + 1]), reads=[bF, b_lg], writes=[b_t1])
            k.op("act", lambda e, h=h: e.activation(out=t2, in_=CF(2), func=AF.Exp, scale=lg[:, 8 + h:9 + h]), reads=[bF, b_lg], writes=[b_t2])
            k.op("dve", lambda e: e.tensor_tensor(out=t1, in0=t1, in1=CF(3), op=ALU.mult), reads=[b_t1, bF], writes=[b_t1])
            k.op("dve", lambda e: e.tensor_tensor(out=t2, in0=t2, in1=CF(4), op=ALU.mult), reads=[b_t2, bF], writes=[b_t2])
            k.op("dve", lambda e, h=h: e.tensor_tensor(out=mask[:, h, :], in0=t1, in1=t2, op=ALU.add), reads=[b_t1, b_t2], writes=[b_c])
            k.op("act", lambda e, h=h: e.activation(out=qf[:, h, :], in_=CF(5), func=AF.Exp, scale=lg[:, h:h + 1]), reads=[bF, b_lg], writes=[b_c])
            k.op("act", lambda e, h=h: e.activation(out=qb[:, h, :], in_=CF(6), func=AF.Exp, scale=lg[:, 8 + h:9 + h]), reads=[bF, b_lg], writes=[b_c])
            k.op("act", lambda e, h=h: e.activation(out=kd[:, 0, h:h + 1], in_=CF(9)[:, 0:1], func=AF.Exp, scale=lg[:, h:h + 1]), reads=[bF, b_lg], writes=[b_c])
            k.op("act", lambda e, h=h: e.activation(out=kd[:, 1, h:h + 1], in_=CF(10)[:, 0:1], func=AF.Exp, scale=lg[:, 8 + h:9 + h]), reads=[bF, b_lg], writes=[b_c])
        k.op("act", lambda e: e.activation(out=cd.rearrange("p a b -> p (a b)"), in_=lg, func=AF.Exp, scale=128.0), reads=[b_lg], writes=[b_c])
        return dict(mask=mask, qf=qf, qb=qb, kd=kd, cd=cd, b=b_c)

    def phase_scan(l, TM, b_TM):
        RC = layer_consts(l)
        bRC = RC["b"]
        chb = sb([128, 2, 16, NCH], F32)
        b_chb = Buf()
        k.dma("sp", chb, CHT.rearrange("q r c -> (q r c)").rearrange("(o n) -> o n", o=1).partition_broadcast(128)
              if False else CHT.rearrange("q r c -> (q r c)").partition_broadcast(128), writes=[b_chb], sem="chb")
        hn = sb([128, 2, KC], F32)
        b_hn = Buf()
        for i in range(2):
            k.dma("sp", hn[:, i, :], hnorm_w[l, i, :].rearrange("(c p) -> p c", p=128), writes=[b_hn], sem="hn",
                  allow_slow_non_contiguous=True)
        S32 = sb([128, 8, 256], F32)
        Sbf = sb([128, 8, 256], BF16)
        C32 = sb([128, 8, 257], F32)
        Cbf = sb([128, 8, 257], BF16)
        bS = [Buf() for _ in range(8)]
        bSb = [Buf() for _ in range(8)]
        bC = [Buf() for _ in range(8)]
        bCb = [Buf() for _ in range(8)]
        kin = Ring("ki", [sb([128, 4, 8, 128], BF16) for _ in range(2)])
        vin = Ring("vi", [sb([128, 2, 8, 257], BF16) for _ in range(2)])
        for v_ap, vb in zip(vin.aps, vin.bufs):
            k.op("pool", lambda e, v_ap=v_ap: e.memset(v_ap[:, 1, :, 256:257], 1.0), writes=[vb])
        kt = Ring("kt", [sb([128, 128], BF16) for _ in range(8)])
        pT = [psb[6].bitcast(BF16), psb[7].bitcast(BF16)]
        tcount = [0]

        def zero_states():
            k.op("pool", lambda e: e.memset(S32, 0.0), writes=bS)
            k.op("pool", lambda e: e.memset(Sbf, 0.0), writes=bSb)
            k.op("pool", lambda e: e.memset(C32, 0.0), writes=bC)
            k.op("pool", lambda e: e.memset(Cbf, 0.0), writes=bCb)

        def load_chunk(c, need_q):
            ki, kb, ksem = kin.next()
            for i, src in enumerate((QR, KR, QM, KM)):
                if not need_q and i in (0, 2):
                    continue
                k.dma("sp", ki[:, i], src[:, c * 128:(c + 1) * 128].rearrange("(h d) t -> d h t", d=128), writes=[kb], sem=ksem)
            vi, vb, vsem = vin.next()
            k.dma("sp", vi[:, 0, :, 0:256], VR[c * 128:(c + 1) * 128, :].rearrange("t (h v) -> t h v", v=256), writes=[vb], sem=vsem)
            k.dma("sp", vi[:, 1, :, 0:256], VM[c * 128:(c + 1) * 128, :].rearrange("t (h v) -> t h v", v=256), writes=[vb], sem=vsem)
            return ki, kb, vi, vb

        def state_update(c, h, ki, kb, vi, vb, d):
            for br in range(2):
                ti = tcount[0]
                tcount[0] += 1
                slot = ti % 8
                pt = pT[0][:, slot * 128:(slot + 1) * 128]
                k.op("pe", lambda e, pt=pt, br=br: e.transpose(pt, ki[:, 1 + 2 * br, h, :], cI), reads=[kb, bI], writes=[psB[6]])
                kk_, kkb, _ = kt.next()
                if br == 0:
                    sc = RC["kd"][:, d, h:h + 1]
                    rd = [bRC]
                else:
                    sc = TM[:, d * 3 + 1, c * 8 + h:c * 8 + h + 1]
                    rd = [b_TM]
                k.op("act", lambda e, kk_=kk_, pt=pt, sc=sc: e.activation(out=kk_, in_=pt, func=AF.Copy, scale=sc),
                     reads=[psB[6]] + rd, writes=[kkb])
                pb = 4 + (ti % 2)
                if br == 0:
                    k.op("pe", lambda e, pb=pb, kk_=kk_: e.matmul(psb[pb][:, 0:256], kk_, vi[:, 0, h, 0:256], start=True, stop=True),
                         reads=[kkb, vb], writes=[psB[pb]])
                    k.op("dve", lambda e, pb=pb: e.scalar_tensor_tensor(out=S32[:, h, :], in0=S32[:, h, :], scalar=RC["cd"][:, d, h:h + 1],
                         in1=psb[pb][:, 0:256], op0=ALU.mult, op1=ALU.add), reads=[psB[pb], bS[h], bRC], writes=[bS[h]])
                    k.op("pool", lambda e: e.tensor_copy(out=Sbf[:, h, :], in_=S32[:, h, :]), reads=[bS[h]], writes=[bSb[h]])
                else:
                    k.op("pe", lambda e, pb=pb, kk_=kk_: e.matmul(psb[pb][:, 0:257], kk_, vi[:, 1, h, :], start=True, stop=True),
                         reads=[kkb, vb], writes=[psB[pb]])
                    k.op("dve", lambda e, pb=pb: e.scalar_tensor_tensor(out=C32[:, h, :], in0=C32[:, h, :],
                         scalar=chb[:, 1, d * 8 + h, c:c + 1], in1=psb[pb][:, 0:257], op0=ALU.mult, op1=ALU.add),
                         reads=[psB[pb], bC[h], b_chb], writes=[bC[h]])
                    k.op("pool", lambda e: e.tensor_copy(out=Cbf[:, h, :], in_=C32[:, h, :]), reads=[bC[h]], writes=[bCb[h]])

        zero_states()
        BWD = list(range(NCC - 1, -1, -1)) + list(range(NCH - 1, NCC - 1, -1))
        for c in BWD:
            ki, kb, vi, vb = load_chunk(c, False)
            k.dma("sp", SBR[c], Sbf, reads=bSb, sem="s1r")
            k.dma("sp", SBM[c], Cbf, reads=bCb, sem="s1m")
            for h in range(8):
                state_update(c, h, ki, kb, vi, vb, 1)
        k.barrier()
        zero_states()
        sbin = Ring("sn", [sb([128, 2, 8, 257], BF16) for _ in range(2)])
        uin = Ring("ui", [sb([128, 16, 128], F32) for _ in range(2)])
        gin = Ring("gi", [sb([128, 2, KC, 128], BF16) for _ in range(2)])
        wk = Ring("wk", [sb([128, 128], BF16) for _ in range(28)])
        wf = Ring("wf", [sb([128, 128], F32) for _ in range(12)])
        ytm = Ring("yt", [sb([128, 2, 8, 256], F32) for _ in range(2)])
        ynb = Ring("yn", [sb([128, 2, 2048], BF16) for _ in range(1)])
        oT = Ring("ot", [sb([128, 2, KC, 128], BF16) for _ in range(1)])
        small = Ring("sm", [sb([128, 8], F32) for _ in range(16)])
        hst = Ring("hs", [sb([128, 2, 16], F32) for _ in range(2)])
        junk = sb([128, 256], BF16)
        b_junk = Buf()
        def post_body(c, yt, ytb, hs, hsb, gi, gb_):
            k.op("act", lambda e: e.activation(out=hs[:, :, 0:8], in_=hs[:, :, 0:8], func=AF.Sqrt, scale=1.0 / 256, bias=EPS),
                 reads=[hsb], writes=[hsb])
            k.op("dve", lambda e: e.reciprocal(out=hs[:, :, 8:16], in_=hs[:, :, 0:8]), reads=[hsb], writes=[hsb])
            yn, ynb_, _ = ynb.next()
            for br in range(2):
                k.op("dve" if br == 0 else "pool", lambda e, br=br: e.tensor_tensor(out=yn[:, br, :].rearrange("p (h v) -> p h v", v=256),
                     in0=yt[:, br], in1=hs[:, br, 8:16].unsqueeze(2).to_broadcast([128, 8, 256]), op=ALU.mult),
                     reads=[ytb, hsb], writes=[ynb_])
            o_, ob, osem = oT.next()
            for br in range(2):
                for half in range(4):
                    for q in range(4, 8):
                        kc = half * 4 + q - 4
                        k.op("pe", lambda e, q=q, kc=kc, br=br: e.transpose(pT[1][:, q * 128:(q + 1) * 128], yn[:, br, kc * 128:(kc + 1) * 128], cI),
                             reads=[ynb_, bI], writes=[psB[7]], inc=(q == 7))
                    for q in range(4, 8):
                        kc = half * 4 + q - 4
                        k.op("dve", lambda e, q=q, kc=kc, br=br: e.scalar_tensor_tensor(out=o_[:, br, kc, :], in0=pT[1][:, q * 128:(q + 1) * 128],
                             scalar=hn[:, br, kc:kc + 1], in1=gi[:, br, kc, :], op0=ALU.mult, op1=ALU.mult),
                             reads=[psB[7], b_hn, gb_], writes=[ob])
            k.dma("sp", YRT[:, c * 128:(c + 1) * 128].rearrange("(kc p) t -> p kc t", p=128), o_[:, 0], reads=[ob], sem=osem)
            k.dma("sp", YMT[:, c * 128:(c + 1) * 128].rearrange("(kc p) t -> p kc t", p=128), o_[:, 1], reads=[ob], sem=osem)

        PB = {}

        def pbuf(key):
            kk_ = key[0] if isinstance(key, tuple) else key
            if kk_ == "sT":
                return psB[key[1]]
            return psB[{"y": 3, "dS": 3, "n0": 4, "n1": 5, "b5": 6, "pt": 7}[kk_]]
        n1r = Ring("n1", [sb([128, 256], F32) for _ in range(4)])
        CH = {}
        IT = {}
        pT0 = pT[1]
        smallps = psb[6][:, 256:512]

        def chunk_ctx(c):
            ki, kb, vi, vb = load_chunk(c, True)
            sn, snb, snsem = sbin.next()
            k.dma("sp", sn[:, 0, :, 0:256], SBR[c], writes=[snb], sem=snsem)
            k.dma("sp", sn[:, 1], SBM[c], writes=[snb], sem=snsem)
            ui, ub, usem = uin.next()
            k.dma("sp", ui, NEGMX[:, c * 128:(c + 1) * 128].partition_broadcast(128), writes=[ub], sem=usem)
            gi, gb_, gsem = gin.next()
            k.dma("sp", gi[:, 0], RG[:, c * 128:(c + 1) * 128].rearrange("(kc p) t -> p kc t", p=128), writes=[gb_], sem=gsem)
            k.dma("sp", gi[:, 1], MO[:, c * 128:(c + 1) * 128].rearrange("(kc p) t -> p kc t", p=128), writes=[gb_], sem=gsem)
            yt, ytb, _ = ytm.next()
            hs, hsb, _ = hst.next()
            CH[c] = dict(ki=ki, kb=kb, vi=vi, vb=vb, sn=sn, snb=snb, ui=ui, ub=ub, gi=gi, gb_=gb_, yt=yt, ytb=ytb, hs=hs, hsb=hsb)

        def st0(i):
            c, h = divmod(i, 8)
            if h == 0:
                chunk_ctx(c)
            C = CH[c]
            ki, kb, ui, ub = C["ki"], C["kb"], C["ui"], C["ub"]
            I = IT[i] = {}
            slot = i % 3
            bank = slot
            o0 = 0
            spsb = pbuf(("sT", slot))
            sps = psb[bank][:, o0:o0 + 128]
            sps2 = psb[bank][:, o0 + 128:o0 + 256]
            I.update(sps=sps, sps2=sps2, spsb=spsb)
            k.op("pe", lambda e: e.matmul(sps, ki[:, 1, h, :], ki[:, 0, h, :], start=True, stop=True), reads=[kb], writes=[spsb], inc=False)
            k.op("pe", lambda e: e.matmul(sps2, ki[:, 3, h, :], ki[:, 2, h, :], start=True, stop=True), reads=[kb], writes=[spsb])
            I["f1"] = []
            I["f2"] = []
            for d in range(2):
                r = d * 8 + h
                f1, f1b, _ = wf.next()
                k.op("dve", lambda e, f1=f1, d=d, r=r: e.scalar_tensor_tensor(out=f1, in0=ui[:, r, :],
                     scalar=TM[:, d * 3 + 0, c * 8 + h:c * 8 + h + 1], in1=CF(7 + d), op0=ALU.add, op1=ALU.min),
                     reads=[ub, b_TM, bF], writes=[f1b])
                I["f1"].append((f1, f1b))
                f2, f2b, _ = wf.next()
                k.op("act", lambda e, f2=f2, r=r: e.activation(out=f2, in_=ui[:, r, :], func=AF.Exp, bias=chb[:, 0, r, c:c + 1]),
                     reads=[ub, b_chb], writes=[f2b])
                I["f2"].append((f2, f2b))
            qf_, qfb, _ = wk.next()
            k.op("pool", lambda e: e.tensor_tensor(out=qf_, in0=ki[:, 0, h, :], in1=RC["qf"][:, h, :], op=ALU.mult), reads=[kb, bRC], writes=[qfb])
            qb_, qbb, _ = wk.next()
            k.op("pool", lambda e: e.tensor_tensor(out=qb_, in0=ki[:, 0, h, :], in1=RC["qb"][:, h, :], op=ALU.mult), reads=[kb, bRC], writes=[qbb])
            I.update(qf=(qf_, qfb), qb=(qb_, qbb))
            I["pt"] = []
            for br in range(2):
                ts_ = (i % 2) * 2 + br
                pt = pT0[:, ts_ * 128:(ts_ + 1) * 128]
                ptb = pbuf(("pt", ts_))
                k.op("pe", lambda e, pt=pt, br=br: e.transpose(pt, ki[:, 1 + 2 * br, h, :], cI), reads=[kb, bI], writes=[ptb])
                I["pt"].append((pt, ptb))

        def st1(i):
            c, h = divmod(i, 8)
            C, I = CH[c], IT[i]
            ki, kb = C["ki"], C["kb"]
            for d in range(2):
                f1, f1b = I["f1"][d]
                k.op("act", lambda e, f1=f1: e.activation(out=f1, in_=f1, func=AF.Exp), reads=[f1b], writes=[f1b])
            I["kk"] = []
            for br in range(2):
                pt, ptb = I["pt"][br]
                kk_, kkb, _ = kt.next()
                if br == 0:
                    sc, rd = RC["kd"][:, 0, h:h + 1], [bRC]
                else:
                    sc, rd = TM[:, 0 * 3 + 1, c * 8 + h:c * 8 + h + 1], [b_TM]
                k.op("act", lambda e, kk_=kk_, pt=pt, sc=sc: e.activation(out=kk_, in_=pt, func=AF.Copy, scale=sc), reads=[ptb] + rd, writes=[kkb])
                I["kk"].append((kk_, kkb))
            sm_, smb, _ = wk.next()
            sps = I["sps"]
            k.op("dve", lambda e: e.tensor_tensor(out=sm_, in0=sps, in1=RC["mask"][:, h, :], op=ALU.mult), reads=[I["spsb"], bRC], writes=[smb])
            I["sm"] = (sm_, smb)
            I["qa"] = []
            for d in range(2):
                f2, f2b = I["f2"][d]
                qa, qab, _ = wk.next()
                k.op("pool", lambda e, qa=qa, f2=f2: e.tensor_tensor(out=qa, in0=ki[:, 2, h, :], in1=f2, op=ALU.mult), reads=[kb, f2b], writes=[qab])
                I["qa"].append((qa, qab))

        def st2(i):
            c, h = divmod(i, 8)
            C, I = CH[c], IT[i]
            vi, vb = C["vi"], C["vb"]
            I["sd"] = []
            sps2 = I["sps2"]
            for d in range(2):
                f1, f1b = I["f1"][d]
                sd, sdb, _ = wk.next()
                k.op("dve", lambda e, sd=sd, f1=f1: e.tensor_tensor(out=sd, in0=sps2, in1=f1, op=ALU.mult), reads=[I["spsb"], f1b], writes=[sdb])
                I["sd"].append((sd, sdb))
            kr, krb = I["kk"][0]
            km, kmb = I["kk"][1]
            b5 = pbuf("b5")
            bds = pbuf("dS")
            k.op("pe", lambda e: e.matmul(psb[3][:, 256:512], kr, vi[:, 0, h, 0:256], start=True, stop=True), reads=[krb, vb], writes=[bds])
            k.op("pe", lambda e: e.matmul(psb[6][:, 0:257], km, vi[:, 1, h, :], start=True, stop=True), reads=[kmb, vb], writes=[b5])

        def st3(i):
            c, h = divmod(i, 8)
            C, I = CH[c], IT[i]
            vi, vb, sn, snb = C["vi"], C["vb"], C["sn"], C["snb"]
            yb_, n0b, n1b = pbuf("y"), pbuf("n0"), pbuf("n1")
            ypa = psb[3][:, 0:256]
            npa = [psb[4][:, 0:257], psb[5][:, 0:257]]
            nb_ = [n0b, n1b]
            sm_, smb = I["sm"]
            qf_, qfb = I["qf"]
            qb_, qbb = I["qb"]
            k.op("pe", lambda e: e.matmul(ypa, sm_, vi[:, 0, h, 0:256], start=True, stop=False), reads=[smb, vb], writes=[yb_], inc=False)
            k.op("pe", lambda e: e.matmul(ypa, qf_, Sbf[:, h, :], start=False, stop=False), reads=[qfb, bSb[h]], writes=[yb_], inc=False)
            k.op("pe", lambda e: e.matmul(ypa, qb_, sn[:, 0, h, 0:256], start=False, stop=True), reads=[qbb, snb], writes=[yb_])
            for d in range(2):
                sd, sdb = I["sd"][d]
                qa, qab = I["qa"][d]
                st_t = Cbf if d == 0 else sn[:, 1]
                st_b = bCb[h] if d == 0 else snb
                k.op("pe", lambda e, d=d, sd=sd: e.matmul(npa[d], sd, vi[:, 1, h, :], start=True, stop=False), reads=[sdb, vb], writes=[nb_[d]], inc=False)
                k.op("pe", lambda e, d=d, qa=qa, st_t=st_t: e.matmul(npa[d], qa, st_t[:, h, :], start=False, stop=True), reads=[qab, st_b], writes=[nb_[d]])
            I.update(ypa=ypa, yb_=yb_, npa=npa, nb_=nb_)
            b5 = pbuf("b5")
            bds = pbuf("dS")
            k.op("dve", lambda e: e.scalar_tensor_tensor(out=S32[:, h, :], in0=S32[:, h, :], scalar=RC["cd"][:, 0, h:h + 1],
                 in1=psb[3][:, 256:512], op0=ALU.mult, op1=ALU.add), reads=[bds, bS[h], bRC], writes=[bS[h]])
            k.op("dve", lambda e: e.scalar_tensor_tensor(out=C32[:, h, :], in0=C32[:, h, :], scalar=chb[:, 1, h, c:c + 1],
                 in1=psb[6][:, 0:257], op0=ALU.mult, op1=ALU.add), reads=[b5, bC[h], b_chb], writes=[bC[h]])

        def st4(i):
            c, h = divmod(i, 8)
            C, I = CH[c], IT[i]
            yt, ytb = C["yt"], C["ytb"]
            ypa, npa = I["ypa"], I["npa"]
            k.op("act", lambda e: e.activation(out=yt[:, 0, h, :], in_=ypa, func=AF.Copy), reads=[I["yb_"]], writes=[ytb])
            k.op("act", lambda e: e.activation(out=yt[:, 1, h, :], in_=npa[0][:, 0:256], func=AF.Copy), reads=[I["nb_"][0]], writes=[ytb])
            n1c, n1cb, _ = n1r.next()
            k.op("act", lambda e: e.activation(out=n1c, in_=npa[1][:, 0:256], func=AF.Copy), reads=[I["nb_"][1]], writes=[n1cb])
            I["n1c"] = (n1c, n1cb)
            I["s8"] = []
            for d in range(2):
                den, denb = npa[d][:, 256:257], I["nb_"][d]
                s8, s8b, _ = small.next()
                k.op("act", lambda e, s8=s8, den=den: e.activation(out=s8[:, 0:1], in_=den, func=AF.Abs), reads=[denb], writes=[s8b])
                I["s8"].append((s8, s8b))
            k.op("pool", lambda e: e.tensor_copy(out=Sbf[:, h, :], in_=S32[:, h, :]), reads=[bS[h]], writes=[bSb[h]])
            k.op("pool", lambda e: e.tensor_copy(out=Cbf[:, h, :], in_=C32[:, h, :]), reads=[bC[h]], writes=[bCb[h]])

        def st5(i):
            c, h = divmod(i, 8)
            C, I = CH[c], IT[i]
            yt, ytb, hs, hsb = C["yt"], C["ytb"], C["hs"], C["hsb"]
            for d in range(2):
                s8, s8b = I["s8"][d]
                k.op("dve", lambda e, s8=s8, d=d: e.tensor_tensor(out=s8[:, 0:1], in0=s8[:, 0:1],
                     in1=TM[:, d * 3 + 2, c * 8 + h:c * 8 + h + 1], op=ALU.max), reads=[s8b, b_TM], writes=[s8b])
                k.op("dve", lambda e, s8=s8: e.reciprocal(out=s8[:, 1:2], in_=s8[:, 0:1]), reads=[s8b], writes=[s8b])
            k.op("act", lambda e: e.activation(out=junk, in_=yt[:, 0, h, :], func=AF.Square, accum_out=hs[:, 0, h:h + 1]),
                 reads=[ytb], writes=[b_junk, hsb])

        def st6(i):
            c, h = divmod(i, 8)
            C, I = CH[c], IT[i]
            yt, ytb = C["yt"], C["ytb"]
            s80, s80b = I["s8"][0]
            s81, s81b = I["s8"][1]
            n1c, n1cb = I["n1c"]
            k.op("dve", lambda e: e.tensor_scalar(out=yt[:, 1, h, :], in0=yt[:, 1, h, :], scalar1=s80[:, 1:2], scalar2=None, op0=ALU.mult),
                 reads=[ytb, s80b], writes=[ytb])
            k.op("dve", lambda e: e.scalar_tensor_tensor(out=yt[:, 1, h, :], in0=n1c, scalar=s81[:, 1:2], in1=yt[:, 1, h, :],
                 op0=ALU.mult, op1=ALU.add), reads=[n1cb, s81b, ytb], writes=[ytb])

        def st7(i):
            c, h = divmod(i, 8)
            C = CH[c]
            yt, ytb, hs, hsb = C["yt"], C["ytb"], C["hs"], C["hsb"]
            k.op("act", lambda e: e.activation(out=junk, in_=yt[:, 1, h, :], func=AF.Square, accum_out=hs[:, 1, h:h + 1]),
                 reads=[ytb], writes=[b_junk, hsb])
            if h == 7:
                post_body(c, C["yt"], C["ytb"], C["hs"], C["hsb"], C["gi"], C["gb_"])
                del CH[c]
            del IT[i]

        steps = [st0, st1, st2, st3, st4, st5, st6, st7]
        NI = NCH * 8
        for tick in range(NI + len(steps) - 1):
            for s_ in range(len(steps) - 1, -1, -1):
                i = tick - s_
                if 0 <= i < NI:
                    steps[s_](i)
        k.barrier()

    def resid_update(t, ps_list, gi_row, xr, gnt, b_gnt, last_layer, stg2, src_sb=None, rstd_ap=None, rstd_b=None):
        pass

    def phase_outproj(l, last):
        cur[0] = persist_end
        for step in range(2):
            save = cur[0]
            aT, aTb = load_resident(YRT if step == 0 else YMT, 2048)
            wring = Ring("wo", [sb([128, KC, 128], BF16) for _ in range(3)])
            gt = Ring("og", [sb([128, 512], BF16) for _ in range(3)])
            zt = Ring("oz", [sb([128, 512], F32) for _ in range(3)])
            yo = Ring("oy", [sb([128, 512], BF16) for _ in range(3)])
            Wm = w_ro[l] if step == 0 else w_mo[l]

            def ev(ci, c0, pb, si, g0, n, step=step):
                g_, gb_, gsem = gt.next()
                k.dma("sp", g_[:, 0:n], GRM[step * 2048 + c0:step * 2048 + c0 + 128, g0:g0 + n], writes=[gb_], sem=gsem)
                z_, zb, zsem = zt.next()
                if step == 0:
                    k.op("dve", lambda e: e.tensor_tensor(out=z_[:, 0:n], in0=psb[pb][:, 0:n], in1=g_[:, 0:n], op=ALU.mult),
                         reads=[psB[pb], gb_], writes=[zb])
                    k.dma("sp", ZR[c0:c0 + 128, g0:g0 + n], z_[:, 0:n], reads=[zb], sem=zsem)
                else:
                    k.dma("sp", z_[:, 0:n], ZR[c0:c0 + 128, g0:g0 + n], writes=[zb], sem=zsem)
                    y_, yb, ysem = yo.next()
                    k.op("dve", lambda e: e.tensor_tensor(out=g_[:, 0:n], in0=psb[pb][:, 0:n], in1=g_[:, 0:n], op=ALU.mult),
                         reads=[psB[pb], gb_], writes=[gb_])
                    k.op("pool", lambda e: e.tensor_tensor(out=y_[:, 0:n], in0=g_[:, 0:n], in1=z_[:, 0:n], op=ALU.add),
                         reads=[gb_, zb], writes=[yb])
                    k.dma("sp", YT[c0:c0 + 128, g0:g0 + n], y_[:, 0:n], reads=[yb], sem=ysem)
            proj_fm(aT, aTb, Wm, [i * 128 for i in range(16)], KC, ev, wring=wring)
            k.barrier()
            cur[0] = save
        wres = sb([128, KC, D], BF16)
        b_wres = Buf()
        for g in range(4):
            k.dma("pool", wres[:, :, g * 512:(g + 1) * 512], wslab_src(w_o[l], g * 512, 512), writes=[b_wres], sem="wr%d" % g)
        gn = sb([128, 2, D], F32)
        b_gn = Buf()
        for m in range(2):
            k.dma("sp", gn[:, m, :], GNROW[m:m + 1, :].partition_broadcast(128), writes=[b_gn], sem="gn")
        yin = Ring("pyi", [sb([128, KC, 128], BF16) for _ in range(2)])
        xin = Ring("pxi", [sb([128, D], F32) for _ in range(2)])
        ot = Ring("pot", [sb([128, D], F32) for _ in range(2)])
        junk = sb([128, D], BF16)
        b_junk = Buf()
        st4 = Ring("ps4", [sb([128, 4], F32) for _ in range(4)])
        for t in range(NCH):
            yi, yib, ysem = yin.next()
            k.dma("sp", yi, YT[:, t * 128:(t + 1) * 128].rearrange("(kc p) t -> p kc t", p=128), writes=[yib], sem=ysem)
            xi, xib, xsem = xin.next()
            k.dma("sp", xi, X[t * 128:(t + 1) * 128, :], writes=[xib], sem=xsem)
            o_, ob, osem = ot.next()
            s4, s4b, _ = st4.next()
            base = (t % 2) * 4
            for g in range(4):
                pb = base + g
                for kc in range(KC):
                    k.op("pe", lambda e, pb=pb, kc=kc, g=g, yi=yi: e.matmul(psb[pb], yi[:, kc, :], wres[:, kc, g * 512:(g + 1) * 512],
                         start=(kc == 0), stop=(kc == KC - 1)), reads=[yib, b_wres], writes=[psB[pb]], inc=(kc == KC - 1))
                k.op("act", lambda e, pb=pb, g=g, o_=o_: e.activation(out=o_[:, g * 512:(g + 1) * 512], in_=psb[pb], func=AF.Copy),
                     reads=[psB[pb]], writes=[ob])
            finish_resid(t, o_, ob, xi, xib, gn, b_gn, s4, s4b, junk, b_junk, osem, last=False)
        k.barrier()

    def finish_resid(t, o_, ob, xi, xib, gn, b_gn, s4, s4b, junk, b_junk, osem, last):
        m = 0 if t >= NCC else 1
        k.op("act", lambda e: e.activation(out=junk, in_=o_, func=AF.Square, accum_out=s4[:, 0:1]), reads=[ob], writes=[b_junk, s4b])
        k.op("act", lambda e: e.activation(out=s4[:, 1:2], in_=s4[:, 0:1], func=AF.Sqrt, scale=1.0 / D, bias=EPS), reads=[s4b], writes=[s4b])
        k.op("dve", lambda e: e.reciprocal(out=s4[:, 2:3], in_=s4[:, 1:2]), reads=[s4b], writes=[s4b])
        k.op("dve", lambda e: e.scalar_tensor_tensor(out=o_, in0=o_, scalar=s4[:, 2:3], in1=gn[:, m, :], op0=ALU.mult, op1=ALU.mult),
             reads=[ob, s4b, b_gn], writes=[ob])
        k.op("pool", lambda e: e.tensor_tensor(out=o_, in0=o_, in1=xi, op=ALU.add), reads=[ob, xib], writes=[ob])
        if last and t >= NCC:
            k.dma("sp", y_out[(t - NCC) * 128:(t - NCC + 1) * 128, :], o_, reads=[ob], sem=osem)
        else:
            k.dma("sp", X[t * 128:(t + 1) * 128, :], o_, reads=[ob], sem=osem)

    def phase_ffn_up(l, hT, hTb):
        save = cur[0]
        wring = Ring("wu", [sb([128, KC, 128], BF16) for _ in range(2)])
        cw = sb([128, 4, 2 * FC], F32)
        b_cw = Buf()
        for kk in range(4):
            for half in range(2):
                src = (fconv_w[l, kk, half * FF:(half + 1) * FF] if kk < 3 else fconv_b[l, half * FF:(half + 1) * FF])
                k.dma("sp", cw[:, kk, :].rearrange("p (c two) -> p c two", two=2)[:, :, half], src.rearrange("(c p) -> p c", p=128),
                      writes=[b_cw], sem="fcw", allow_slow_non_contiguous=True)
        cols = []
        for c in range(FC):
            cols += [c * 128, FF + c * 128]
        build.conv_chunks(hT, hTb, w_up[l], cols, cw, wring, None, mode="ffn", cwb=b_cw)
        k.barrier()
        cur[0] = save

    def phase_ffn_down(l, last):
        cur[0] = persist_end
        HALF = 1024
        ssq = sb([128, NCH, 2], F32)
        ssq2 = sb([128, NCH, 2], F32)
        b_ssq = Buf()
        ssq_end = [cur[0]]
        wres = sb([128, FC, HALF], BF16)
        junk = sb([128, 512], BF16)
        b_junk = Buf()
        ain = Ring("dai", [sb([128, FC, 128], BF16) for _ in range(2)])
        ost = Ring("dos", [sb([128, HALF], F32) for _ in range(2)])
        b_wres = Buf()
        for hcol in range(2):
            for g in range(2):
                k.dma("pool", wres[:, :, g * 512:(g + 1) * 512],
                      w_dn[l][:, hcol * HALF + g * 512:hcol * HALF + (g + 1) * 512].rearrange("(kc p) n -> p kc n", p=128),
                      writes=[b_wres], sem="dw%d" % g)
            for t in range(NCH):
                ai, aib, asem = ain.next()
                k.dma("sp", ai, ACTT[:, t * 128:(t + 1) * 128].rearrange("(kc p) t -> p kc t", p=128), writes=[aib], sem=asem)
                o_, ob, osem = ost.next()
                for g in range(2):
                    pb = (t % 2) * 2 + g
                    for kc in range(FC):
                        k.op("pe", lambda e, pb=pb, kc=kc, g=g, ai=ai: e.matmul(psb[pb], ai[:, kc, :], wres[:, kc, g * 512:(g + 1) * 512],
                             start=(kc == 0), stop=(kc == FC - 1)), reads=[aib, b_wres], writes=[psB[pb]], inc=(kc == FC - 1))
                    k.op("act", lambda e, pb=pb, g=g, o_=o_: e.activation(out=o_[:, g * 512:(g + 1) * 512], in_=psb[pb], func=AF.Copy),
                         reads=[psB[pb]], writes=[ob])
                k.op("act", lambda e, o_=o_, t=t, hcol=hcol: e.activation(out=junk, in_=o_[:, 0:512], func=AF.Square,
                     accum_out=ssq[:, t, hcol:hcol + 1]), reads=[ob], writes=[b_junk, b_ssq])
                k.op("act", lambda e, o_=o_, t=t, hcol=hcol: e.activation(out=junk, in_=o_[:, 512:1024], func=AF.Square,
                     accum_out=ssq2[:, t, hcol:hcol + 1]), reads=[ob], writes=[b_junk, b_ssq])
                k.dma("sp", FO[t * 128:(t + 1) * 128, hcol * HALF:(hcol + 1) * HALF], o_, reads=[ob], sem=osem)
        k.barrier()
        cur[0] = ssq_end[0]
        gn = sb([128, 2, D], F32)
        b_gn = Buf()
        for m in range(2):
            k.dma("sp", gn[:, m, :], GNROW[2 + m:3 + m, :].partition_broadcast(128), writes=[b_gn], sem="gn")
        fin = Ring("dfi", [sb([128, D], F32) for _ in range(2)])
        xin = Ring("dxi", [sb([128, D], F32) for _ in range(2)])
        st4 = Ring("ds4", [sb([128, 4], F32) for _ in range(4)])
        for t in range(NCH):
            o_, ob, osem = fin.next()
            k.dma("sp", o_, FO[t * 128:(t + 1) * 128, :], writes=[ob], sem=osem + "l")
            xi, xib, xsem = xin.next()
            k.dma("sp", xi, X[t * 128:(t + 1) * 128, :], writes=[xib], sem=xsem)
            s4, s4b, _ = st4.next()
            m = 0 if t >= NCC else 1
            k.op("dve", lambda e, s4=s4, t=t: e.tensor_tensor(out=s4[:, 0:2], in0=ssq[:, t, :], in1=ssq2[:, t, :], op=ALU.add), reads=[b_ssq], writes=[s4b])
            k.op("dve", lambda e, s4=s4: e.tensor_tensor(out=s4[:, 0:1], in0=s4[:, 0:1], in1=s4[:, 1:2], op=ALU.add), reads=[s4b], writes=[s4b])
            k.op("act", lambda e, s4=s4: e.activation(out=s4[:, 1:2], in_=s4[:, 0:1], func=AF.Sqrt, scale=1.0 / D, bias=EPS), reads=[s4b], writes=[s4b])
            k.op("dve", lambda e, s4=s4: e.reciprocal(out=s4[:, 2:3], in_=s4[:, 1:2]), reads=[s4b], writes=[s4b])
            k.op("dve", lambda e, s4=s4, o_=o_, m=m: e.scalar_tensor_tensor(out=o_, in0=o_, scalar=s4[:, 2:3], in1=gn[:, m, :], op0=ALU.mult, op1=ALU.mult),
                 reads=[ob, s4b, b_gn], writes=[ob])
            k.op("pool", lambda e, o_=o_, xi=xi: e.tensor_tensor(out=o_, in0=o_, in1=xi, op=ALU.add), reads=[ob, xib], writes=[ob])
            if last and t >= NCC:
                k.dma("sp", y_out[(t - NCC) * 128:(t - NCC + 1) * 128, :], o_, reads=[ob], sem=osem)
            else:
                k.dma("sp", X[t * 128:(t + 1) * 128, :], o_, reads=[ob], sem=osem)
        k.barrier()


    for l in range(L):
        last = l == L - 1
        phase_adaln(l)
        if stop == "adaln":
            break
        cur[0] = persist_end
        hT, hTb = norm_to_hT(0)
        if stop == "norm":
            break
        phase_inproj(l, hT, hTb)
        if stop == "inproj":
            break
        TM, b_TM = phase_gates(l)
        if stop == "gates":
            break
        phase_scan(l, TM, b_TM)
        if stop == "scan":
            break
        phase_outproj(l, last)
        if stop == "outproj":
            break
        cur[0] = persist_end
        hT, hTb = norm_to_hT(1)
        phase_ffn_up(l, hT, hTb)
        if stop == "ffnup":
            break
        phase_ffn_down(l, last)
    k.barrier()
    k.emit()
    return nc


def make_in_maps(inputs, cfg):
    n = cfg["NCORES"]
    hc = host_consts(cfg["TL"])
    maps = []
    shared = {kk: np.ascontiguousarray(inputs[kk]) for kk in (
        "w_ada", "b_ada", "norm_w", "w_in", "mlstm_conv_w", "mlstm_conv_b", "mlstm_gate_b", "ret_decay_exp",
        "head_norm_w", "w_ret_out", "w_mlstm_out", "w_o", "w_up", "ffn_conv_w", "ffn_conv_b", "w_down")}
    for b in range(n):
        m = dict(shared)
        m["x"] = np.ascontiguousarray(inputs["x"][b])
        m["ctx"] = np.ascontiguousarray(inputs["ctx"][b])
        m["cvec"] = np.ascontiguousarray(np.stack([inputs["c"][b], inputs["c_ctx"]], axis=0))
        m.update(hc)
        maps.append(m)
    return maps


def kernel(**inputs):
    cfg = dict(CFG)
    nc = build(cfg)
    maps = make_in_maps(inputs, cfg)
    res = run_bass_kernel_spmd(nc, maps, core_ids=list(range(cfg["NCORES"])))
    return np.stack([res.results[b]["y"] for b in range(cfg["NCORES"])], axis=0).astype(np.float32)
```

```python
import numpy as np
import ml_dtypes
import concourse.bass as bass
import concourse.mybir as mybir
from concourse.bass_utils import run_bass_kernel_spmd

F32 = mybir.dt.float32
BF16 = mybir.dt.bfloat16
ALU = mybir.AluOpType
AF = mybir.ActivationFunctionType

D = 2048
KC = 16
HD = 8
EPS = 1e-6
NEG = -30000.0
CFG = dict(TC=256, TL=4096, L=4, FF=5632, NCORES=8, dbg=False, stop=None)


class Buf:
    __slots__ = ("w", "r")

    def __init__(self):
        self.w = None
        self.r = {}


class Eng:
    def __init__(self, name, sem):
        self.name = name
        self.sem = sem
        self.cnt = 0
        self.q = []
        self.known = {}


class K:
    def __init__(self, nc):
        self.nc = nc
        self.E = {n: Eng(n, nc.alloc_semaphore(name="p_" + n)) for n in ("pe", "act", "dve", "pool", "sp")}
        self.dsem = {}
        self.freel = {}
        self.allsem = []

    def dma_sem(self, name, kind="sp"):
        if name not in self.dsem:
            fl = self.freel.setdefault(kind, [])
            if fl:
                self.dsem[name] = fl.pop()
            else:
                self.dsem[name] = [self.nc.alloc_semaphore(name="d_%d" % len(self.allsem)), 0, kind]
                self.allsem.append(self.dsem[name])
        return name

    def _waits(self, eng, reads, writes):
        deps = {}

        def add(t):
            if t is None:
                return
            s, v = t
            if deps.get(s, 0) < v:
                deps[s] = v

        for b in reads:
            add(b.w)
        for b in writes:
            add(b.w)
            for s, v in b.r.items():
                add((s, v))
        for s, v in deps.items():
            if s is eng.sem:
                if eng.name == "pe" or v > eng.cnt:
                    continue
            if eng.known.get(s, 0) >= v:
                continue
            eng.known[s] = v
            eng.q.append(("w", s, v))

    def _mark(self, tag, reads, writes):
        s, v = tag
        for b in reads:
            if b.r.get(s, 0) < v:
                b.r[s] = v
        for b in writes:
            b.w = tag
            b.r = {}

    def op(self, en, fn, reads=(), writes=(), inc=True):
        eng = self.E[en]
        self._waits(eng, reads, writes)
        if inc:
            eng.cnt += 1
            eng.q.append(("i", fn, eng.sem))
            tag = (eng.sem, eng.cnt)
        else:
            eng.q.append(("n", fn))
            tag = (eng.sem, eng.cnt + 1)
        self._mark(tag, reads, writes)

    def dma(self, en, out, in_, reads=(), writes=(), sem=None, **kw):
        eng = self.E[en]
        self.dma_sem(sem, en)
        self._waits(eng, reads, writes)
        d = self.dsem[sem]
        d[1] += 16
        eng.q.append(("d", out, in_, d[0], kw))
        self._mark((d[0], d[1]), reads, writes)

    def barrier(self):
        for en, eng in self.E.items():
            for en2, e2 in self.E.items():
                if e2 is eng or e2.cnt == 0:
                    continue
                if eng.known.get(e2.sem, 0) < e2.cnt:
                    eng.known[e2.sem] = e2.cnt
                    eng.q.append(("w", e2.sem, e2.cnt))
            for d in self.allsem:
                if d[1] > 0 and eng.known.get(d[0], 0) < d[1]:
                    eng.known[d[0]] = d[1]
                    eng.q.append(("w", d[0], d[1]))
        self.freel = {}
        for d in self.allsem:
            self.freel.setdefault(d[2], []).append(d)
        self.dsem = {}

    def emit(self):
        nc = self.nc
        with nc.Block() as block:
            def run(eng):
                def f(e):
                    for it in eng.q:
                        kk = it[0]
                        if kk == "w":
                            e.wait_ge(it[1], it[2])
                        elif kk == "i":
                            it[1](e).then_inc(it[2], 1)
                        elif kk == "n":
                            it[1](e)
                        else:
                            e.dma_start(out=it[1], in_=it[2], **it[4]).then_inc(it[3], 16)
                return f
            block.tensor(run(self.E["pe"]))
            block.scalar(run(self.E["act"]))
            block.vector(run(self.E["dve"]))
            block.gpsimd(run(self.E["pool"]))
            block.sync(run(self.E["sp"]))


class Ring:
    def __init__(self, name, aps):
        self.name = name
        self.aps = aps
        self.bufs = [Buf() for _ in aps]
        self.i = -1

    def next(self):
        self.i = (self.i + 1) % len(self.aps)
        return self.aps[self.i], self.bufs[self.i], "%s%d" % (self.name, self.i)


def host_consts(TL):
    ident = np.eye(128, dtype=np.float32)
    perm = np.zeros((128, 128), np.float32)
    for m in range(128):
        partner = m + 32 if (m % 64) < 32 else m - 32
        perm[partner, m] = 1.0
    j = np.arange(128, dtype=np.float32)[:, None]
    i = np.arange(128, dtype=np.float32)[None, :]
    one = np.ones((128, 128), np.float32)
    blocks = [
        ident,
        np.maximum(i - j, 0.0),
        np.maximum(j - i, 0.0),
        (i >= j).astype(np.float32),
        (j >= i).astype(np.float32),
        (i + 1.0) * one,
        (128.0 - i) * one,
        np.where(j <= i, 0.0, NEG).astype(np.float32),
        np.where(j >= i, 0.0, NEG).astype(np.float32),
        (127.0 - j) * one,
        j * one,
    ]
    cF = np.concatenate(blocks, axis=1).astype(np.float32)
    t = np.arange(TL)
    rows = (t // 64).astype(np.float32)
    cols = (t % 64).astype(np.float32)
    half = 32
    freqs = (10000.0 ** (-np.arange(half, dtype=np.float32) / half)).astype(np.float32)
    rope = np.zeros((128, 2, TL), np.float32)
    for p in range(128):
        pos = rows if p < 64 else cols
        ang = (pos * freqs[p % 32]).astype(np.float32)
        rope[p, 0] = np.cos(ang)
        rope[p, 1] = np.sin(ang) * (-1.0 if (p % 64) < 32 else 1.0)
    return dict(cI=ident.astype(ml_dtypes.bfloat16), cP=perm.astype(ml_dtypes.bfloat16), cF=cF, cROPE=rope)


def build(cfg):
    TC, TL, L, FF, dbg, stop = cfg["TC"], cfg["TL"], cfg["L"], cfg["FF"], cfg["dbg"], cfg["stop"]
    TT = TC + TL
    NCH = TT // 128
    NCC = TC // 128
    FC = FF // 128
    INC = 16416
    nc = bass.Bass("TRN2", target_bir_lowering=False)
    k = K(nc)

    def din(name, shape, dt=F32):
        return nc.dram_tensor(name, list(shape), dt, kind="ExternalInput").ap()

    def dscr(name, shape, dt):
        return nc.dram_tensor(name, list(shape), dt, kind=("ExternalOutput" if dbg else "Internal")).ap()

    x_in = din("x", [TL, D])
    ctx_in = din("ctx", [TC, D])
    cvec = din("cvec", [2, D])
    w_ada = din("w_ada", [L, D, 6 * D])
    b_ada = din("b_ada", [L, 6 * D])
    norm_w = din("norm_w", [L, 4, D])
    w_in = din("w_in", [L, D, INC])
    mconv_w = din("mlstm_conv_w", [L, 3, 2048])
    mconv_b = din("mlstm_conv_b", [L, 2048])
    gate_b = din("mlstm_gate_b", [L, 4, 8])
    decay_e = din("ret_decay_exp", [L, 2, 8])
    hnorm_w = din("head_norm_w", [L, 2, 2048])
    w_ro = din("w_ret_out", [L, D, D])
    w_mo = din("w_mlstm_out", [L, D, D])
    w_o = din("w_o", [L, D, D])
    w_up = din("w_up", [L, D, 2 * FF])
    fconv_w = din("ffn_conv_w", [L, 3, 2 * FF])
    fconv_b = din("ffn_conv_b", [L, 2 * FF])
    w_dn = din("w_down", [L, FF, D])
    cI_d = din("cI", [128, 128], BF16)
    cP_d = din("cP", [128, 128], BF16)
    cF_d = din("cF", [128, 11 * 128])
    cR_d = din("cROPE", [128, 2, TL])
    y_out = nc.dram_tensor("y", [TL, D], F32, kind="ExternalOutput").ap()

    X = dscr("X", [TT, D], F32)
    QR = dscr("QR", [1024, TT], BF16)
    KR = dscr("KR", [1024, TT], BF16)
    QM = dscr("QM", [1024, TT], BF16)
    KM = dscr("KM", [1024, TT], BF16)
    VR = dscr("VR", [TT, 2048], BF16)
    VM = dscr("VM", [TT, 2048], BF16)
    RG = dscr("RG", [2048, TT], BF16)
    MO = dscr("MO", [2048, TT], BF16)
    GRM = dscr("GRM", [4096, TT], BF16)
    GATES = dscr("GATES", [4, 8, TT], F32)
    NEGMX = dscr("NEGMX", [16, TT], F32)
    CHT = dscr("CHT", [2, 16, NCH], F32)
    SBR = dscr("SBR", [NCH, 128, 8, 256], BF16)
    SBM = dscr("SBM", [NCH, 128, 8, 257], BF16)
    YRT = dscr("YRT", [2048, TT], BF16)
    YMT = dscr("YMT", [2048, TT], BF16)
    ZR = dscr("ZR", [2048, TT], F32)
    YT = dscr("YT", [2048, TT], BF16)
    ACTT = dscr("ACTT", [FF, TT], BF16)
    FO = dscr("FO", [TT, D], F32)
    GNROW = dscr("GNROW", [4, D], F32)

    cnt = [0]
    cur = [0]

    ARENA_BYTES = 206 * 1024
    arena = nc.alloc_sbuf_tensor("arena", [128, ARENA_BYTES], mybir.dt.uint8).ap()

    def sb(shape, dt):
        nel = int(np.prod(shape[1:]))
        nbytes = nel * (4 if dt == F32 else 2)
        off = cur[0]
        cur[0] += (nbytes + 63) // 64 * 64
        assert cur[0] <= ARENA_BYTES, ("sbuf overflow", cur[0])
        flat = arena[0:shape[0], off:off + nbytes].bitcast(dt)
        if len(shape) == 2:
            return flat
        names = ["a%d" % i for i in range(len(shape) - 1)]
        pat = "p (%s) -> p %s" % (" ".join(names), " ".join(names))
        return flat.rearrange(pat, **{n: int(v) for n, v in zip(names[1:], shape[2:])})

    psb = [nc.alloc_psum_tensor("ps%d" % i, [128, 512], F32).ap() for i in range(8)]
    psB = [Buf() for _ in range(8)]

    cI = sb([128, 128], BF16)
    cP = sb([128, 128], BF16)
    cF = sb([128, 11 * 128], F32)
    bI, bP, bF = Buf(), Buf(), Buf()
    k.dma("sp", cI, cI_d, writes=[bI], sem="c0")
    k.dma("sp", cP, cP_d, writes=[bP], sem="c1")
    k.dma("sp", cF, cF_d, writes=[bF], sem="c2")

    def CF(i):
        return cF[:, i * 128:(i + 1) * 128]

    ones_bf = sb([128, 128], BF16)
    b_ones = Buf()
    k.op("pool", lambda e: e.memset(ones_bf, 1.0), writes=[b_ones])
    sT = sb([128, KC, 2], BF16)
    b_sT = Buf()
    MODS = sb([128, 8, KC], F32)
    b_MODS = Buf()
    persist_end = cur[0]

    k.dma("sp", X[0:TC, :], ctx_in, sem="c3")
    k.dma("sp", X[TC:TT, :], x_in, sem="c4")

    cur[0] = persist_end
    cv32 = sb([128, KC, 2], F32)
    b_cv = Buf()
    for m in range(2):
        k.dma("sp", cv32[:, :, m], cvec[m, :].rearrange("(kc p) -> p kc", p=128), writes=[b_cv], sem="c5",
              allow_slow_non_contiguous=True)
    k.op("act", lambda e: e.activation(out=sT, in_=cv32, func=AF.Silu), reads=[b_cv], writes=[b_sT])
    k.barrier()

    segs = [(0, TC)] + [(TC + i * 512, 512) for i in range(TL // 512)]

    def wslab_src(w2d, c0, ncols):
        return w2d[:, c0:c0 + ncols].rearrange("(kc p) n -> p kc n", p=128)

    def phase_adaln(l):
        cur[0] = persist_end
        sRep = sb([128, 2, KC, 128], BF16)
        b_sRep = Buf()
        for m in range(2):
            k.op("dve", lambda e, m=m: e.tensor_copy(out=sRep[:, m], in_=sT[:, :, m:m + 1].to_broadcast([128, KC, 128])),
                 reads=[b_sT], writes=[b_sRep])
        wr = Ring("wa", [sb([128, KC, 1536], BF16) for _ in range(2)])
        brow = sb([1, 6 * D], BF16)
        b_brow = Buf()
        k.dma("pool", brow, b_ada[l:l + 1, :], writes=[b_brow], sem="ab")
        nwf = sb([128, 2, KC], F32)
        b_nwf = Buf()
        for i, r in enumerate((0, 2)):
            k.dma("sp", nwf[:, i, :], norm_w[l, r, :].rearrange("(kc p) -> p kc", p=128), writes=[b_nwf], sem="an",
                  allow_slow_non_contiguous=True)
        nwb = sb([128, 2, D], F32)
        b_nwb = Buf()
        for i, r in enumerate((1, 3)):
            k.dma("sp", nwb[:, i, :], norm_w[l, r:r + 1, :].partition_broadcast(128), writes=[b_nwb], sem="anb")
        modfm = sb([128, 96, 2], F32)
        b_modfm = Buf()
        gtile = sb([128, 4, D], F32)
        b_gt = Buf()
        pm = psb[0][:, 0:192].rearrange("p (j m) -> p j m", m=2)
        for s in range(8):
            ws, wb, wsem = wr.next()
            k.dma("pool", ws, wslab_src(w_ada[l], s * 1536, 1536), writes=[wb], sem=wsem)
            for jj in range(12):
                j = s * 12 + jj
                for kc in range(KC):
                    k.op("pe", lambda e, j=j, jj=jj, kc=kc, ws=ws: e.matmul(pm[:, j, :], ws[:, kc, jj * 128:(jj + 1) * 128],
                         sT[:, kc, :], start=(kc == 0), stop=False), reads=[wb, b_sT], writes=[psB[0]], inc=False)
                k.op("pe", lambda e, j=j: e.matmul(pm[:, j, :], brow[0:1, j * 128:(j + 1) * 128], ones_bf[0:1, 0:2],
                     start=False, stop=True), reads=[b_brow, b_ones], writes=[psB[0]])
            for g in range(3):
                c0 = s * 1536 + g * 512
                which = None
                if 2 * D <= c0 < 3 * D:
                    which = 0
                elif 5 * D <= c0 < 6 * D:
                    which = 1
                if which is None:
                    continue
                off = c0 - (2 * D if which == 0 else 5 * D)
                for m in range(2):
                    pb = 1 + m
                    for kc in range(KC):
                        k.op("pe", lambda e, pb=pb, m=m, kc=kc, ws=ws, g=g: e.matmul(psb[pb], sRep[:, m, kc, :],
                             ws[:, kc, g * 512:(g + 1) * 512], start=(kc == 0), stop=False),
                             reads=[wb, b_sRep], writes=[psB[pb]], inc=False)
                    k.op("pe", lambda e, pb=pb, c0=c0: e.matmul(psb[pb], ones_bf[0:1, :], brow[0:1, c0:c0 + 512],
                         start=False, stop=True), reads=[b_brow, b_ones], writes=[psB[pb]])
                    k.op("dve", lambda e, pb=pb, which=which, m=m, off=off: e.tensor_tensor(
                        out=gtile[:, which * 2 + m, off:off + 512], in0=psb[pb], in1=nwb[:, which, off:off + 512],
                        op=ALU.mult), reads=[psB[pb], b_nwb], writes=[b_gt])
        k.op("dve", lambda e: e.tensor_copy(out=modfm, in_=pm), reads=[psB[0]], writes=[b_modfm])
        for which in range(2):
            shj, scj = (0, 16) if which == 0 else (48, 64)
            for m in range(2):
                o = which * 4 + m * 2
                k.op("dve", lambda e, o=o, scj=scj, m=m, which=which: e.scalar_tensor_tensor(
                    out=MODS[:, o, :], in0=modfm[:, scj:scj + 16, m], scalar=1.0, in1=nwf[:, which, :],
                    op0=ALU.add, op1=ALU.mult), reads=[b_modfm, b_nwf], writes=[b_MODS])
                k.op("dve", lambda e, o=o, shj=shj, m=m: e.tensor_copy(out=MODS[:, o + 1, :], in_=modfm[:, shj:shj + 16, m]),
                     reads=[b_modfm], writes=[b_MODS])
        for i in range(4):
            k.dma("sp", GNROW[i:i + 1, :], gtile[0:1, i, :], reads=[b_gt], sem="ag")
        k.barrier()

    def norm_to_hT(which):
        hT = sb([128, KC, TT], BF16)
        hTb = [Buf() for _ in range(NCH)]
        save = cur[0]
        xr = Ring("xi", [sb([128, D], F32) for _ in range(2)])
        junk = sb([128, D], BF16)
        b_junk = Buf()
        xsr = Ring("xs", [sb([128, D], BF16) for _ in range(2)])
        ss = sb([128, NCH], F32)
        rs = sb([128, NCH], F32)
        rstd = sb([128, NCH], F32)
        b_ss = [Buf() for _ in range(NCH)]
        pT = [psb[i].bitcast(BF16) for i in range(4)]
        for t in range(NCH):
            o = which * 4 + (0 if t >= NCC else 2)
            xi, xb, xsem = xr.next()
            k.dma("sp", xi, X[t * 128:(t + 1) * 128, :], writes=[xb], sem=xsem)
            k.op("act", lambda e, xi=xi, t=t: e.activation(out=junk, in_=xi, func=AF.Square, accum_out=ss[:, t:t + 1]),
                 reads=[xb], writes=[b_junk, b_ss[t]])
            k.op("act", lambda e, t=t: e.activation(out=rs[:, t:t + 1], in_=ss[:, t:t + 1], func=AF.Sqrt, scale=1.0 / D,
                 bias=CF(3)[:, 0:1] if False else EPS), reads=[b_ss[t]], writes=[b_ss[t]])
            k.op("dve", lambda e, t=t: e.reciprocal(out=rstd[:, t:t + 1], in_=rs[:, t:t + 1]), reads=[b_ss[t]], writes=[b_ss[t]])
            xs, xsb, _ = xsr.next()
            k.op("act", lambda e, xs=xs, xi=xi, t=t: e.activation(out=xs, in_=xi, func=AF.Copy, scale=rstd[:, t:t + 1]),
                 reads=[xb, b_ss[t]], writes=[xsb])
            for half in range(2):
                pb = (t % 2) * 2 + half
                for q in range(8):
                    kc = half * 8 + q
                    k.op("pe", lambda e, pb=pb, q=q, kc=kc, xs=xs: e.transpose(pT[pb][:, q * 128:(q + 1) * 128],
                         xs[:, kc * 128:(kc + 1) * 128], cI), reads=[xsb, bI], writes=[psB[pb]], inc=(q == 7))
                for q in range(8):
                    kc = half * 8 + q
                    en = "dve" if q % 2 == 0 else "act"
                    if en == "dve":
                        k.op("dve", lambda e, pb=pb, q=q, kc=kc, t=t, o=o: e.tensor_scalar(
                            out=hT[:, kc, t * 128:(t + 1) * 128], in0=pT[pb][:, q * 128:(q + 1) * 128],
                            scalar1=MODS[:, o, kc:kc + 1], scalar2=MODS[:, o + 1, kc:kc + 1], op0=ALU.mult, op1=ALU.add),
                            reads=[psB[pb], b_MODS], writes=[hTb[t]])
                    else:
                        k.op("act", lambda e, pb=pb, q=q, kc=kc, t=t, o=o: e.activation(
                            out=hT[:, kc, t * 128:(t + 1) * 128], in_=pT[pb][:, q * 128:(q + 1) * 128], func=AF.Identity,
                            scale=MODS[:, o, kc:kc + 1], bias=MODS[:, o + 1, kc:kc + 1]),
                            reads=[psB[pb], b_MODS], writes=[hTb[t]])
        k.barrier()
        cur[0] = save
        return hT, hTb

    def load_resident(src, rows):
        n = rows // 128
        t = sb([128, n, TT], BF16)
        b = Buf()
        for kc in range(n):
            k.dma("sp", t[:, kc, :], src[kc * 128:(kc + 1) * 128, :], writes=[b], sem="lr%d" % (kc % 4))
        return t, [b] * NCH

    def proj_fm(hT, hTb, w2d, col_list, nk, evac_chunk, M=128, wring=None):
        pbi = [0]
        for ci, c0 in enumerate(col_list):
            ws, wb, wsem = wring.next()
            k.dma("pool", ws[:, :, 0:M], wslab_src(w2d, c0, M), writes=[wb], sem=wsem)
            outs = []
            for si, (g0, n) in enumerate(segs):
                pb = pbi[0] % 4
                pbi[0] += 1
                tl = [hTb[t] for t in range(g0 // 128, (g0 + n) // 128)]
                for kc in range(nk):
                    k.op("pe", lambda e, pb=pb, kc=kc, ws=ws, g0=g0, n=n: e.matmul(psb[pb][0:M, 0:n], ws[:, kc, 0:M],
                         hT[:, kc, g0:g0 + n], start=(kc == 0), stop=(kc == nk - 1)),
                         reads=[wb] + tl, writes=[psB[pb]], inc=(kc == nk - 1))
                evac_chunk(ci, c0, pb, si, g0, n)

    def phase_inproj(l, hT, hTb):
        save = cur[0]
        W = w_in[l]
        rings = {}

        def reset():
            k.barrier()
            cur[0] = save
            rings["w"] = Ring("wi", [sb([128, KC, 128], BF16) for _ in range(2)])
            rings["s"] = Ring("sg", [sb([128, 512], BF16) for _ in range(4)])
            return rings["w"], rings["s"]
        wring, stg = reset()
        def mk_simple(dst, func, base):
            def ev(ci, c0, pb, si, g0, n):
                st, sbf, ssem = stg.next()
                k.op("act", lambda e: e.activation(out=st[:, 0:n], in_=psb[pb][:, 0:n], func=func), reads=[psB[pb]], writes=[sbf])
                r0 = c0 - base
                k.dma("sp", dst[r0:r0 + 128, g0:g0 + n], st[:, 0:n], reads=[sbf], sem=ssem)
            return ev
        proj_fm(hT, hTb, W, [4096 + i * 128 for i in range(16)], KC, mk_simple(RG, AF.Silu, 4096), wring=wring)
        proj_fm(hT, hTb, W, [10240 + i * 128 for i in range(16)], KC, mk_simple(MO, AF.Sigmoid, 10240), wring=wring)
        proj_fm(hT, hTb, W, [12320 + i * 128 for i in range(32)], KC, mk_simple(GRM, AF.Sigmoid, 12320), wring=wring)
        wring, stg = reset()
        ropeT = Ring("rp", [sb([128, 2, 512], F32) for _ in range(4)])
        xbf = Ring("xb", [sb([128, 512], BF16) for _ in range(4)])
        t12 = Ring("t1", [sb([128, 2, 512], F32) for _ in range(4)])
        def mk_rope(dst, base, scale):
            def ev(ci, c0, pb, si, g0, n):
                r0 = c0 - base
                st, sbf, ssem = stg.next()
                if g0 < TC:
                    k.op("act", lambda e: e.activation(out=st[:, 0:n], in_=psb[pb][:, 0:n], func=AF.Copy, scale=scale),
                         reads=[psB[pb]], writes=[sbf])
                else:
                    rp, rpb, rsem = ropeT.next()
                    k.dma("sp", rp[:, :, 0:n], cR_d[:, :, g0 - TC:g0 - TC + n], writes=[rpb], sem=rsem)
                    xb_, xbb, _ = xbf.next()
                    k.op("act", lambda e: e.activation(out=xb_[:, 0:n], in_=psb[pb][:, 0:n], func=AF.Copy, scale=scale),
                         reads=[psB[pb]], writes=[xbb])
                    pp = 4 + pb
                    k.op("pe", lambda e: e.matmul(psb[pp][:, 0:n], cP, xb_[:, 0:n], start=True, stop=True),
                         reads=[xbb, bP], writes=[psB[pp]])
                    tt, ttb, _ = t12.next()
                    k.op("dve", lambda e: e.tensor_tensor(out=tt[:, 0, 0:n], in0=xb_[:, 0:n], in1=rp[:, 0, 0:n], op=ALU.mult),
                         reads=[xbb, rpb], writes=[ttb])
                    k.op("dve", lambda e: e.tensor_tensor(out=tt[:, 1, 0:n], in0=psb[pp][:, 0:n], in1=rp[:, 1, 0:n], op=ALU.mult),
                         reads=[psB[pp], rpb], writes=[ttb])
                    k.op("dve", lambda e: e.tensor_tensor(out=st[:, 0:n], in0=tt[:, 0, 0:n], in1=tt[:, 1, 0:n], op=ALU.add),
                         reads=[ttb], writes=[sbf])
                k.dma("sp", dst[r0:r0 + 128, g0:g0 + n], st[:, 0:n], reads=[sbf], sem=ssem)
            return ev
        proj_fm(hT, hTb, W, [i * 128 for i in range(8)], KC, mk_rope(QR, 0, 128.0 ** -0.5), wring=wring)
        proj_fm(hT, hTb, W, [1024 + i * 128 for i in range(8)], KC, mk_rope(KR, 1024, 1.0), wring=wring)
        wring, stg = reset()
        cw = sb([128, 4, KC], F32)
        b_cw = Buf()
        for kk in range(3):
            k.dma("sp", cw[:, kk, :], mconv_w[l, kk, :].rearrange("(c p) -> p c", p=128), writes=[b_cw], sem="cw",
                  allow_slow_non_contiguous=True)
        k.dma("sp", cw[:, 3, :], mconv_b[l, :].rearrange("(c p) -> p c", p=128), writes=[b_cw], sem="cw",
              allow_slow_non_contiguous=True)
        conv_chunks(hT, hTb, W, [6144 + i * 128 for i in range(16)], cw, wring,
                    lambda ci: ((QM, ci * 128, 128.0 ** -0.5) if ci < 8 else (KM, (ci - 8) * 128, 1.0)), mode="silu", cwb=b_cw)
        wring, stg = reset()
        gb = sb([8, 4], F32)
        b_gb = Buf()
        for g in range(4):
            k.dma("sp", gb[:, g:g + 1], gate_b[l, g, :].rearrange("(h o) -> h o", o=1), writes=[b_gb], sem="gb")
        gst = Ring("gs", [sb([8, 512], F32) for _ in range(2)])
        def ev_g(ci, c0, pb, si, g0, n):
            st, sbf, ssem = gst.next()
            k.op("act", lambda e: e.activation(out=st[:, 0:n], in_=psb[pb][0:8, 0:n], func=AF.Identity, bias=gb[:, ci:ci + 1]),
                 reads=[psB[pb], b_gb], writes=[sbf])
            k.dma("sp", GATES[ci, :, g0:g0 + n], st[:, 0:n], reads=[sbf], sem=ssem)
        proj_fm(hT, hTb, W, [12288 + g * 8 for g in range(4)], KC, ev_g, M=8, wring=wring)
        wring, stg = reset()
        wv = Ring("wv", [sb([128, KC, 512], BF16) for _ in range(2)])
        vst = Ring("vs", [sb([128, 512], BF16) for _ in range(3)])
        pbi = 0
        for dst, base in ((VR, 2048), (VM, 8192)):
            for g in range(4):
                ws, wb, wsem = wv.next()
                k.dma("pool", ws, wslab_src(W, base + g * 512, 512), writes=[wb], sem=wsem)
                for t in range(NCH):
                    pb = pbi % 4
                    pbi += 1
                    for kc in range(KC):
                        k.op("pe", lambda e, pb=pb, kc=kc, ws=ws, t=t: e.matmul(psb[pb], hT[:, kc, t * 128:(t + 1) * 128],
                             ws[:, kc, :], start=(kc == 0), stop=(kc == KC - 1)), reads=[wb, hTb[t]], writes=[psB[pb]],
                             inc=(kc == KC - 1))
                    st, sbf, ssem = vst.next()
                    k.op("act", lambda e, pb=pb, st=st: e.activation(out=st, in_=psb[pb], func=AF.Copy), reads=[psB[pb]], writes=[sbf])
                    k.dma("sp", dst[t * 128:(t + 1) * 128, g * 512:(g + 1) * 512], st, reads=[sbf], sem=ssem)
        k.barrier()
        cur[0] = save

    def conv_chunks(hT, hTb, W, cols, cw, wring, dst_fn, mode, cwb=None):
        raw = Ring("rw", [sb([128, TT + 3], F32) for _ in range(1)])
        for r_ap in raw.aps:
            k.op("pool", lambda e, r_ap=r_ap: e.memset(r_ap, 0.0), writes=[raw.bufs[raw.aps.index(r_ap)]])
        acc = Ring("ac", [sb([128, TT], F32) for _ in range(1)])
        ost = Ring("oc", [sb([128, TT], BF16) for _ in range(1)])
        state = {}

        def colof(g):
            return g + 1 if g < TC else g + 2

        def ev(ci, c0, pb, si, g0, n):
            if si == 0:
                state["raw"] = raw.next()
            rw, rwb, _ = state["raw"]
            k.op("act", lambda e: e.activation(out=rw[:, colof(g0):colof(g0) + n], in_=psb[pb][:, 0:n], func=AF.Copy),
                 reads=[psB[pb]], writes=[rwb])
            if si != len(segs) - 1:
                return
            a, ab, _ = acc.next()
            def views(off):
                return [(rw[:, off:off + TC], a[:, 0:TC]), (rw[:, off + TC + 1:off + TC + 1 + TL], a[:, TC:TT])]
            for (src, dsta) in views(1):
                k.op("dve", lambda e, src=src, dsta=dsta: e.tensor_scalar(out=dsta, in0=src, scalar1=cw[:, 1, ci:ci + 1],
                     scalar2=cw[:, 3, ci:ci + 1], op0=ALU.mult, op1=ALU.add), reads=[rwb, state["cwb"]], writes=[ab])
            for kk, off in ((0, 0), (2, 2)):
                for (src, dsta) in views(off):
                    k.op("dve", lambda e, src=src, dsta=dsta, kk=kk: e.scalar_tensor_tensor(
                        out=dsta, in0=src, scalar=cw[:, kk, ci:ci + 1], in1=dsta, op0=ALU.mult, op1=ALU.add),
                        reads=[rwb, state["cwb"], ab], writes=[ab])
            if mode == "silu":
                dst, r0, scale = dst_fn(ci)
                o, ob, osem = ost.next()
                if scale == 1.0:
                    k.op("act", lambda e: e.activation(out=o, in_=a, func=AF.Silu), reads=[ab], writes=[ob])
                else:
                    k.op("act", lambda e: e.activation(out=a, in_=a, func=AF.Silu), reads=[ab], writes=[ab])
                    k.op("act", lambda e: e.activation(out=o, in_=a, func=AF.Copy, scale=scale), reads=[ab], writes=[ob])
                k.dma("sp", dst[r0:r0 + 128, :], o, reads=[ob], sem=osem)
            else:
                if ci % 2 == 0:
                    o, ob, osem = ost.next()
                    k.op("act", lambda e: e.activation(out=o, in_=a, func=AF.Silu), reads=[ab], writes=[ob])
                    state["a"] = (o, ob, osem)
                else:
                    o, ob, osem = state["a"]
                    k.op("dve", lambda e: e.tensor_tensor(out=o, in0=o, in1=a, op=ALU.mult), reads=[ob, ab], writes=[ob])
                    r0 = (ci // 2) * 128
                    k.dma("sp", ACTT[r0:r0 + 128, :], o, reads=[ob], sem=osem)
        state["cwb"] = cwb
        proj_fm(hT, hTb, W, cols, KC, ev, wring=wring)

    build.conv_chunks = conv_chunks
    def phase_gates(l):
        cur[0] = persist_end
        TM = sb([128, 6, NCH * 8], F32)
        tm_end = cur[0]
        G = [sb([8, TT], F32) for _ in range(4)]
        bG = [Buf() for _ in range(4)]
        for g in range(4):
            k.dma("sp", G[g], GATES[g], writes=[bG[g]], sem="pg%d" % g)
        one8 = sb([8, 1], F32)
        b_o8 = Buf()
        k.op("pool", lambda e: e.memset(one8, 1.0), writes=[b_o8])
        b_TM = Buf()
        cht = sb([8, 2, 2, NCH], F32)
        b_cht = Buf()
        tmp = sb([8, TT], F32)
        b_tmp = Buf()
        for d in range(2):
            IG, FG, bIG, bFG = G[2 * d], G[2 * d + 1], bG[2 * d], bG[2 * d + 1]
            k.op("act", lambda e, FG=FG: e.activation(out=FG, in_=FG, func=AF.Exp, scale=-1.0), reads=[bFG], writes=[bFG])
            k.op("act", lambda e, FG=FG: e.activation(out=FG, in_=FG, func=AF.Ln, bias=1.0), reads=[bFG], writes=[bFG])
            if d == 0:
                sl = [(slice(0, TT), 0.0)]
            else:
                sl = [("rc", None), ("rl", None)]
            def scan(out, dat, op1, init0, d=d):
                if d == 0:
                    k.op("dve", lambda e: e.tensor_tensor_scan(out=out[:, 0:TT], data0=one8[:, 0:1].to_broadcast([8, TT]),
                         data1=dat[:, 0:TT], initial=init0, op0=ALU.mult, op1=op1), reads=[b_o8, bIG, bFG, b_tmp], writes=[b_tmp, bIG, bFG])
                else:
                    k.op("dve", lambda e: e.tensor_tensor_scan(out=out[:, TC - 1::-1] if True else None,
                         data0=one8[:, 0:1].to_broadcast([8, TC]), data1=dat[:, TC - 1::-1], initial=init0, op0=ALU.mult, op1=op1),
                         reads=[b_o8, bIG, bFG, b_tmp], writes=[b_tmp, bIG, bFG])
                    k.op("dve", lambda e: e.tensor_tensor_scan(out=out[:, TT - 1:TC - 1:-1],
                         data0=one8[:, 0:1].to_broadcast([8, TL]), data1=dat[:, TT - 1:TC - 1:-1], initial=out[:, 0:1],
                         op0=ALU.mult, op1=op1), reads=[b_o8, bIG, bFG, b_tmp], writes=[b_tmp, bIG, bFG])
            scan(tmp, FG, ALU.add, 0.0)
            k.op("dve", lambda e, IG=IG: e.tensor_tensor(out=IG, in0=IG, in1=tmp, op=ALU.add), reads=[bIG, b_tmp], writes=[bIG])
            k.op("pool", lambda e, FG=FG: e.tensor_copy(out=FG, in_=tmp), reads=[b_tmp], writes=[bFG])
            scan(tmp, IG, ALU.max, 0.0)
            k.op("dve", lambda e, FG=FG: e.tensor_tensor(out=FG, in0=FG, in1=tmp, op=ALU.subtract), reads=[bFG, b_tmp], writes=[bFG])
            k.op("act", lambda e, FG=FG: e.activation(out=FG, in_=FG, func=AF.Exp), reads=[bFG], writes=[bFG])
            Mx3 = tmp.rearrange("p (c j) -> p c j", j=128)
            endj = 127 if d == 0 else 0
            Mend = Mx3[:, :, endj]
            mp = cht[:, d, 0, :]
            k.op("pool", lambda e, mp=mp: e.memset(mp, 0.0), writes=[b_cht])
            if d == 0:
                k.op("dve", lambda e, mp=mp, Mend=Mend: e.tensor_copy(out=mp[:, 1:NCH], in_=Mend[:, 0:NCH - 1]), reads=[b_tmp], writes=[b_cht])
            else:
                if NCC > 1:
                    k.op("dve", lambda e, mp=mp, Mend=Mend: e.tensor_copy(out=mp[:, 0:NCC - 1], in_=Mend[:, 1:NCC]), reads=[b_tmp], writes=[b_cht])
                k.op("dve", lambda e, mp=mp, Mend=Mend: e.tensor_copy(out=mp[:, NCC:NCH - 1], in_=Mend[:, NCC + 1:NCH]), reads=[b_tmp], writes=[b_cht])
                k.op("dve", lambda e, mp=mp, Mend=Mend: e.tensor_copy(out=mp[:, NCH - 1:NCH], in_=Mend[:, 0:1]), reads=[b_tmp], writes=[b_cht])
            k.op("dve", lambda e, mp=mp, Mend=Mend, d=d: e.tensor_tensor(out=cht[:, d, 1, :], in0=mp, in1=Mend, op=ALU.subtract),
                 reads=[b_tmp, b_cht], writes=[b_cht])
            k.op("act", lambda e, d=d: e.activation(out=cht[:, d, 1, :], in_=cht[:, d, 1, :], func=AF.Exp), reads=[b_cht], writes=[b_cht])
            wt = sb([8, TT], F32) if d == 0 else state_wt[0]
            if d == 0:
                state_wt.append(wt)
            b_wt = Buf()
            k.op("dve", lambda e, IG=IG, Mend=Mend, wt=wt: e.tensor_tensor(out=wt.rearrange("p (c j) -> p c j", j=128),
                 in0=IG.rearrange("p (c j) -> p c j", j=128), in1=Mend.unsqueeze(2).to_broadcast([8, NCH, 128]), op=ALU.subtract),
                 reads=[bIG, b_tmp], writes=[b_wt])
            k.op("act", lambda e, wt=wt: e.activation(out=wt, in_=wt, func=AF.Exp), reads=[b_wt], writes=[b_wt])
            for qi, (src, sbuf_) in enumerate(((IG, bIG), (wt, b_wt), (FG, bFG))):
                pb = d * 3 + qi
                for c in range(NCH):
                    k.op("pe", lambda e, pb=pb, c=c, src=src: e.matmul(psb[pb][:, c * 8:(c + 1) * 8], src[:, c * 128:(c + 1) * 128],
                         CF(0)[0:8, 0:8], start=True, stop=True), reads=[sbuf_, bF], writes=[psB[pb]], inc=(c == NCH - 1))
                k.op("dve", lambda e, pb=pb: e.tensor_copy(out=TM[:, pb, :], in_=psb[pb][:, 0:NCH * 8]), reads=[psB[pb]], writes=[b_TM])
            k.op("pool", lambda e: e.tensor_scalar(out=tmp, in0=tmp, scalar1=-1.0, scalar2=None, op0=ALU.mult), reads=[b_tmp], writes=[b_tmp])
            k.dma("sp", NEGMX[d * 8:(d + 1) * 8, :], tmp, reads=[b_tmp], sem="pn%d" % d)
            for q in range(2):
                k.dma("sp", CHT[q, d * 8:(d + 1) * 8, :], cht[:, d, q, :], reads=[b_cht], sem="pc%d" % q)
        k.barrier()
        cur[0] = tm_end
        return TM, b_TM

    state_wt = []
    def layer_consts(l):
        de = sb([128, 16], F32)
        b_de = Buf()
        k.dma("sp", de, decay_e[l:l + 1].rearrange("o a b -> o (a b)").partition_broadcast(128), writes=[b_de], sem="ld")
        lg = sb([128, 16], F32)
        b_lg = Buf()
        k.op("act", lambda e: e.activation(out=lg, in_=de, func=AF.Exp, scale=-float(np.log(2.0))), reads=[b_de], writes=[b_lg])
        k.op("act", lambda e: e.activation(out=lg, in_=lg, func=AF.Ln, scale=-1.0, bias=1.0), reads=[b_lg], writes=[b_lg])
        mask = sb([128, 8, 128], BF16)
        qf = sb([128, 8, 128], BF16)
        qb = sb([128, 8, 128], BF16)
        kd = sb([128, 2, 8], F32)
        cd = sb([128, 2, 8], F32)
        t1 = sb([128, 128], F32)
        t2 = sb([128, 128], F32)
        b_c, b_t1, b_t2 = Buf(), Buf(), Buf()
        for h in range(8):
            k.op("act", lambda e, h=h: e.activation(out=t1, in_=CF(1), func=AF.Exp, scale=lg[:, h:h + 1]), reads=[bF, b_lg], writes=[b_t1])
            k.op("act", lambda e, h=h: e.activation(out=t2, in_=CF(2), func=AF.Exp, scale=lg[:, 8 + h:9 + h]), reads=[bF, b_lg], writes=[b_t2])
            k.op("dve", lambda e: e.tensor_tensor(out=t1, in0=t1, in1=CF(3), op=ALU.mult), reads=[b_t1, bF], writes=[b_t1])
            k.op("dve", lambda e: e.tensor_tensor(out=t2, in0=t2, in1=CF(4), op=ALU.mult), reads=[b_t2, bF], writes=[b_t2])
            k.op("dve", lambda e, h=h: e.tensor_tensor(out=mask[:, h, :], in0=t1, in1=t2, op=ALU.add), reads=[b_t1, b_t2], writes=[b_c])
            k.op("act", lambda e, h=h: e.activation(out=qf[:, h, :], in_=CF(5), func=AF.Exp, scale=lg[:, h:h + 1]), reads=[bF, b_lg], writes=[b_c])
            k.op("act", lambda e, h=h: e.activation(out=qb[:, h, :], in_=CF(6), func=AF.Exp, scale=lg[:, 8 + h:9 + h]), reads=[bF, b_lg], writes=[b_c])
            k.op("act", lambda e, h=h: e.activation(out=kd[:, 0, h:h + 1], in_=CF(9)[:, 0:1], func=AF.Exp, scale=lg[:, h:h + 1]), reads=[bF, b_lg], writes=[b_c])
            k.op("act", lambda e, h=h: e.activation(out=kd[:, 1, h:h + 1], in_=CF(10)[:, 0:1], func=AF.Exp, scale=lg[:, 8 + h:9 + h]), reads=[bF, b_lg], writes=[b_c])
        k.op("act", lambda e: e.activation(out=cd.rearrange("p a b -> p (a b)"), in_=lg, func=AF.Exp, scale=128.0), reads=[b_lg], writes=[b_c])
        return dict(mask=mask, qf=qf, qb=qb, kd=kd, cd=cd, b=b_c)

    def phase_scan(l, TM, b_TM):
        RC = layer_consts(l)
        bRC = RC["b"]
        chb = sb([128, 2, 16, NCH], F32)
        b_chb = Buf()
        k.dma("sp", chb, CHT.rearrange("q r c -> (q r c)").rearrange("(o n) -> o n", o=1).partition_broadcast(128)
              if False else CHT.rearrange("q r c -> (q r c)").partition_broadcast(128), writes=[b_chb], sem="chb")
        hn = sb([128, 2, KC], F32)
        b_hn = Buf()
        for i in range(2):
            k.dma("sp", hn[:, i, :], hnorm_w[l, i, :].rearrange("(c p) -> p c", p=128), writes=[b_hn], sem="hn",
                  allow_slow_non_contiguous=True)
        S32 = sb([128, 8, 256], F32)
        Sbf = sb([128, 8, 256], BF16)
        C32 = sb([128, 8, 257], F32)
        Cbf = sb([128, 8, 257], BF16)
        bS = [Buf() for _ in range(8)]
        bSb = [Buf() for _ in range(8)]
        bC = [Buf() for _ in range(8)]
        bCb = [Buf() for _ in range(8)]
        kin = Ring("ki", [sb([128, 4, 8, 128], BF16) for _ in range(2)])
        vin = Ring("vi", [sb([128, 2, 8, 257], BF16) for _ in range(2)])
        for v_ap, vb in zip(vin.aps, vin.bufs):
            k.op("pool", lambda e, v_ap=v_ap: e.memset(v_ap[:, 1, :, 256:257], 1.0), writes=[vb])
        kt = Ring("kt", [sb([128, 128], BF16) for _ in range(8)])
        pT = [psb[6].bitcast(BF16), psb[7].bitcast(BF16)]
        tcount = [0]

        def zero_states():
            k.op("pool", lambda e: e.memset(S32, 0.0), writes=bS)
            k.op("pool", lambda e: e.memset(Sbf, 0.0), writes=bSb)
            k.op("pool", lambda e: e.memset(C32, 0.0), writes=bC)
            k.op("pool", lambda e: e.memset(Cbf, 0.0), writes=bCb)

        def load_chunk(c, need_q):
            ki, kb, ksem = kin.next()
            for i, src in enumerate((QR, KR, QM, KM)):
                if not need_q and i in (0, 2):
                    continue
                k.dma("sp", ki[:, i], src[:, c * 128:(c + 1) * 128].rearrange("(h d) t -> d h t", d=128), writes=[kb], sem=ksem)
            vi, vb, vsem = vin.next()
            k.dma("sp", vi[:, 0, :, 0:256], VR[c * 128:(c + 1) * 128, :].rearrange("t (h v) -> t h v", v=256), writes=[vb], sem=vsem)
            k.dma("sp", vi[:, 1, :, 0:256], VM[c * 128:(c + 1) * 128, :].rearrange("t (h v) -> t h v", v=256), writes=[vb], sem=vsem)
            return ki, kb, vi, vb

        def state_update(c, h, ki, kb, vi, vb, d):
            for br in range(2):
                ti = tcount[0]
                tcount[0] += 1
                slot = ti % 8
                pt = pT[0][:, slot * 128:(slot + 1) * 128]
                k.op("pe", lambda e, pt=pt, br=br: e.transpose(pt, ki[:, 1 + 2 * br, h, :], cI), reads=[kb, bI], writes=[psB[6]])
                kk_, kkb, _ = kt.next()
                if br == 0:
                    sc = RC["kd"][:, d, h:h + 1]
                    rd = [bRC]
                else:
                    sc = TM[:, d * 3 + 1, c * 8 + h:c * 8 + h + 1]
                    rd = [b_TM]
                k.op("act", lambda e, kk_=kk_, pt=pt, sc=sc: e.activation(out=kk_, in_=pt, func=AF.Copy, scale=sc),
                     reads=[psB[6]] + rd, writes=[kkb])
                pb = 4 + (ti % 2)
                if br == 0:
                    k.op("pe", lambda e, pb=pb, kk_=kk_: e.matmul(psb[pb][:, 0:256], kk_, vi[:, 0, h, 0:256], start=True, stop=True),
                         reads=[kkb, vb], writes=[psB[pb]])
                    k.op("dve", lambda e, pb=pb: e.scalar_tensor_tensor(out=S32[:, h, :], in0=S32[:, h, :], scalar=RC["cd"][:, d, h:h + 1],
                         in1=psb[pb][:, 0:256], op0=ALU.mult, op1=ALU.add), reads=[psB[pb], bS[h], bRC], writes=[bS[h]])
                    k.op("act", lambda e: e.activation(out=Sbf[:, h, :], in_=S32[:, h, :], func=AF.Copy), reads=[bS[h]], writes=[bSb[h]])
                else:
                    k.op("pe", lambda e, pb=pb, kk_=kk_: e.matmul(psb[pb][:, 0:257], kk_, vi[:, 1, h, :], start=True, stop=True),
                         reads=[kkb, vb], writes=[psB[pb]])
                    k.op("dve", lambda e, pb=pb: e.scalar_tensor_tensor(out=C32[:, h, :], in0=C32[:, h, :],
                         scalar=chb[:, 1, d * 8 + h, c:c + 1], in1=psb[pb][:, 0:257], op0=ALU.mult, op1=ALU.add),
                         reads=[psB[pb], bC[h], b_chb], writes=[bC[h]])
                    k.op("act", lambda e: e.activation(out=Cbf[:, h, :], in_=C32[:, h, :], func=AF.Copy), reads=[bC[h]], writes=[bCb[h]])

        zero_states()
        BWD = list(range(NCC - 1, -1, -1)) + list(range(NCH - 1, NCC - 1, -1))
        for c in BWD:
            ki, kb, vi, vb = load_chunk(c, False)
            k.dma("sp", SBR[c], Sbf, reads=bSb, sem="s1r")
            k.dma("sp", SBM[c], Cbf, reads=bCb, sem="s1m")
            for h in range(8):
                state_update(c, h, ki, kb, vi, vb, 1)
        k.barrier()
        zero_states()
        sbin = Ring("sn", [sb([128, 2, 8, 257], BF16) for _ in range(2)])
        uin = Ring("ui", [sb([128, 16, 128], F32) for _ in range(2)])
        gin = Ring("gi", [sb([128, 2, KC, 128], BF16) for _ in range(2)])
        wk = Ring("wk", [sb([128, 128], BF16) for _ in range(28)])
        wf = Ring("wf", [sb([128, 128], F32) for _ in range(12)])
        ytm = Ring("yt", [sb([128, 2, 8, 256], F32) for _ in range(2)])
        ynb = Ring("yn", [sb([128, 2, 2048], BF16) for _ in range(1)])
        oT = Ring("ot", [sb([128, 2, KC, 128], BF16) for _ in range(1)])
        small = Ring("sm", [sb([128, 8], F32) for _ in range(16)])
        hst = Ring("hs", [sb([128, 2, 16], F32) for _ in range(2)])
        junk = sb([128, 256], BF16)
        b_junk = Buf()
        def post_body(c, yt, ytb, hs, hsb, gi, gb_):
            k.op("act", lambda e: e.activation(out=hs[:, :, 0:8], in_=hs[:, :, 0:8], func=AF.Sqrt, scale=1.0 / 256, bias=EPS),
                 reads=[hsb], writes=[hsb])
            k.op("dve", lambda e: e.reciprocal(out=hs[:, :, 8:16], in_=hs[:, :, 0:8]), reads=[hsb], writes=[hsb])
            yn, ynb_, _ = ynb.next()
            for br in range(2):
                k.op("dve" if br == 0 else "pool", lambda e, br=br: e.tensor_tensor(out=yn[:, br, :].rearrange("p (h v) -> p h v", v=256),
                     in0=yt[:, br], in1=hs[:, br, 8:16].unsqueeze(2).to_broadcast([128, 8, 256]), op=ALU.mult),
                     reads=[ytb, hsb], writes=[ynb_])
            o_, ob, osem = oT.next()
            for br in range(2):
                for half in range(4):
                    for q in range(4, 8):
                        kc = half * 4 + q - 4
                        k.op("pe", lambda e, q=q, kc=kc, br=br: e.transpose(pT[1][:, q * 128:(q + 1) * 128], yn[:, br, kc * 128:(kc + 1) * 128], cI),
                             reads=[ynb_, bI], writes=[psB[7]], inc=(q == 7))
                    for q in range(4, 8):
                        kc = half * 4 + q - 4
                        k.op("dve", lambda e, q=q, kc=kc, br=br: e.scalar_tensor_tensor(out=o_[:, br, kc, :], in0=pT[1][:, q * 128:(q + 1) * 128],
                             scalar=hn[:, br, kc:kc + 1], in1=gi[:, br, kc, :], op0=ALU.mult, op1=ALU.mult),
                             reads=[psB[7], b_hn, gb_], writes=[ob])
            k.dma("sp", YRT[:, c * 128:(c + 1) * 128].rearrange("(kc p) t -> p kc t", p=128), o_[:, 0], reads=[ob], sem=osem)
            k.dma("sp", YMT[:, c * 128:(c + 1) * 128].rearrange("(kc p) t -> p kc t", p=128), o_[:, 1], reads=[ob], sem=osem)

        PB = {}

        def pbuf(key):
            kk_ = key[0] if isinstance(key, tuple) else key
            if kk_ == "sT":
                return psB[key[1]]
            return psB[{"y": 3, "dS": 3, "n0": 4, "n1": 5, "b5": 6, "pt": 7}[kk_]]
        n1r = Ring("n1", [sb([128, 256], F32) for _ in range(4)])
        CH = {}
        IT = {}
        pT0 = pT[1]
        smallps = psb[6][:, 256:512]

        def chunk_ctx(c):
            ki, kb, vi, vb = load_chunk(c, True)
            sn, snb, snsem = sbin.next()
            k.dma("sp", sn[:, 0, :, 0:256], SBR[c], writes=[snb], sem=snsem)
            k.dma("sp", sn[:, 1], SBM[c], writes=[snb], sem=snsem)
            ui, ub, usem = uin.next()
            k.dma("sp", ui, NEGMX[:, c * 128:(c + 1) * 128].partition_broadcast(128), writes=[ub], sem=usem)
            gi, gb_, gsem = gin.next()
            k.dma("sp", gi[:, 0], RG[:, c * 128:(c + 1) * 128].rearrange("(kc p) t -> p kc t", p=128), writes=[gb_], sem=gsem)
            k.dma("sp", gi[:, 1], MO[:, c * 128:(c + 1) * 128].rearrange("(kc p) t -> p kc t", p=128), writes=[gb_], sem=gsem)
            yt, ytb, _ = ytm.next()
            hs, hsb, _ = hst.next()
            CH[c] = dict(ki=ki, kb=kb, vi=vi, vb=vb, sn=sn, snb=snb, ui=ui, ub=ub, gi=gi, gb_=gb_, yt=yt, ytb=ytb, hs=hs, hsb=hsb)

        def st0(i):
            c, h = divmod(i, 8)
            if h == 0:
                chunk_ctx(c)
            C = CH[c]
            ki, kb, ui, ub = C["ki"], C["kb"], C["ui"], C["ub"]
            I = IT[i] = {}
            slot = i % 3
            bank = slot
            o0 = 0
            spsb = pbuf(("sT", slot))
            sps = psb[bank][:, o0:o0 + 128]
            sps2 = psb[bank][:, o0 + 128:o0 + 256]
            I.update(sps=sps, sps2=sps2, spsb=spsb)
            k.op("pe", lambda e: e.matmul(sps, ki[:, 1, h, :], ki[:, 0, h, :], start=True, stop=True), reads=[kb], writes=[spsb], inc=False)
            k.op("pe", lambda e: e.matmul(sps2, ki[:, 3, h, :], ki[:, 2, h, :], start=True, stop=True), reads=[kb], writes=[spsb])
            I["f1"] = []
            I["f2"] = []
            for d in range(2):
                r = d * 8 + h
                f1, f1b, _ = wf.next()
                k.op("dve", lambda e, f1=f1, d=d, r=r: e.scalar_tensor_tensor(out=f1, in0=ui[:, r, :],
                     scalar=TM[:, d * 3 + 0, c * 8 + h:c * 8 + h + 1], in1=CF(7 + d), op0=ALU.add, op1=ALU.min),
                     reads=[ub, b_TM, bF], writes=[f1b])
                I["f1"].append((f1, f1b))
                f2, f2b, _ = wf.next()
                k.op("act", lambda e, f2=f2, r=r: e.activation(out=f2, in_=ui[:, r, :], func=AF.Exp, bias=chb[:, 0, r, c:c + 1]),
                     reads=[ub, b_chb], writes=[f2b])
                I["f2"].append((f2, f2b))
            qf_, qfb, _ = wk.next()
            k.op("pool", lambda e: e.tensor_tensor(out=qf_, in0=ki[:, 0, h, :], in1=RC["qf"][:, h, :], op=ALU.mult), reads=[kb, bRC], writes=[qfb])
            qb_, qbb, _ = wk.next()
            k.op("pool", lambda e: e.tensor_tensor(out=qb_, in0=ki[:, 0, h, :], in1=RC["qb"][:, h, :], op=ALU.mult), reads=[kb, bRC], writes=[qbb])
            I.update(qf=(qf_, qfb), qb=(qb_, qbb))
            I["pt"] = []
            for br in range(2):
                ts_ = (i % 2) * 2 + br
                pt = pT0[:, ts_ * 128:(ts_ + 1) * 128]
                ptb = pbuf(("pt", ts_))
                k.op("pe", lambda e, pt=pt, br=br: e.transpose(pt, ki[:, 1 + 2 * br, h, :], cI), reads=[kb, bI], writes=[ptb])
                I["pt"].append((pt, ptb))

        def st1(i):
            c, h = divmod(i, 8)
            C, I = CH[c], IT[i]
            ki, kb = C["ki"], C["kb"]
            for d in range(2):
                f1, f1b = I["f1"][d]
                k.op("act", lambda e, f1=f1: e.activation(out=f1, in_=f1, func=AF.Exp), reads=[f1b], writes=[f1b])
            I["kk"] = []
            for br in range(2):
                pt, ptb = I["pt"][br]
                kk_, kkb, _ = kt.next()
                if br == 0:
                    sc, rd = RC["kd"][:, 0, h:h + 1], [bRC]
                else:
                    sc, rd = TM[:, 0 * 3 + 1, c * 8 + h:c * 8 + h + 1], [b_TM]
                k.op("act", lambda e, kk_=kk_, pt=pt, sc=sc: e.activation(out=kk_, in_=pt, func=AF.Copy, scale=sc), reads=[ptb] + rd, writes=[kkb])
                I["kk"].append((kk_, kkb))
            sm_, smb, _ = wk.next()
            sps = I["sps"]
            k.op("dve", lambda e: e.tensor_tensor(out=sm_, in0=sps, in1=RC["mask"][:, h, :], op=ALU.mult), reads=[I["spsb"], bRC], writes=[smb])
            I["sm"] = (sm_, smb)
            I["qa"] = []
            for d in range(2):
                f2, f2b = I["f2"][d]
                qa, qab, _ = wk.next()
                k.op("pool", lambda e, qa=qa, f2=f2: e.tensor_tensor(out=qa, in0=ki[:, 2, h, :], in1=f2, op=ALU.mult), reads=[kb, f2b], writes=[qab])
                I["qa"].append((qa, qab))

        def st2(i):
            c, h = divmod(i, 8)
            C, I = CH[c], IT[i]
            vi, vb = C["vi"], C["vb"]
            I["sd"] = []
            sps2 = I["sps2"]
            for d in range(2):
                f1, f1b = I["f1"][d]
                sd, sdb, _ = wk.next()
                k.op("dve", lambda e, sd=sd, f1=f1: e.tensor_tensor(out=sd, in0=sps2, in1=f1, op=ALU.mult), reads=[I["spsb"], f1b], writes=[sdb])
                I["sd"].append((sd, sdb))
            kr, krb = I["kk"][0]
            km, kmb = I["kk"][1]
            b5 = pbuf("b5")
            bds = pbuf("dS")
            k.op("pe", lambda e: e.matmul(psb[3][:, 256:512], kr, vi[:, 0, h, 0:256], start=True, stop=True), reads=[krb, vb], writes=[bds])
            k.op("pe", lambda e: e.matmul(psb[6][:, 0:257], km, vi[:, 1, h, :], start=True, stop=True), reads=[kmb, vb], writes=[b5])

        def st3(i):
            c, h = divmod(i, 8)
            C, I = CH[c], IT[i]
            vi, vb, sn, snb = C["vi"], C["vb"], C["sn"], C["snb"]
            yb_, n0b, n1b = pbuf("y"), pbuf("n0"), pbuf("n1")
            ypa = psb[3][:, 0:256]
            npa = [psb[4][:, 0:257], psb[5][:, 0:257]]
            nb_ = [n0b, n1b]
            sm_, smb = I["sm"]
            qf_, qfb = I["qf"]
            qb_, qbb = I["qb"]
            k.op("pe", lambda e: e.matmul(ypa, sm_, vi[:, 0, h, 0:256], start=True, stop=False), reads=[smb, vb], writes=[yb_], inc=False)
            k.op("pe", lambda e: e.matmul(ypa, qf_, Sbf[:, h, :], start=False, stop=False), reads=[qfb, bSb[h]], writes=[yb_], inc=False)
            k.op("pe", lambda e: e.matmul(ypa, qb_, sn[:, 0, h, 0:256], start=False, stop=True), reads=[qbb, snb], writes=[yb_])
            for d in range(2):
                sd, sdb = I["sd"][d]
                qa, qab = I["qa"][d]
                st_t = Cbf if d == 0 else sn[:, 1]
                st_b = bCb[h] if d == 0 else snb
                k.op("pe", lambda e, d=d, sd=sd: e.matmul(npa[d], sd, vi[:, 1, h, :], start=True, stop=False), reads=[sdb, vb], writes=[nb_[d]], inc=False)
                k.op("pe", lambda e, d=d, qa=qa, st_t=st_t: e.matmul(npa[d], qa, st_t[:, h, :], start=False, stop=True), reads=[qab, st_b], writes=[nb_[d]])
            I.update(ypa=ypa, yb_=yb_, npa=npa, nb_=nb_)
            b5 = pbuf("b5")
            bds = pbuf("dS")
            k.op("dve", lambda e: e.scalar_tensor_tensor(out=S32[:, h, :], in0=S32[:, h, :], scalar=RC["cd"][:, 0, h:h + 1],
                 in1=psb[3][:, 256:512], op0=ALU.mult, op1=ALU.add), reads=[bds, bS[h], bRC], writes=[bS[h]])
            k.op("dve", lambda e: e.scalar_tensor_tensor(out=C32[:, h, :], in0=C32[:, h, :], scalar=chb[:, 1, h, c:c + 1],
                 in1=psb[6][:, 0:257], op0=ALU.mult, op1=ALU.add), reads=[b5, bC[h], b_chb], writes=[bC[h]])

        def st4(i):
            c, h = divmod(i, 8)
            C, I = CH[c], IT[i]
            yt, ytb = C["yt"], C["ytb"]
            ypa, npa = I["ypa"], I["npa"]
            k.op("act", lambda e: e.activation(out=yt[:, 0, h, :], in_=ypa, func=AF.Copy), reads=[I["yb_"]], writes=[ytb])
            k.op("act", lambda e: e.activation(out=yt[:, 1, h, :], in_=npa[0][:, 0:256], func=AF.Copy), reads=[I["nb_"][0]], writes=[ytb])
            n1c, n1cb, _ = n1r.next()
            k.op("act", lambda e: e.activation(out=n1c, in_=npa[1][:, 0:256], func=AF.Copy), reads=[I["nb_"][1]], writes=[n1cb])
            I["n1c"] = (n1c, n1cb)
            I["s8"] = []
            for d in range(2):
                den, denb = npa[d][:, 256:257], I["nb_"][d]
                s8, s8b, _ = small.next()
                k.op("act", lambda e, s8=s8, den=den: e.activation(out=s8[:, 0:1], in_=den, func=AF.Abs), reads=[denb], writes=[s8b])
                I["s8"].append((s8, s8b))
            k.op("pool", lambda e: e.tensor_copy(out=Sbf[:, h, :], in_=S32[:, h, :]), reads=[bS[h]], writes=[bSb[h]])
            k.op("pool", lambda e: e.tensor_copy(out=Cbf[:, h, :], in_=C32[:, h, :]), reads=[bC[h]], writes=[bCb[h]])

        def st5(i):
            c, h = divmod(i, 8)
            C, I = CH[c], IT[i]
            yt, ytb, hs, hsb = C["yt"], C["ytb"], C["hs"], C["hsb"]
            for d in range(2):
                s8, s8b = I["s8"][d]
                k.op("dve", lambda e, s8=s8, d=d: e.tensor_tensor(out=s8[:, 0:1], in0=s8[:, 0:1],
                     in1=TM[:, d * 3 + 2, c * 8 + h:c * 8 + h + 1], op=ALU.max), reads=[s8b, b_TM], writes=[s8b])
                k.op("dve", lambda e, s8=s8: e.reciprocal(out=s8[:, 1:2], in_=s8[:, 0:1]), reads=[s8b], writes=[s8b])
            k.op("act", lambda e: e.activation(out=junk, in_=yt[:, 0, h, :], func=AF.Square, accum_out=hs[:, 0, h:h + 1]),
                 reads=[ytb], writes=[b_junk, hsb])

        def st6(i):
            c, h = divmod(i, 8)
            C, I = CH[c], IT[i]
            yt, ytb = C["yt"], C["ytb"]
            s80, s80b = I["s8"][0]
            s81, s81b = I["s8"][1]
            n1c, n1cb = I["n1c"]
            k.op("dve", lambda e: e.tensor_scalar(out=yt[:, 1, h, :], in0=yt[:, 1, h, :], scalar1=s80[:, 1:2], scalar2=None, op0=ALU.mult),
                 reads=[ytb, s80b], writes=[ytb])
            k.op("dve", lambda e: e.scalar_tensor_tensor(out=yt[:, 1, h, :], in0=n1c, scalar=s81[:, 1:2], in1=yt[:, 1, h, :],
                 op0=ALU.mult, op1=ALU.add), reads=[n1cb, s81b, ytb], writes=[ytb])

        def st7(i):
            c, h = divmod(i, 8)
            C = CH[c]
            yt, ytb, hs, hsb = C["yt"], C["ytb"], C["hs"], C["hsb"]
            k.op("act", lambda e: e.activation(out=junk, in_=yt[:, 1, h, :], func=AF.Square, accum_out=hs[:, 1, h:h + 1]),
                 reads=[ytb], writes=[b_junk, hsb])
            if h == 7:
                post_body(c, C["yt"], C["ytb"], C["hs"], C["hsb"], C["gi"], C["gb_"])
                del CH[c]
            del IT[i]

        steps = [st0, st1, st2, st3, st4, st5, st6, st7]
        NI = NCH * 8
        for tick in range(NI + len(steps) - 1):
            for s_ in range(len(steps) - 1, -1, -1):
                i = tick - s_
                if 0 <= i < NI:
                    steps[s_](i)
        k.barrier()

    def resid_update(t, ps_list, gi_row, xr, gnt, b_gnt, last_layer, stg2, src_sb=None, rstd_ap=None, rstd_b=None):
        pass

    def phase_outproj(l, last):
        cur[0] = persist_end
        for step in range(2):
            save = cur[0]
            aT, aTb = load_resident(YRT if step == 0 else YMT, 2048)
            wring = Ring("wo", [sb([128, KC, 128], BF16) for _ in range(3)])
            gt = Ring("og", [sb([128, 512], BF16) for _ in range(3)])
            zt = Ring("oz", [sb([128, 512], F32) for _ in range(3)])
            yo = Ring("oy", [sb([128, 512], BF16) for _ in range(3)])
            Wm = w_ro[l] if step == 0 else w_mo[l]

            def ev(ci, c0, pb, si, g0, n, step=step):
                g_, gb_, gsem = gt.next()
                k.dma("sp", g_[:, 0:n], GRM[step * 2048 + c0:step * 2048 + c0 + 128, g0:g0 + n], writes=[gb_], sem=gsem)
                z_, zb, zsem = zt.next()
                if step == 0:
                    k.op("dve", lambda e: e.tensor_tensor(out=z_[:, 0:n], in0=psb[pb][:, 0:n], in1=g_[:, 0:n], op=ALU.mult),
                         reads=[psB[pb], gb_], writes=[zb])
                    k.dma("act", ZR[c0:c0 + 128, g0:g0 + n], z_[:, 0:n], reads=[zb], sem=zsem)
                else:
                    k.dma("sp", z_[:, 0:n], ZR[c0:c0 + 128, g0:g0 + n], writes=[zb], sem=zsem)
                    y_, yb, ysem = yo.next()
                    k.op("dve", lambda e: e.tensor_tensor(out=g_[:, 0:n], in0=psb[pb][:, 0:n], in1=g_[:, 0:n], op=ALU.mult),
                         reads=[psB[pb], gb_], writes=[gb_])
                    k.op("pool", lambda e: e.tensor_tensor(out=y_[:, 0:n], in0=g_[:, 0:n], in1=z_[:, 0:n], op=ALU.add),
                         reads=[gb_, zb], writes=[yb])
                    k.dma("act", YT[c0:c0 + 128, g0:g0 + n], y_[:, 0:n], reads=[yb], sem=ysem)
            proj_fm(aT, aTb, Wm, [i * 128 for i in range(16)], KC, ev, wring=wring)
            k.barrier()
            cur[0] = save
        wres = sb([128, KC, D], BF16)
        b_wres = Buf()
        for g in range(4):
            k.dma("pool", wres[:, :, g * 512:(g + 1) * 512], wslab_src(w_o[l], g * 512, 512), writes=[b_wres], sem="wr%d" % g)
        gn = sb([128, 2, D], F32)
        b_gn = Buf()
        for m in range(2):
            k.dma("sp", gn[:, m, :], GNROW[m:m + 1, :].partition_broadcast(128), writes=[b_gn], sem="gn")
        yin = Ring("pyi", [sb([128, KC, 128], BF16) for _ in range(2)])
        xin = Ring("pxi", [sb([128, D], F32) for _ in range(2)])
        ot = Ring("pot", [sb([128, D], F32) for _ in range(2)])
        junk = sb([128, D], BF16)
        b_junk = Buf()
        st4 = Ring("ps4", [sb([128, 4], F32) for _ in range(4)])
        for t in range(NCH):
            yi, yib, ysem = yin.next()
            k.dma("sp", yi, YT[:, t * 128:(t + 1) * 128].rearrange("(kc p) t -> p kc t", p=128), writes=[yib], sem=ysem)
            xi, xib, xsem = xin.next()
            k.dma("sp", xi, X[t * 128:(t + 1) * 128, :], writes=[xib], sem=xsem)
            o_, ob, osem = ot.next()
            s4, s4b, _ = st4.next()
            base = (t % 2) * 4
            for g in range(4):
                pb = base + g
                for kc in range(KC):
                    k.op("pe", lambda e, pb=pb, kc=kc, g=g, yi=yi: e.matmul(psb[pb], yi[:, kc, :], wres[:, kc, g * 512:(g + 1) * 512],
                         start=(kc == 0), stop=(kc == KC - 1)), reads=[yib, b_wres], writes=[psB[pb]], inc=(kc == KC - 1))
                k.op("act", lambda e, pb=pb, g=g, o_=o_: e.activation(out=o_[:, g * 512:(g + 1) * 512], in_=psb[pb], func=AF.Copy),
                     reads=[psB[pb]], writes=[ob])
            finish_resid(t, o_, ob, xi, xib, gn, b_gn, s4, s4b, junk, b_junk, osem, last=False)
        k.barrier()

    def finish_resid(t, o_, ob, xi, xib, gn, b_gn, s4, s4b, junk, b_junk, osem, last):
        m = 0 if t >= NCC else 1
        k.op("act", lambda e: e.activation(out=junk, in_=o_, func=AF.Square, accum_out=s4[:, 0:1]), reads=[ob], writes=[b_junk, s4b])
        k.op("act", lambda e: e.activation(out=s4[:, 1:2], in_=s4[:, 0:1], func=AF.Sqrt, scale=1.0 / D, bias=EPS), reads=[s4b], writes=[s4b])
        k.op("dve", lambda e: e.reciprocal(out=s4[:, 2:3], in_=s4[:, 1:2]), reads=[s4b], writes=[s4b])
        k.op("dve", lambda e: e.scalar_tensor_tensor(out=o_, in0=o_, scalar=s4[:, 2:3], in1=gn[:, m, :], op0=ALU.mult, op1=ALU.mult),
             reads=[ob, s4b, b_gn], writes=[ob])
        k.op("pool", lambda e: e.tensor_tensor(out=o_, in0=o_, in1=xi, op=ALU.add), reads=[ob, xib], writes=[ob])
        if last and t >= NCC:
            k.dma("pool", y_out[(t - NCC) * 128:(t - NCC + 1) * 128, :], o_, reads=[ob], sem=osem + "s")
        else:
            k.dma("pool", X[t * 128:(t + 1) * 128, :], o_, reads=[ob], sem=osem + "s")

    def phase_ffn_up(l, hT, hTb):
        save = cur[0]
        wring = Ring("wu", [sb([128, KC, 128], BF16) for _ in range(2)])
        cw = sb([128, 4, 2 * FC], F32)
        b_cw = Buf()
        for kk in range(4):
            for half in range(2):
                src = (fconv_w[l, kk, half * FF:(half + 1) * FF] if kk < 3 else fconv_b[l, half * FF:(half + 1) * FF])
                k.dma("sp", cw[:, kk, :].rearrange("p (c two) -> p c two", two=2)[:, :, half], src.rearrange("(c p) -> p c", p=128),
                      writes=[b_cw], sem="fcw", allow_slow_non_contiguous=True)
        cols = []
        for c in range(FC):
            cols += [c * 128, FF + c * 128]
        build.conv_chunks(hT, hTb, w_up[l], cols, cw, wring, None, mode="ffn", cwb=b_cw)
        k.barrier()
        cur[0] = save

    def phase_ffn_down(l, last):
        cur[0] = persist_end
        HALF = 1024
        ssq = sb([128, NCH, 2], F32)
        ssq2 = sb([128, NCH, 2], F32)
        b_ssq = Buf()
        ssq_end = [cur[0]]
        wres = sb([128, FC, HALF], BF16)
        junk = sb([128, 512], BF16)
        b_junk = Buf()
        ain = Ring("dai", [sb([128, FC, 128], BF16) for _ in range(2)])
        ost = Ring("dos", [sb([128, HALF], F32) for _ in range(2)])
        b_wres = Buf()
        for hcol in range(2):
            for g in range(2):
                k.dma("pool", wres[:, :, g * 512:(g + 1) * 512],
                      w_dn[l][:, hcol * HALF + g * 512:hcol * HALF + (g + 1) * 512].rearrange("(kc p) n -> p kc n", p=128),
                      writes=[b_wres], sem="dw%d" % g)
            for t in range(NCH):
                ai, aib, asem = ain.next()
                k.dma("sp", ai, ACTT[:, t * 128:(t + 1) * 128].rearrange("(kc p) t -> p kc t", p=128), writes=[aib], sem=asem)
                o_, ob, osem = ost.next()
                for g in range(2):
                    pb = (t % 2) * 2 + g
                    for kc in range(FC):
                        k.op("pe", lambda e, pb=pb, kc=kc, g=g, ai=ai: e.matmul(psb[pb], ai[:, kc, :], wres[:, kc, g * 512:(g + 1) * 512],
                             start=(kc == 0), stop=(kc == FC - 1)), reads=[aib, b_wres], writes=[psB[pb]], inc=(kc == FC - 1))
                    k.op("act", lambda e, pb=pb, g=g, o_=o_: e.activation(out=o_[:, g * 512:(g + 1) * 512], in_=psb[pb], func=AF.Copy),
                         reads=[psB[pb]], writes=[ob])
                k.op("act", lambda e, o_=o_, t=t, hcol=hcol: e.activation(out=junk, in_=o_[:, 0:512], func=AF.Square,
                     accum_out=ssq[:, t, hcol:hcol + 1]), reads=[ob], writes=[b_junk, b_ssq])
                k.op("act", lambda e, o_=o_, t=t, hcol=hcol: e.activation(out=junk, in_=o_[:, 512:1024], func=AF.Square,
                     accum_out=ssq2[:, t, hcol:hcol + 1]), reads=[ob], writes=[b_junk, b_ssq])
                k.dma("pool", FO[t * 128:(t + 1) * 128, hcol * HALF:(hcol + 1) * HALF], o_, reads=[ob], sem=osem)
        k.barrier()
        cur[0] = ssq_end[0]
        gn = sb([128, 2, D], F32)
        b_gn = Buf()
        for m in range(2):
            k.dma("sp", gn[:, m, :], GNROW[2 + m:3 + m, :].partition_broadcast(128), writes=[b_gn], sem="gn")
        fin = Ring("dfi", [sb([128, D], F32) for _ in range(2)])
        xin = Ring("dxi", [sb([128, D], F32) for _ in range(2)])
        st4 = Ring("ds4", [sb([128, 4], F32) for _ in range(4)])
        for t in range(NCH):
            o_, ob, osem = fin.next()
            k.dma("sp", o_, FO[t * 128:(t + 1) * 128, :], writes=[ob], sem=osem + "l")
            xi, xib, xsem = xin.next()
            k.dma("sp", xi, X[t * 128:(t + 1) * 128, :], writes=[xib], sem=xsem)
            s4, s4b, _ = st4.next()
            m = 0 if t >= NCC else 1
            k.op("dve", lambda e, s4=s4, t=t: e.tensor_tensor(out=s4[:, 0:2], in0=ssq[:, t, :], in1=ssq2[:, t, :], op=ALU.add), reads=[b_ssq], writes=[s4b])
            k.op("dve", lambda e, s4=s4: e.tensor_tensor(out=s4[:, 0:1], in0=s4[:, 0:1], in1=s4[:, 1:2], op=ALU.add), reads=[s4b], writes=[s4b])
            k.op("act", lambda e, s4=s4: e.activation(out=s4[:, 1:2], in_=s4[:, 0:1], func=AF.Sqrt, scale=1.0 / D, bias=EPS), reads=[s4b], writes=[s4b])
            k.op("dve", lambda e, s4=s4: e.reciprocal(out=s4[:, 2:3], in_=s4[:, 1:2]), reads=[s4b], writes=[s4b])
            k.op("dve", lambda e, s4=s4, o_=o_, m=m: e.scalar_tensor_tensor(out=o_, in0=o_, scalar=s4[:, 2:3], in1=gn[:, m, :], op0=ALU.mult, op1=ALU.mult),
                 reads=[ob, s4b, b_gn], writes=[ob])
            k.op("pool", lambda e, o_=o_, xi=xi: e.tensor_tensor(out=o_, in0=o_, in1=xi, op=ALU.add), reads=[ob, xib], writes=[ob])
            if last and t >= NCC:
                k.dma("pool", y_out[(t - NCC) * 128:(t - NCC + 1) * 128, :], o_, reads=[ob], sem=osem)
            else:
                k.dma("pool", X[t * 128:(t + 1) * 128, :], o_, reads=[ob], sem=osem)
        k.barrier()


    for l in range(L):
        last = l == L - 1
        phase_adaln(l)
        if stop == "adaln":
            break
        cur[0] = persist_end
        hT, hTb = norm_to_hT(0)
        if stop == "norm":
            break
        phase_inproj(l, hT, hTb)
        if stop == "inproj":
            break
        TM, b_TM = phase_gates(l)
        if stop == "gates":
            break
        phase_scan(l, TM, b_TM)
        if stop == "scan":
            break
        phase_outproj(l, last)
        if stop == "outproj":
            break
        cur[0] = persist_end
        hT, hTb = norm_to_hT(1)
        phase_ffn_up(l, hT, hTb)
        if stop == "ffnup":
            break
        phase_ffn_down(l, last)
    k.barrier()
    k.emit()
    return nc


def make_in_maps(inputs, cfg):
    n = cfg["NCORES"]
    hc = host_consts(cfg["TL"])
    maps = []
    shared = {kk: np.ascontiguousarray(inputs[kk]) for kk in (
        "w_ada", "b_ada", "norm_w", "w_in", "mlstm_conv_w", "mlstm_conv_b", "mlstm_gate_b", "ret_decay_exp",
        "head_norm_w", "w_ret_out", "w_mlstm_out", "w_o", "w_up", "ffn_conv_w", "ffn_conv_b", "w_down")}
    for b in range(n):
        m = dict(shared)
        m["x"] = np.ascontiguousarray(inputs["x"][b])
        m["ctx"] = np.ascontiguousarray(inputs["ctx"][b])
        m["cvec"] = np.ascontiguousarray(np.stack([inputs["c"][b], inputs["c_ctx"]], axis=0))
        m.update(hc)
        maps.append(m)
    return maps


def kernel(**inputs):
    cfg = dict(CFG)
    nc = build(cfg)
    maps = make_in_maps(inputs, cfg)
    res = run_bass_kernel_spmd(nc, maps, core_ids=list(range(cfg["NCORES"])))
    return np.stack([res.results[b]["y"] for b in range(cfg["NCORES"])], axis=0).astype(np.float32)
```

```python
import numpy as np
import ml_dtypes
import concourse.bass as bass
import concourse.mybir as mybir
from concourse.bass_utils import run_bass_kernel_spmd

F32 = mybir.dt.float32
BF16 = mybir.dt.bfloat16
ALU = mybir.AluOpType
AF = mybir.ActivationFunctionType

D = 2048
KC = 16
HD = 8
EPS = 1e-6
NEG = -30000.0
CFG = dict(TC=256, TL=4096, L=4, FF=5632, NCORES=8, dbg=False, stop=None)


class Buf:
    __slots__ = ("w", "r")

    def __init__(self):
        self.w = None
        self.r = {}


class Eng:
    def __init__(self, name, sem):
        self.name = name
        self.sem = sem
        self.cnt = 0
        self.q = []
        self.known = {}


class K:
    def __init__(self, nc):
        self.nc = nc
        self.E = {n: Eng(n, nc.alloc_semaphore(name="p_" + n)) for n in ("pe", "act", "dve", "pool", "sp")}
        self.dsem = {}
        self.freel = {}
        self.allsem = []

    def dma_sem(self, name, kind="sp"):
        if name not in self.dsem:
            fl = self.freel.setdefault(kind, [])
            if fl:
                self.dsem[name] = fl.pop()
            else:
                self.dsem[name] = [self.nc.alloc_semaphore(name="d_%d" % len(self.allsem)), 0, kind]
                self.allsem.append(self.dsem[name])
        return name

    def _waits(self, eng, reads, writes):
        deps = {}

        def add(t):
            if t is None:
                return
            s, v = t
            if deps.get(s, 0) < v:
                deps[s] = v

        for b in reads:
            add(b.w)
        for b in writes:
            add(b.w)
            for s, v in b.r.items():
                add((s, v))
        for s, v in deps.items():
            if s is eng.sem:
                if eng.name == "pe" or v > eng.cnt:
                    continue
            if eng.known.get(s, 0) >= v:
                continue
            eng.known[s] = v
            eng.q.append(("w", s, v))

    def _mark(self, tag, reads, writes):
        s, v = tag
        for b in reads:
            if b.r.get(s, 0) < v:
                b.r[s] = v
        for b in writes:
            b.w = tag
            b.r = {}

    def op(self, en, fn, reads=(), writes=(), inc=True):
        eng = self.E[en]
        self._waits(eng, reads, writes)
        if inc:
            eng.cnt += 1
            eng.q.append(("i", fn, eng.sem))
            tag = (eng.sem, eng.cnt)
        else:
            eng.q.append(("n", fn))
            tag = (eng.sem, eng.cnt + 1)
        self._mark(tag, reads, writes)

    def dma(self, en, out, in_, reads=(), writes=(), sem=None, **kw):
        eng = self.E[en]
        self.dma_sem(sem, en)
        self._waits(eng, reads, writes)
        d = self.dsem[sem]
        d[1] += 16
        eng.q.append(("d", out, in_, d[0], kw))
        self._mark((d[0], d[1]), reads, writes)

    def barrier(self):
        for en, eng in self.E.items():
            for en2, e2 in self.E.items():
                if e2 is eng or e2.cnt == 0:
                    continue
                if eng.known.get(e2.sem, 0) < e2.cnt:
                    eng.known[e2.sem] = e2.cnt
                    eng.q.append(("w", e2.sem, e2.cnt))
            for d in self.allsem:
                if d[1] > 0 and eng.known.get(d[0], 0) < d[1]:
                    eng.known[d[0]] = d[1]
                    eng.q.append(("w", d[0], d[1]))
        self.freel = {}
        for d in self.allsem:
            self.freel.setdefault(d[2], []).append(d)
        self.dsem = {}

    def emit(self):
        nc = self.nc
        with nc.Block() as block:
            def run(eng):
                def f(e):
                    for it in eng.q:
                        kk = it[0]
                        if kk == "w":
                            e.wait_ge(it[1], it[2])
                        elif kk == "i":
                            it[1](e).then_inc(it[2], 1)
                        elif kk == "n":
                            it[1](e)
                        else:
                            e.dma_start(out=it[1], in_=it[2], **it[4]).then_inc(it[3], 16)
                return f
            block.tensor(run(self.E["pe"]))
            block.scalar(run(self.E["act"]))
            block.vector(run(self.E["dve"]))
            block.gpsimd(run(self.E["pool"]))
            block.sync(run(self.E["sp"]))


class Ring:
    def __init__(self, name, aps):
        self.name = name
        self.aps = aps
        self.bufs = [Buf() for _ in aps]
        self.i = -1

    def next(self):
        self.i = (self.i + 1) % len(self.aps)
        return self.aps[self.i], self.bufs[self.i], "%s%d" % (self.name, self.i)


def host_consts(TL):
    ident = np.eye(128, dtype=np.float32)
    perm = np.zeros((128, 128), np.float32)
    for m in range(128):
        partner = m + 32 if (m % 64) < 32 else m - 32
        perm[partner, m] = 1.0
    j = np.arange(128, dtype=np.float32)[:, None]
    i = np.arange(128, dtype=np.float32)[None, :]
    one = np.ones((128, 128), np.float32)
    blocks = [
        ident,
        np.maximum(i - j, 0.0),
        np.maximum(j - i, 0.0),
        (i >= j).astype(np.float32),
        (j >= i).astype(np.float32),
        (i + 1.0) * one,
        (128.0 - i) * one,
        np.where(j <= i, 0.0, NEG).astype(np.float32),
        np.where(j >= i, 0.0, NEG).astype(np.float32),
        (127.0 - j) * one,
        j * one,
    ]
    cF = np.concatenate(blocks, axis=1).astype(np.float32)
    t = np.arange(TL)
    rows = (t // 64).astype(np.float32)
    cols = (t % 64).astype(np.float32)
    half = 32
    freqs = (10000.0 ** (-np.arange(half, dtype=np.float32) / half)).astype(np.float32)
    rope = np.zeros((128, 2, TL), np.float32)
    for p in range(128):
        pos = rows if p < 64 else cols
        ang = (pos * freqs[p % 32]).astype(np.float32)
        rope[p, 0] = np.cos(ang)
        rope[p, 1] = np.sin(ang) * (-1.0 if (p % 64) < 32 else 1.0)
    return dict(cI=ident.astype(ml_dtypes.bfloat16), cP=perm.astype(ml_dtypes.bfloat16), cF=cF, cROPE=rope)


def build(cfg):
    TC, TL, L, FF, dbg, stop = cfg["TC"], cfg["TL"], cfg["L"], cfg["FF"], cfg["dbg"], cfg["stop"]
    TT = TC + TL
    NCH = TT // 128
    NCC = TC // 128
    FC = FF // 128
    INC = 16416
    nc = bass.Bass("TRN2", target_bir_lowering=False)
    k = K(nc)

    def din(name, shape, dt=F32):
        return nc.dram_tensor(name, list(shape), dt, kind="ExternalInput").ap()

    def dscr(name, shape, dt):
        return nc.dram_tensor(name, list(shape), dt, kind=("ExternalOutput" if dbg else "Internal")).ap()

    x_in = din("x", [TL, D])
    ctx_in = din("ctx", [TC, D])
    cvec = din("cvec", [2, D])
    w_ada = din("w_ada", [L, D, 6 * D])
    b_ada = din("b_ada", [L, 6 * D])
    norm_w = din("norm_w", [L, 4, D])
    w_in = din("w_in", [L, D, INC])
    mconv_w = din("mlstm_conv_w", [L, 3, 2048])
    mconv_b = din("mlstm_conv_b", [L, 2048])
    gate_b = din("mlstm_gate_b", [L, 4, 8])
    decay_e = din("ret_decay_exp", [L, 2, 8])
    hnorm_w = din("head_norm_w", [L, 2, 2048])
    w_ro = din("w_ret_out", [L, D, D])
    w_mo = din("w_mlstm_out", [L, D, D])
    w_o = din("w_o", [L, D, D])
    w_up = din("w_up", [L, D, 2 * FF])
    fconv_w = din("ffn_conv_w", [L, 3, 2 * FF])
    fconv_b = din("ffn_conv_b", [L, 2 * FF])
    w_dn = din("w_down", [L, FF, D])
    cI_d = din("cI", [128, 128], BF16)
    cP_d = din("cP", [128, 128], BF16)
    cF_d = din("cF", [128, 11 * 128])
    cR_d = din("cROPE", [128, 2, TL])
    y_out = nc.dram_tensor("y", [TL, D], F32, kind="ExternalOutput").ap()

    X = dscr("X", [TT, D], F32)
    QR = dscr("QR", [1024, TT], BF16)
    KR = dscr("KR", [1024, TT], BF16)
    QM = dscr("QM", [1024, TT], BF16)
    KM = dscr("KM", [1024, TT], BF16)
    VR = dscr("VR", [TT, 2048], BF16)
    VM = dscr("VM", [TT, 2048], BF16)
    RG = dscr("RG", [2048, TT], BF16)
    MO = dscr("MO", [2048, TT], BF16)
    GRM = dscr("GRM", [4096, TT], BF16)
    GATES = dscr("GATES", [4, 8, TT], F32)
    NEGMX = dscr("NEGMX", [16, TT], F32)
    CHT = dscr("CHT", [2, 16, NCH], F32)
    SBR = dscr("SBR", [NCH, 128, 8, 256], BF16)
    SBM = dscr("SBM", [NCH, 128, 8, 257], BF16)
    YRT = dscr("YRT", [2048, TT], BF16)
    YMT = dscr("YMT", [2048, TT], BF16)
    ZR = dscr("ZR", [2048, TT], F32)
    YT = dscr("YT", [2048, TT], BF16)
    ACTT = dscr("ACTT", [FF, TT], BF16)
    FO = dscr("FO", [TT, D], F32)
    GNROW = dscr("GNROW", [4, D], F32)

    cnt = [0]
    cur = [0]

    ARENA_BYTES = 206 * 1024
    arena = nc.alloc_sbuf_tensor("arena", [128, ARENA_BYTES], mybir.dt.uint8).ap()

    def sb(shape, dt):
        nel = int(np.prod(shape[1:]))
        nbytes = nel * (4 if dt == F32 else 2)
        off = cur[0]
        cur[0] += (nbytes + 63) // 64 * 64
        assert cur[0] <= ARENA_BYTES, ("sbuf overflow", cur[0])
        flat = arena[0:shape[0], off:off + nbytes].bitcast(dt)
        if len(shape) == 2:
            return flat
        names = ["a%d" % i for i in range(len(shape) - 1)]
        pat = "p (%s) -> p %s" % (" ".join(names), " ".join(names))
        return flat.rearrange(pat, **{n: int(v) for n, v in zip(names[1:], shape[2:])})

    psb = [nc.alloc_psum_tensor("ps%d" % i, [128, 512], F32).ap() for i in range(8)]
    psB = [Buf() for _ in range(8)]

    cI = sb([128, 128], BF16)
    cP = sb([128, 128], BF16)
    cF = sb([128, 11 * 128], F32)
    bI, bP, bF = Buf(), Buf(), Buf()
    k.dma("sp", cI, cI_d, writes=[bI], sem="c0")
    k.dma("sp", cP, cP_d, writes=[bP], sem="c1")
    k.dma("sp", cF, cF_d, writes=[bF], sem="c2")

    def CF(i):
        return cF[:, i * 128:(i + 1) * 128]

    ones_bf = sb([128, 128], BF16)
    b_ones = Buf()
    k.op("pool", lambda e: e.memset(ones_bf, 1.0), writes=[b_ones])
    sT = sb([128, KC, 2], BF16)
    b_sT = Buf()
    MODS = sb([128, 8, KC], F32)
    b_MODS = Buf()
    persist_end = cur[0]

    k.dma("sp", X[0:TC, :], ctx_in, sem="c3")
    k.dma("sp", X[TC:TT, :], x_in, sem="c4")

    cur[0] = persist_end
    cv32 = sb([128, KC, 2], F32)
    b_cv = Buf()
    for m in range(2):
        k.dma("sp", cv32[:, :, m], cvec[m, :].rearrange("(kc p) -> p kc", p=128), writes=[b_cv], sem="c5",
              allow_slow_non_contiguous=True)
    k.op("act", lambda e: e.activation(out=sT, in_=cv32, func=AF.Silu), reads=[b_cv], writes=[b_sT])
    k.barrier()

    segs = [(0, TC)] + [(TC + i * 512, 512) for i in range(TL // 512)]

    def wslab_src(w2d, c0, ncols):
        return w2d[:, c0:c0 + ncols].rearrange("(kc p) n -> p kc n", p=128)

    def phase_adaln(l):
        cur[0] = persist_end
        sRep = sb([128, 2, KC, 128], BF16)
        b_sRep = Buf()
        for m in range(2):
            k.op("dve", lambda e, m=m: e.tensor_copy(out=sRep[:, m], in_=sT[:, :, m:m + 1].to_broadcast([128, KC, 128])),
                 reads=[b_sT], writes=[b_sRep])
        wr = Ring("wa", [sb([128, KC, 1536], BF16) for _ in range(2)])
        brow = sb([1, 6 * D], BF16)
        b_brow = Buf()
        k.dma("pool", brow, b_ada[l:l + 1, :], writes=[b_brow], sem="ab")
        nwf = sb([128, 2, KC], F32)
        b_nwf = Buf()
        for i, r in enumerate((0, 2)):
            k.dma("sp", nwf[:, i, :], norm_w[l, r, :].rearrange("(kc p) -> p kc", p=128), writes=[b_nwf], sem="an",
                  allow_slow_non_contiguous=True)
        nwb = sb([128, 2, D], F32)
        b_nwb = Buf()
        for i, r in enumerate((1, 3)):
            k.dma("sp", nwb[:, i, :], norm_w[l, r:r + 1, :].partition_broadcast(128), writes=[b_nwb], sem="anb")
        modfm = sb([128, 96, 2], F32)
        b_modfm = Buf()
        gtile = sb([128, 4, D], F32)
        b_gt = Buf()
        pm = psb[0][:, 0:192].rearrange("p (j m) -> p j m", m=2)
        for s in range(8):
            ws, wb, wsem = wr.next()
            k.dma("pool", ws, wslab_src(w_ada[l], s * 1536, 1536), writes=[wb], sem=wsem)
            for jj in range(12):
                j = s * 12 + jj
                for kc in range(KC):
                    k.op("pe", lambda e, j=j, jj=jj, kc=kc, ws=ws: e.matmul(pm[:, j, :], ws[:, kc, jj * 128:(jj + 1) * 128],
                         sT[:, kc, :], start=(kc == 0), stop=False), reads=[wb, b_sT], writes=[psB[0]], inc=False)
                k.op("pe", lambda e, j=j: e.matmul(pm[:, j, :], brow[0:1, j * 128:(j + 1) * 128], ones_bf[0:1, 0:2],
                     start=False, stop=True), reads=[b_brow, b_ones], writes=[psB[0]])
            for g in range(3):
                c0 = s * 1536 + g * 512
                which = None
                if 2 * D <= c0 < 3 * D:
                    which = 0
                elif 5 * D <= c0 < 6 * D:
                    which = 1
                if which is None:
                    continue
                off = c0 - (2 * D if which == 0 else 5 * D)
                for m in range(2):
                    pb = 1 + m
                    for kc in range(KC):
                        k.op("pe", lambda e, pb=pb, m=m, kc=kc, ws=ws, g=g: e.matmul(psb[pb], sRep[:, m, kc, :],
                             ws[:, kc, g * 512:(g + 1) * 512], start=(kc == 0), stop=False),
                             reads=[wb, b_sRep], writes=[psB[pb]], inc=False)
                    k.op("pe", lambda e, pb=pb, c0=c0: e.matmul(psb[pb], ones_bf[0:1, :], brow[0:1, c0:c0 + 512],
                         start=False, stop=True), reads=[b_brow, b_ones], writes=[psB[pb]])
                    k.op("dve", lambda e, pb=pb, which=which, m=m, off=off: e.tensor_tensor(
                        out=gtile[:, which * 2 + m, off:off + 512], in0=psb[pb], in1=nwb[:, which, off:off + 512],
                        op=ALU.mult), reads=[psB[pb], b_nwb], writes=[b_gt])
        k.op("dve", lambda e: e.tensor_copy(out=modfm, in_=pm), reads=[psB[0]], writes=[b_modfm])
        for which in range(2):
            shj, scj = (0, 16) if which == 0 else (48, 64)
            for m in range(2):
                o = which * 4 + m * 2
                k.op("dve", lambda e, o=o, scj=scj, m=m, which=which: e.scalar_tensor_tensor(
                    out=MODS[:, o, :], in0=modfm[:, scj:scj + 16, m], scalar=1.0, in1=nwf[:, which, :],
                    op0=ALU.add, op1=ALU.mult), reads=[b_modfm, b_nwf], writes=[b_MODS])
                k.op("dve", lambda e, o=o, shj=shj, m=m: e.tensor_copy(out=MODS[:, o + 1, :], in_=modfm[:, shj:shj + 16, m]),
                     reads=[b_modfm], writes=[b_MODS])
        for i in range(4):
            k.dma("sp", GNROW[i:i + 1, :], gtile[0:1, i, :], reads=[b_gt], sem="ag")
        k.barrier()

    def norm_to_hT(which):
        hT = sb([128, KC, TT], BF16)
        hTb = [Buf() for _ in range(NCH)]
        save = cur[0]
        xr = Ring("xi", [sb([128, D], F32) for _ in range(2)])
        junk = sb([128, D], BF16)
        b_junk = Buf()
        xsr = Ring("xs", [sb([128, D], BF16) for _ in range(2)])
        ss = sb([128, NCH], F32)
        rs = sb([128, NCH], F32)
        rstd = sb([128, NCH], F32)
        b_ss = [Buf() for _ in range(NCH)]
        pT = [psb[i].bitcast(BF16) for i in range(4)]
        for t in range(NCH):
            o = which * 4 + (0 if t >= NCC else 2)
            xi, xb, xsem = xr.next()
            k.dma("sp", xi, X[t * 128:(t + 1) * 128, :], writes=[xb], sem=xsem)
            k.op("act", lambda e, xi=xi, t=t: e.activation(out=junk, in_=xi, func=AF.Square, accum_out=ss[:, t:t + 1]),
                 reads=[xb], writes=[b_junk, b_ss[t]])
            k.op("act", lambda e, t=t: e.activation(out=rs[:, t:t + 1], in_=ss[:, t:t + 1], func=AF.Sqrt, scale=1.0 / D,
                 bias=CF(3)[:, 0:1] if False else EPS), reads=[b_ss[t]], writes=[b_ss[t]])
            k.op("dve", lambda e, t=t: e.reciprocal(out=rstd[:, t:t + 1], in_=rs[:, t:t + 1]), reads=[b_ss[t]], writes=[b_ss[t]])
            xs, xsb, _ = xsr.next()
            k.op("act", lambda e, xs=xs, xi=xi, t=t: e.activation(out=xs, in_=xi, func=AF.Copy, scale=rstd[:, t:t + 1]),
                 reads=[xb, b_ss[t]], writes=[xsb])
            for half in range(2):
                pb = (t % 2) * 2 + half
                for q in range(8):
                    kc = half * 8 + q
                    k.op("pe", lambda e, pb=pb, q=q, kc=kc, xs=xs: e.transpose(pT[pb][:, q * 128:(q + 1) * 128],
                         xs[:, kc * 128:(kc + 1) * 128], cI), reads=[xsb, bI], writes=[psB[pb]], inc=(q == 7))
                for q in range(8):
                    kc = half * 8 + q
                    en = "dve" if q % 2 == 0 else "act"
                    if en == "dve":
                        k.op("dve", lambda e, pb=pb, q=q, kc=kc, t=t, o=o: e.tensor_scalar(
                            out=hT[:, kc, t * 128:(t + 1) * 128], in0=pT[pb][:, q * 128:(q + 1) * 128],
                            scalar1=MODS[:, o, kc:kc + 1], scalar2=MODS[:, o + 1, kc:kc + 1], op0=ALU.mult, op1=ALU.add),
                            reads=[psB[pb], b_MODS], writes=[hTb[t]])
                    else:
                        k.op("act", lambda e, pb=pb, q=q, kc=kc, t=t, o=o: e.activation(
                            out=hT[:, kc, t * 128:(t + 1) * 128], in_=pT[pb][:, q * 128:(q + 1) * 128], func=AF.Identity,
                            scale=MODS[:, o, kc:kc + 1], bias=MODS[:, o + 1, kc:kc + 1]),
                            reads=[psB[pb], b_MODS], writes=[hTb[t]])
        k.barrier()
        cur[0] = save
        return hT, hTb

    def load_resident(src, rows):
        n = rows // 128
        t = sb([128, n, TT], BF16)
        b = Buf()
        for kc in range(n):
            k.dma("sp", t[:, kc, :], src[kc * 128:(kc + 1) * 128, :], writes=[b], sem="lr%d" % (kc % 4))
        return t, [b] * NCH

    def proj_fm(hT, hTb, w2d, col_list, nk, evac_chunk, M=128, wring=None):
        pbi = [0]
        for ci, c0 in enumerate(col_list):
            ws, wb, wsem = wring.next()
            k.dma("pool", ws[:, :, 0:M], wslab_src(w2d, c0, M), writes=[wb], sem=wsem)
            outs = []
            for si, (g0, n) in enumerate(segs):
                pb = pbi[0] % 4
                pbi[0] += 1
                tl = [hTb[t] for t in range(g0 // 128, (g0 + n) // 128)]
                for kc in range(nk):
                    k.op("pe", lambda e, pb=pb, kc=kc, ws=ws, g0=g0, n=n: e.matmul(psb[pb][0:M, 0:n], ws[:, kc, 0:M],
                         hT[:, kc, g0:g0 + n], start=(kc == 0), stop=(kc == nk - 1)),
                         reads=[wb] + tl, writes=[psB[pb]], inc=(kc == nk - 1))
                evac_chunk(ci, c0, pb, si, g0, n)

    def phase_inproj(l, hT, hTb):
        save = cur[0]
        W = w_in[l]
        rings = {}

        def reset():
            k.barrier()
            cur[0] = save
            rings["w"] = Ring("wi", [sb([128, KC, 128], BF16) for _ in range(2)])
            rings["s"] = Ring("sg", [sb([128, 512], BF16) for _ in range(4)])
            return rings["w"], rings["s"]
        wring, stg = reset()
        def mk_simple(dst, func, base):
            def ev(ci, c0, pb, si, g0, n):
                st, sbf, ssem = stg.next()
                k.op("act", lambda e: e.activation(out=st[:, 0:n], in_=psb[pb][:, 0:n], func=func), reads=[psB[pb]], writes=[sbf])
                r0 = c0 - base
                k.dma("sp", dst[r0:r0 + 128, g0:g0 + n], st[:, 0:n], reads=[sbf], sem=ssem)
            return ev
        proj_fm(hT, hTb, W, [4096 + i * 128 for i in range(16)], KC, mk_simple(RG, AF.Silu, 4096), wring=wring)
        proj_fm(hT, hTb, W, [10240 + i * 128 for i in range(16)], KC, mk_simple(MO, AF.Sigmoid, 10240), wring=wring)
        proj_fm(hT, hTb, W, [12320 + i * 128 for i in range(32)], KC, mk_simple(GRM, AF.Sigmoid, 12320), wring=wring)
        wring, stg = reset()
        ropeT = Ring("rp", [sb([128, 2, 512], F32) for _ in range(4)])
        xbf = Ring("xb", [sb([128, 512], BF16) for _ in range(4)])
        t12 = Ring("t1", [sb([128, 2, 512], F32) for _ in range(4)])
        def mk_rope(dst, base, scale):
            def ev(ci, c0, pb, si, g0, n):
                r0 = c0 - base
                st, sbf, ssem = stg.next()
                if g0 < TC:
                    k.op("act", lambda e: e.activation(out=st[:, 0:n], in_=psb[pb][:, 0:n], func=AF.Copy, scale=scale),
                         reads=[psB[pb]], writes=[sbf])
                else:
                    rp, rpb, rsem = ropeT.next()
                    k.dma("act", rp[:, :, 0:n], cR_d[:, :, g0 - TC:g0 - TC + n], writes=[rpb], sem=rsem)
                    xb_, xbb, _ = xbf.next()
                    k.op("act", lambda e: e.activation(out=xb_[:, 0:n], in_=psb[pb][:, 0:n], func=AF.Copy, scale=scale),
                         reads=[psB[pb]], writes=[xbb])
                    pp = 4 + pb
                    k.op("pe", lambda e: e.matmul(psb[pp][:, 0:n], cP, xb_[:, 0:n], start=True, stop=True),
                         reads=[xbb, bP], writes=[psB[pp]])
                    tt, ttb, _ = t12.next()
                    k.op("dve", lambda e: e.tensor_tensor(out=tt[:, 0, 0:n], in0=xb_[:, 0:n], in1=rp[:, 0, 0:n], op=ALU.mult),
                         reads=[xbb, rpb], writes=[ttb])
                    k.op("dve", lambda e: e.tensor_tensor(out=tt[:, 1, 0:n], in0=psb[pp][:, 0:n], in1=rp[:, 1, 0:n], op=ALU.mult),
                         reads=[psB[pp], rpb], writes=[ttb])
                    k.op("dve", lambda e: e.tensor_tensor(out=st[:, 0:n], in0=tt[:, 0, 0:n], in1=tt[:, 1, 0:n], op=ALU.add),
                         reads=[ttb], writes=[sbf])
                k.dma("sp", dst[r0:r0 + 128, g0:g0 + n], st[:, 0:n], reads=[sbf], sem=ssem)
            return ev
        proj_fm(hT, hTb, W, [i * 128 for i in range(8)], KC, mk_rope(QR, 0, 128.0 ** -0.5), wring=wring)
        proj_fm(hT, hTb, W, [1024 + i * 128 for i in range(8)], KC, mk_rope(KR, 1024, 1.0), wring=wring)
        wring, stg = reset()
        cw = sb([128, 4, KC], F32)
        b_cw = Buf()
        for kk in range(3):
            k.dma("sp", cw[:, kk, :], mconv_w[l, kk, :].rearrange("(c p) -> p c", p=128), writes=[b_cw], sem="cw",
                  allow_slow_non_contiguous=True)
        k.dma("sp", cw[:, 3, :], mconv_b[l, :].rearrange("(c p) -> p c", p=128), writes=[b_cw], sem="cw",
              allow_slow_non_contiguous=True)
        conv_chunks(hT, hTb, W, [6144 + i * 128 for i in range(16)], cw, wring,
                    lambda ci: ((QM, ci * 128, 128.0 ** -0.5) if ci < 8 else (KM, (ci - 8) * 128, 1.0)), mode="silu", cwb=b_cw)
        wring, stg = reset()
        gb = sb([8, 4], F32)
        b_gb = Buf()
        for g in range(4):
            k.dma("sp", gb[:, g:g + 1], gate_b[l, g, :].rearrange("(h o) -> h o", o=1), writes=[b_gb], sem="gb")
        gst = Ring("gs", [sb([8, 512], F32) for _ in range(2)])
        def ev_g(ci, c0, pb, si, g0, n):
            st, sbf, ssem = gst.next()
            k.op("act", lambda e: e.activation(out=st[:, 0:n], in_=psb[pb][0:8, 0:n], func=AF.Identity, bias=gb[:, ci:ci + 1]),
                 reads=[psB[pb], b_gb], writes=[sbf])
            k.dma("sp", GATES[ci, :, g0:g0 + n], st[:, 0:n], reads=[sbf], sem=ssem)
        proj_fm(hT, hTb, W, [12288 + g * 8 for g in range(4)], KC, ev_g, M=8, wring=wring)
        wring, stg = reset()
        wv = Ring("wv", [sb([128, KC, 512], BF16) for _ in range(2)])
        vst = Ring("vs", [sb([128, 512], BF16) for _ in range(3)])
        pbi = 0
        for dst, base in ((VR, 2048), (VM, 8192)):
            for g in range(4):
                ws, wb, wsem = wv.next()
                k.dma("pool", ws, wslab_src(W, base + g * 512, 512), writes=[wb], sem=wsem)
                for t in range(NCH):
                    pb = pbi % 4
                    pbi += 1
                    for kc in range(KC):
                        k.op("pe", lambda e, pb=pb, kc=kc, ws=ws, t=t: e.matmul(psb[pb], hT[:, kc, t * 128:(t + 1) * 128],
                             ws[:, kc, :], start=(kc == 0), stop=(kc == KC - 1)), reads=[wb, hTb[t]], writes=[psB[pb]],
                             inc=(kc == KC - 1))
                    st, sbf, ssem = vst.next()
                    k.op("act", lambda e, pb=pb, st=st: e.activation(out=st, in_=psb[pb], func=AF.Copy), reads=[psB[pb]], writes=[sbf])
                    k.dma("sp", dst[t * 128:(t + 1) * 128, g * 512:(g + 1) * 512], st, reads=[sbf], sem=ssem)
        k.barrier()
        cur[0] = save

    def conv_chunks(hT, hTb, W, cols, cw, wring, dst_fn, mode, cwb=None):
        raw = Ring("rw", [sb([128, TT + 3], F32) for _ in range(1)])
        for r_ap in raw.aps:
            k.op("pool", lambda e, r_ap=r_ap: e.memset(r_ap, 0.0), writes=[raw.bufs[raw.aps.index(r_ap)]])
        acc = Ring("ac", [sb([128, TT], F32) for _ in range(1)])
        ost = Ring("oc", [sb([128, TT], BF16) for _ in range(1)])
        state = {}

        def colof(g):
            return g + 1 if g < TC else g + 2

        def ev(ci, c0, pb, si, g0, n):
            if si == 0:
                state["raw"] = raw.next()
            rw, rwb, _ = state["raw"]
            k.op("act", lambda e: e.activation(out=rw[:, colof(g0):colof(g0) + n], in_=psb[pb][:, 0:n], func=AF.Copy),
                 reads=[psB[pb]], writes=[rwb])
            if si != len(segs) - 1:
                return
            a, ab, _ = acc.next()
            def views(off):
                return [(rw[:, off:off + TC], a[:, 0:TC]), (rw[:, off + TC + 1:off + TC + 1 + TL], a[:, TC:TT])]
            for (src, dsta) in views(1):
                k.op("dve", lambda e, src=src, dsta=dsta: e.tensor_scalar(out=dsta, in0=src, scalar1=cw[:, 1, ci:ci + 1],
                     scalar2=cw[:, 3, ci:ci + 1], op0=ALU.mult, op1=ALU.add), reads=[rwb, state["cwb"]], writes=[ab])
            for kk, off in ((0, 0), (2, 2)):
                for (src, dsta) in views(off):
                    k.op("dve", lambda e, src=src, dsta=dsta, kk=kk: e.scalar_tensor_tensor(
                        out=dsta, in0=src, scalar=cw[:, kk, ci:ci + 1], in1=dsta, op0=ALU.mult, op1=ALU.add),
                        reads=[rwb, state["cwb"], ab], writes=[ab])
            if mode == "silu":
                dst, r0, scale = dst_fn(ci)
                o, ob, osem = ost.next()
                if scale == 1.0:
                    k.op("act", lambda e: e.activation(out=o, in_=a, func=AF.Silu), reads=[ab], writes=[ob])
                else:
                    k.op("act", lambda e: e.activation(out=a, in_=a, func=AF.Silu), reads=[ab], writes=[ab])
                    k.op("act", lambda e: e.activation(out=o, in_=a, func=AF.Copy, scale=scale), reads=[ab], writes=[ob])
                k.dma("sp", dst[r0:r0 + 128, :], o, reads=[ob], sem=osem)
            else:
                if ci % 2 == 0:
                    o, ob, osem = ost.next()
                    k.op("act", lambda e: e.activation(out=o, in_=a, func=AF.Silu), reads=[ab], writes=[ob])
                    state["a"] = (o, ob, osem)
                else:
                    o, ob, osem = state["a"]
                    k.op("dve", lambda e: e.tensor_tensor(out=o, in0=o, in1=a, op=ALU.mult), reads=[ob, ab], writes=[ob])
                    r0 = (ci // 2) * 128
                    k.dma("sp", ACTT[r0:r0 + 128, :], o, reads=[ob], sem=osem)
        state["cwb"] = cwb
        proj_fm(hT, hTb, W, cols, KC, ev, wring=wring)

    build.conv_chunks = conv_chunks
    def phase_gates(l):
        cur[0] = persist_end
        TM = sb([128, 6, NCH * 8], F32)
        tm_end = cur[0]
        G = [sb([8, TT], F32) for _ in range(4)]
        bG = [Buf() for _ in range(4)]
        for g in range(4):
            k.dma("sp", G[g], GATES[g], writes=[bG[g]], sem="pg%d" % g)
        one8 = sb([8, 1], F32)
        b_o8 = Buf()
        k.op("pool", lambda e: e.memset(one8, 1.0), writes=[b_o8])
        b_TM = Buf()
        cht = sb([8, 2, 2, NCH], F32)
        b_cht = Buf()
        tmp = sb([8, TT], F32)
        b_tmp = Buf()
        for d in range(2):
            IG, FG, bIG, bFG = G[2 * d], G[2 * d + 1], bG[2 * d], bG[2 * d + 1]
            k.op("act", lambda e, FG=FG: e.activation(out=FG, in_=FG, func=AF.Exp, scale=-1.0), reads=[bFG], writes=[bFG])
            k.op("act", lambda e, FG=FG: e.activation(out=FG, in_=FG, func=AF.Ln, bias=1.0), reads=[bFG], writes=[bFG])
            if d == 0:
                sl = [(slice(0, TT), 0.0)]
            else:
                sl = [("rc", None), ("rl", None)]
            def scan(out, dat, op1, init0, d=d):
                if d == 0:
                    k.op("dve", lambda e: e.tensor_tensor_scan(out=out[:, 0:TT], data0=one8[:, 0:1].to_broadcast([8, TT]),
                         data1=dat[:, 0:TT], initial=init0, op0=ALU.mult, op1=op1), reads=[b_o8, bIG, bFG, b_tmp], writes=[b_tmp, bIG, bFG])
                else:
                    k.op("dve", lambda e: e.tensor_tensor_scan(out=out[:, TC - 1::-1] if True else None,
                         data0=one8[:, 0:1].to_broadcast([8, TC]), data1=dat[:, TC - 1::-1], initial=init0, op0=ALU.mult, op1=op1),
                         reads=[b_o8, bIG, bFG, b_tmp], writes=[b_tmp, bIG, bFG])
                    k.op("dve", lambda e: e.tensor_tensor_scan(out=out[:, TT - 1:TC - 1:-1],
                         data0=one8[:, 0:1].to_broadcast([8, TL]), data1=dat[:, TT - 1:TC - 1:-1], initial=out[:, 0:1],
                         op0=ALU.mult, op1=op1), reads=[b_o8, bIG, bFG, b_tmp], writes=[b_tmp, bIG, bFG])
            scan(tmp, FG, ALU.add, 0.0)
            k.op("dve", lambda e, IG=IG: e.tensor_tensor(out=IG, in0=IG, in1=tmp, op=ALU.add), reads=[bIG, b_tmp], writes=[bIG])
            k.op("pool", lambda e, FG=FG: e.tensor_copy(out=FG, in_=tmp), reads=[b_tmp], writes=[bFG])
            scan(tmp, IG, ALU.max, 0.0)
            k.op("dve", lambda e, FG=FG: e.tensor_tensor(out=FG, in0=FG, in1=tmp, op=ALU.subtract), reads=[bFG, b_tmp], writes=[bFG])
            k.op("act", lambda e, FG=FG: e.activation(out=FG, in_=FG, func=AF.Exp), reads=[bFG], writes=[bFG])
            Mx3 = tmp.rearrange("p (c j) -> p c j", j=128)
            endj = 127 if d == 0 else 0
            Mend = Mx3[:, :, endj]
            mp = cht[:, d, 0, :]
            k.op("pool", lambda e, mp=mp: e.memset(mp, 0.0), writes=[b_cht])
            if d == 0:
                k.op("dve", lambda e, mp=mp, Mend=Mend: e.tensor_copy(out=mp[:, 1:NCH], in_=Mend[:, 0:NCH - 1]), reads=[b_tmp], writes=[b_cht])
            else:
                if NCC > 1:
                    k.op("dve", lambda e, mp=mp, Mend=Mend: e.tensor_copy(out=mp[:, 0:NCC - 1], in_=Mend[:, 1:NCC]), reads=[b_tmp], writes=[b_cht])
                k.op("dve", lambda e, mp=mp, Mend=Mend: e.tensor_copy(out=mp[:, NCC:NCH - 1], in_=Mend[:, NCC + 1:NCH]), reads=[b_tmp], writes=[b_cht])
                k.op("dve", lambda e, mp=mp, Mend=Mend: e.tensor_copy(out=mp[:, NCH - 1:NCH], in_=Mend[:, 0:1]), reads=[b_tmp], writes=[b_cht])
            k.op("dve", lambda e, mp=mp, Mend=Mend, d=d: e.tensor_tensor(out=cht[:, d, 1, :], in0=mp, in1=Mend, op=ALU.subtract),
                 reads=[b_tmp, b_cht], writes=[b_cht])
            k.op("act", lambda e, d=d: e.activation(out=cht[:, d, 1, :], in_=cht[:, d, 1, :], func=AF.Exp), reads=[b_cht], writes=[b_cht])
            wt = sb([8, TT], F32) if d == 0 else state_wt[0]
            if d == 0:
                state_wt.append(wt)
            b_wt = Buf()
            k.op("dve", lambda e, IG=IG, Mend=Mend, wt=wt: e.tensor_tensor(out=wt.rearrange("p (c j) -> p c j", j=128),
                 in0=IG.rearrange("p (c j) -> p c j", j=128), in1=Mend.unsqueeze(2).to_broadcast([8, NCH, 128]), op=ALU.subtract),
                 reads=[bIG, b_tmp], writes=[b_wt])
            k.op("act", lambda e, wt=wt: e.activation(out=wt, in_=wt, func=AF.Exp), reads=[b_wt], writes=[b_wt])
            for qi, (src, sbuf_) in enumerate(((IG, bIG), (wt, b_wt), (FG, bFG))):
                pb = d * 3 + qi
                for c in range(NCH):
                    k.op("pe", lambda e, pb=pb, c=c, src=src: e.matmul(psb[pb][:, c * 8:(c + 1) * 8], src[:, c * 128:(c + 1) * 128],
                         CF(0)[0:8, 0:8], start=True, stop=True), reads=[sbuf_, bF], writes=[psB[pb]], inc=(c == NCH - 1))
                k.op("dve", lambda e, pb=pb: e.tensor_copy(out=TM[:, pb, :], in_=psb[pb][:, 0:NCH * 8]), reads=[psB[pb]], writes=[b_TM])
            k.op("pool", lambda e: e.tensor_scalar(out=tmp, in0=tmp, scalar1=-1.0, scalar2=None, op0=ALU.mult), reads=[b_tmp], writes=[b_tmp])
            k.dma("sp", NEGMX[d * 8:(d + 1) * 8, :], tmp, reads=[b_tmp], sem="pn%d" % d)
            for q in range(2):
                k.dma("sp", CHT[q, d * 8:(d + 1) * 8, :], cht[:, d, q, :], reads=[b_cht], sem="pc%d" % q)
        k.barrier()
        cur[0] = tm_end
        return TM, b_TM

    state_wt = []
    def layer_consts(l):
        de = sb([128, 16], F32)
        b_de = Buf()
        k.dma("sp", de, decay_e[l:l + 1].rearrange("o a b -> o (a b)").partition_broadcast(128), writes=[b_de], sem="ld")
        lg = sb([128, 16], F32)
        b_lg = Buf()
        k.op("act", lambda e: e.activation(out=lg, in_=de, func=AF.Exp, scale=-float(np.log(2.0))), reads=[b_de], writes=[b_lg])
        k.op("act", lambda e: e.activation(out=lg, in_=lg, func=AF.Ln, scale=-1.0, bias=1.0), reads=[b_lg], writes=[b_lg])
        mask = sb([128, 8, 128], BF16)
        qf = sb([128, 8, 128], BF16)
        qb = sb([128, 8, 128], BF16)
        kd = sb([128, 2, 8], F32)
        cd = sb([128, 2, 8], F32)
        t1 = sb([128, 128], F32)
        t2 = sb([128, 128], F32)
        b_c, b_t1, b_t2 = Buf(), Buf(), Buf()
        for h in range(8):
            k.op("act", lambda e, h=h: e.activation(out=t1, in_=CF(1), func=AF.Exp, scale=lg[:, h:h + 1]), reads=[bF, b_lg], writes=[b_t1])
            k.op("act", lambda e, h=h: e.activation(out=t2, in_=CF(2), func=AF.Exp, scale=lg[:, 8 + h:9 + h]), reads=[bF, b_lg], writes=[b_t2])
            k.op("dve", lambda e: e.tensor_tensor(out=t1, in0=t1, in1=CF(3), op=ALU.mult), reads=[b_t1, bF], writes=[b_t1])
            k.op("dve", lambda e: e.tensor_tensor(out=t2, in0=t2, in1=CF(4), op=ALU.mult), reads=[b_t2, bF], writes=[b_t2])
            k.op("dve", lambda e, h=h: e.tensor_tensor(out=mask[:, h, :], in0=t1, in1=t2, op=ALU.add), reads=[b_t1, b_t2], writes=[b_c])
            k.op("act", lambda e, h=h: e.activation(out=qf[:, h, :], in_=CF(5), func=AF.Exp, scale=lg[:, h:h + 1]), reads=[bF, b_lg], writes=[b_c])
            k.op("act", lambda e, h=h: e.activation(out=qb[:, h, :], in_=CF(6), func=AF.Exp, scale=lg[:, 8 + h:9 + h]), reads=[bF, b_lg], writes=[b_c])
            k.op("act", lambda e, h=h: e.activation(out=kd[:, 0, h:h + 1], in_=CF(9)[:, 0:1], func=AF.Exp, scale=lg[:, h:h + 1]), reads=[bF, b_lg], writes=[b_c])
            k.op("act", lambda e, h=h: e.activation(out=kd[:, 1, h:h + 1], in_=CF(10)[:, 0:1], func=AF.Exp, scale=lg[:, 8 + h:9 + h]), reads=[bF, b_lg], writes=[b_c])
        k.op("act", lambda e: e.activation(out=cd.rearrange("p a b -> p (a b)"), in_=lg, func=AF.Exp, scale=128.0), reads=[b_lg], writes=[b_c])
        return dict(mask=mask, qf=qf, qb=qb, kd=kd, cd=cd, b=b_c)

    def phase_scan(l, TM, b_TM):
        RC = layer_consts(l)
        bRC = RC["b"]
        chb = sb([128, 2, 16, NCH], F32)
        b_chb = Buf()
        k.dma("sp", chb, CHT.rearrange("q r c -> (q r c)").rearrange("(o n) -> o n", o=1).partition_broadcast(128)
              if False else CHT.rearrange("q r c -> (q r c)").partition_broadcast(128), writes=[b_chb], sem="chb")
        hn = sb([128, 2, KC], F32)
        b_hn = Buf()
        for i in range(2):
            k.dma("sp", hn[:, i, :], hnorm_w[l, i, :].rearrange("(c p) -> p c", p=128), writes=[b_hn], sem="hn",
                  allow_slow_non_contiguous=True)
        S32 = sb([128, 8, 256], F32)
        Sbf = sb([128, 8, 256], BF16)
        C32 = sb([128, 8, 257], F32)
        Cbf = sb([128, 8, 257], BF16)
        bS = [Buf() for _ in range(8)]
        bSb = [Buf() for _ in range(8)]
        bC = [Buf() for _ in range(8)]
        bCb = [Buf() for _ in range(8)]
        kin = Ring("ki", [sb([128, 4, 8, 128], BF16) for _ in range(2)])
        vin = Ring("vi", [sb([128, 2, 8, 257], BF16) for _ in range(2)])
        for v_ap, vb in zip(vin.aps, vin.bufs):
            k.op("pool", lambda e, v_ap=v_ap: e.memset(v_ap[:, 1, :, 256:257], 1.0), writes=[vb])
        kt = Ring("kt", [sb([128, 128], BF16) for _ in range(8)])
        pT = [psb[6].bitcast(BF16), psb[7].bitcast(BF16)]
        tcount = [0]

        def zero_states():
            k.op("pool", lambda e: e.memset(S32, 0.0), writes=bS)
            k.op("pool", lambda e: e.memset(Sbf, 0.0), writes=bSb)
            k.op("pool", lambda e: e.memset(C32, 0.0), writes=bC)
            k.op("pool", lambda e: e.memset(Cbf, 0.0), writes=bCb)

        def load_chunk(c, need_q):
            ki, kb, ksem = kin.next()
            for i, src in enumerate((QR, KR, QM, KM)):
                if not need_q and i in (0, 2):
                    continue
                k.dma("sp", ki[:, i], src[:, c * 128:(c + 1) * 128].rearrange("(h d) t -> d h t", d=128), writes=[kb], sem=ksem)
            vi, vb, vsem = vin.next()
            k.dma("sp", vi[:, 0, :, 0:256], VR[c * 128:(c + 1) * 128, :].rearrange("t (h v) -> t h v", v=256), writes=[vb], sem=vsem)
            k.dma("sp", vi[:, 1, :, 0:256], VM[c * 128:(c + 1) * 128, :].rearrange("t (h v) -> t h v", v=256), writes=[vb], sem=vsem)
            return ki, kb, vi, vb

        def state_update(c, h, ki, kb, vi, vb, d):
            for br in range(2):
                ti = tcount[0]
                tcount[0] += 1
                slot = ti % 8
                pt = pT[0][:, slot * 128:(slot + 1) * 128]
                k.op("pe", lambda e, pt=pt, br=br: e.transpose(pt, ki[:, 1 + 2 * br, h, :], cI), reads=[kb, bI], writes=[psB[6]])
                kk_, kkb, _ = kt.next()
                if br == 0:
                    sc = RC["kd"][:, d, h:h + 1]
                    rd = [bRC]
                else:
                    sc = TM[:, d * 3 + 1, c * 8 + h:c * 8 + h + 1]
                    rd = [b_TM]
                k.op("act", lambda e, kk_=kk_, pt=pt, sc=sc: e.activation(out=kk_, in_=pt, func=AF.Copy, scale=sc),
                     reads=[psB[6]] + rd, writes=[kkb])
                pb = 4 + (ti % 2)
                if br == 0:
                    k.op("pe", lambda e, pb=pb, kk_=kk_: e.matmul(psb[pb][:, 0:256], kk_, vi[:, 0, h, 0:256], start=True, stop=True),
                         reads=[kkb, vb], writes=[psB[pb]])
                    k.op("dve", lambda e, pb=pb: e.scalar_tensor_tensor(out=S32[:, h, :], in0=S32[:, h, :], scalar=RC["cd"][:, d, h:h + 1],
                         in1=psb[pb][:, 0:256], op0=ALU.mult, op1=ALU.add), reads=[psB[pb], bS[h], bRC], writes=[bS[h]])
                    k.op("act", lambda e: e.activation(out=Sbf[:, h, :], in_=S32[:, h, :], func=AF.Copy), reads=[bS[h]], writes=[bSb[h]])
                else:
                    k.op("pe", lambda e, pb=pb, kk_=kk_: e.matmul(psb[pb][:, 0:257], kk_, vi[:, 1, h, :], start=True, stop=True),
                         reads=[kkb, vb], writes=[psB[pb]])
                    k.op("dve", lambda e, pb=pb: e.scalar_tensor_tensor(out=C32[:, h, :], in0=C32[:, h, :],
                         scalar=chb[:, 1, d * 8 + h, c:c + 1], in1=psb[pb][:, 0:257], op0=ALU.mult, op1=ALU.add),
                         reads=[psB[pb], bC[h], b_chb], writes=[bC[h]])
                    k.op("act", lambda e: e.activation(out=Cbf[:, h, :], in_=C32[:, h, :], func=AF.Copy), reads=[bC[h]], writes=[bCb[h]])

        zero_states()
        BWD = list(range(NCC - 1, -1, -1)) + list(range(NCH - 1, NCC - 1, -1))
        for c in BWD:
            ki, kb, vi, vb = load_chunk(c, False)
            k.dma("sp", SBR[c], Sbf, reads=bSb, sem="s1r")
            k.dma("sp", SBM[c], Cbf, reads=bCb, sem="s1m")
            for h in range(8):
                state_update(c, h, ki, kb, vi, vb, 1)
        k.barrier()
        zero_states()
        sbin = Ring("sn", [sb([128, 2, 8, 257], BF16) for _ in range(2)])
        uin = Ring("ui", [sb([128, 16, 128], F32) for _ in range(2)])
        gin = Ring("gi", [sb([128, 2, KC, 128], BF16) for _ in range(2)])
        wk = Ring("wk", [sb([128, 128], BF16) for _ in range(28)])
        wf = Ring("wf", [sb([128, 128], F32) for _ in range(12)])
        ytm = Ring("yt", [sb([128, 2, 8, 256], F32) for _ in range(2)])
        ynb = Ring("yn", [sb([128, 2, 2048], BF16) for _ in range(1)])
        oT = Ring("ot", [sb([128, 2, KC, 128], BF16) for _ in range(1)])
        small = Ring("sm", [sb([128, 8], F32) for _ in range(16)])
        hst = Ring("hs", [sb([128, 2, 16], F32) for _ in range(2)])
        junk = sb([128, 256], BF16)
        b_junk = Buf()
        def post_body(c, yt, ytb, hs, hsb, gi, gb_):
            k.op("act", lambda e: e.activation(out=hs[:, :, 0:8], in_=hs[:, :, 0:8], func=AF.Sqrt, scale=1.0 / 256, bias=EPS),
                 reads=[hsb], writes=[hsb])
            k.op("dve", lambda e: e.reciprocal(out=hs[:, :, 8:16], in_=hs[:, :, 0:8]), reads=[hsb], writes=[hsb])
            yn, ynb_, _ = ynb.next()
            for br in range(2):
                k.op("dve" if br == 0 else "pool", lambda e, br=br: e.tensor_tensor(out=yn[:, br, :].rearrange("p (h v) -> p h v", v=256),
                     in0=yt[:, br], in1=hs[:, br, 8:16].unsqueeze(2).to_broadcast([128, 8, 256]), op=ALU.mult),
                     reads=[ytb, hsb], writes=[ynb_])
            o_, ob, osem = oT.next()
            for br in range(2):
                for half in range(4):
                    for q in range(4, 8):
                        kc = half * 4 + q - 4
                        k.op("pe", lambda e, q=q, kc=kc, br=br: e.transpose(pT[1][:, q * 128:(q + 1) * 128], yn[:, br, kc * 128:(kc + 1) * 128], cI),
                             reads=[ynb_, bI], writes=[psB[7]], inc=(q == 7))
                    for q in range(4, 8):
                        kc = half * 4 + q - 4
                        k.op("dve", lambda e, q=q, kc=kc, br=br: e.scalar_tensor_tensor(out=o_[:, br, kc, :], in0=pT[1][:, q * 128:(q + 1) * 128],
                             scalar=hn[:, br, kc:kc + 1], in1=gi[:, br, kc, :], op0=ALU.mult, op1=ALU.mult),
                             reads=[psB[7], b_hn, gb_], writes=[ob])
            k.dma("sp", YRT[:, c * 128:(c + 1) * 128].rearrange("(kc p) t -> p kc t", p=128), o_[:, 0], reads=[ob], sem=osem)
            k.dma("sp", YMT[:, c * 128:(c + 1) * 128].rearrange("(kc p) t -> p kc t", p=128), o_[:, 1], reads=[ob], sem=osem)

        PB = {}

        def pbuf(key):
            kk_ = key[0] if isinstance(key, tuple) else key
            if kk_ == "sT":
                return psB[key[1]]
            return psB[{"y": 3, "dS": 3, "n0": 4, "n1": 5, "b5": 6, "pt": 7}[kk_]]
        n1r = Ring("n1", [sb([128, 256], F32) for _ in range(4)])
        CH = {}
        IT = {}
        pT0 = pT[1]
        smallps = psb[6][:, 256:512]

        def chunk_ctx(c):
            ki, kb, vi, vb = load_chunk(c, True)
            sn, snb, snsem = sbin.next()
            k.dma("sp", sn[:, 0, :, 0:256], SBR[c], writes=[snb], sem=snsem)
            k.dma("sp", sn[:, 1], SBM[c], writes=[snb], sem=snsem)
            ui, ub, usem = uin.next()
            k.dma("sp", ui, NEGMX[:, c * 128:(c + 1) * 128].partition_broadcast(128), writes=[ub], sem=usem)
            gi, gb_, gsem = gin.next()
            k.dma("sp", gi[:, 0], RG[:, c * 128:(c + 1) * 128].rearrange("(kc p) t -> p kc t", p=128), writes=[gb_], sem=gsem)
            k.dma("sp", gi[:, 1], MO[:, c * 128:(c + 1) * 128].rearrange("(kc p) t -> p kc t", p=128), writes=[gb_], sem=gsem)
            yt, ytb, _ = ytm.next()
            hs, hsb, _ = hst.next()
            CH[c] = dict(ki=ki, kb=kb, vi=vi, vb=vb, sn=sn, snb=snb, ui=ui, ub=ub, gi=gi, gb_=gb_, yt=yt, ytb=ytb, hs=hs, hsb=hsb)

        def st0(i):
            c, h = divmod(i, 8)
            if h == 0:
                chunk_ctx(c)
            C = CH[c]
            ki, kb, ui, ub = C["ki"], C["kb"], C["ui"], C["ub"]
            I = IT[i] = {}
            slot = i % 3
            bank = slot
            o0 = 0
            spsb = pbuf(("sT", slot))
            sps = psb[bank][:, o0:o0 + 128]
            sps2 = psb[bank][:, o0 + 128:o0 + 256]
            I.update(sps=sps, sps2=sps2, spsb=spsb)
            k.op("pe", lambda e: e.matmul(sps, ki[:, 1, h, :], ki[:, 0, h, :], start=True, stop=True), reads=[kb], writes=[spsb], inc=False)
            k.op("pe", lambda e: e.matmul(sps2, ki[:, 3, h, :], ki[:, 2, h, :], start=True, stop=True), reads=[kb], writes=[spsb])
            I["f1"] = []
            I["f2"] = []
            for d in range(2):
                r = d * 8 + h
                f1, f1b, _ = wf.next()
                k.op("dve", lambda e, f1=f1, d=d, r=r: e.scalar_tensor_tensor(out=f1, in0=ui[:, r, :],
                     scalar=TM[:, d * 3 + 0, c * 8 + h:c * 8 + h + 1], in1=CF(7 + d), op0=ALU.add, op1=ALU.min),
                     reads=[ub, b_TM, bF], writes=[f1b])
                I["f1"].append((f1, f1b))
                f2, f2b, _ = wf.next()
                k.op("act", lambda e, f2=f2, r=r: e.activation(out=f2, in_=ui[:, r, :], func=AF.Exp, bias=chb[:, 0, r, c:c + 1]),
                     reads=[ub, b_chb], writes=[f2b])
                I["f2"].append((f2, f2b))
            qf_, qfb, _ = wk.next()
            k.op("pool", lambda e: e.tensor_tensor(out=qf_, in0=ki[:, 0, h, :], in1=RC["qf"][:, h, :], op=ALU.mult), reads=[kb, bRC], writes=[qfb])
            qb_, qbb, _ = wk.next()
            k.op("pool", lambda e: e.tensor_tensor(out=qb_, in0=ki[:, 0, h, :], in1=RC["qb"][:, h, :], op=ALU.mult), reads=[kb, bRC], writes=[qbb])
            I.update(qf=(qf_, qfb), qb=(qb_, qbb))
            I["pt"] = []
            for br in range(2):
                ts_ = (i % 2) * 2 + br
                pt = pT0[:, ts_ * 128:(ts_ + 1) * 128]
                ptb = pbuf(("pt", ts_))
                k.op("pe", lambda e, pt=pt, br=br: e.transpose(pt, ki[:, 1 + 2 * br, h, :], cI), reads=[kb, bI], writes=[ptb])
                I["pt"].append((pt, ptb))

        def st1(i):
            c, h = divmod(i, 8)
            C, I = CH[c], IT[i]
            ki, kb = C["ki"], C["kb"]
            for d in range(2):
                f1, f1b = I["f1"][d]
                k.op("act", lambda e, f1=f1: e.activation(out=f1, in_=f1, func=AF.Exp), reads=[f1b], writes=[f1b])
            I["kk"] = []
            for br in range(2):
                pt, ptb = I["pt"][br]
                kk_, kkb, _ = kt.next()
                if br == 0:
                    sc, rd = RC["kd"][:, 0, h:h + 1], [bRC]
                else:
                    sc, rd = TM[:, 0 * 3 + 1, c * 8 + h:c * 8 + h + 1], [b_TM]
                k.op("act", lambda e, kk_=kk_, pt=pt, sc=sc: e.activation(out=kk_, in_=pt, func=AF.Copy, scale=sc), reads=[ptb] + rd, writes=[kkb])
                I["kk"].append((kk_, kkb))
            sm_, smb, _ = wk.next()
            sps = I["sps"]
            k.op("dve", lambda e: e.tensor_tensor(out=sm_, in0=sps, in1=RC["mask"][:, h, :], op=ALU.mult), reads=[I["spsb"], bRC], writes=[smb])
            I["sm"] = (sm_, smb)
            I["qa"] = []
            for d in range(2):
                f2, f2b = I["f2"][d]
                qa, qab, _ = wk.next()
                k.op("pool", lambda e, qa=qa, f2=f2: e.tensor_tensor(out=qa, in0=ki[:, 2, h, :], in1=f2, op=ALU.mult), reads=[kb, f2b], writes=[qab])
                I["qa"].append((qa, qab))

        def st2(i):
            c, h = divmod(i, 8)
            C, I = CH[c], IT[i]
            vi, vb = C["vi"], C["vb"]
            I["sd"] = []
            sps2 = I["sps2"]
            for d in range(2):
                f1, f1b = I["f1"][d]
                sd, sdb, _ = wk.next()
                k.op("dve", lambda e, sd=sd, f1=f1: e.tensor_tensor(out=sd, in0=sps2, in1=f1, op=ALU.mult), reads=[I["spsb"], f1b], writes=[sdb])
                I["sd"].append((sd, sdb))
            kr, krb = I["kk"][0]
            km, kmb = I["kk"][1]
            b5 = pbuf("b5")
            bds = pbuf("dS")
            k.op("pe", lambda e: e.matmul(psb[3][:, 256:512], kr, vi[:, 0, h, 0:256], start=True, stop=True), reads=[krb, vb], writes=[bds])
            k.op("pe", lambda e: e.matmul(psb[6][:, 0:257], km, vi[:, 1, h, :], start=True, stop=True), reads=[kmb, vb], writes=[b5])

        def st3(i):
            c, h = divmod(i, 8)
            C, I = CH[c], IT[i]
            vi, vb, sn, snb = C["vi"], C["vb"], C["sn"], C["snb"]
            yb_, n0b, n1b = pbuf("y"), pbuf("n0"), pbuf("n1")
            ypa = psb[3][:, 0:256]
            npa = [psb[4][:, 0:257], psb[5][:, 0:257]]
            nb_ = [n0b, n1b]
            sm_, smb = I["sm"]
            qf_, qfb = I["qf"]
            qb_, qbb = I["qb"]
            k.op("pe", lambda e: e.matmul(ypa, sm_, vi[:, 0, h, 0:256], start=True, stop=False), reads=[smb, vb], writes=[yb_], inc=False)
            k.op("pe", lambda e: e.matmul(ypa, qf_, Sbf[:, h, :], start=False, stop=False), reads=[qfb, bSb[h]], writes=[yb_], inc=False)
            k.op("pe", lambda e: e.matmul(ypa, qb_, sn[:, 0, h, 0:256], start=False, stop=True), reads=[qbb, snb], writes=[yb_])
            for d in range(2):
                sd, sdb = I["sd"][d]
                qa, qab = I["qa"][d]
                st_t = Cbf if d == 0 else sn[:, 1]
                st_b = bCb[h] if d == 0 else snb
                k.op("pe", lambda e, d=d, sd=sd: e.matmul(npa[d], sd, vi[:, 1, h, :], start=True, stop=False), reads=[sdb, vb], writes=[nb_[d]], inc=False)
                k.op("pe", lambda e, d=d, qa=qa, st_t=st_t: e.matmul(npa[d], qa, st_t[:, h, :], start=False, stop=True), reads=[qab, st_b], writes=[nb_[d]])
            I.update(ypa=ypa, yb_=yb_, npa=npa, nb_=nb_)
            b5 = pbuf("b5")
            bds = pbuf("dS")
            k.op("dve", lambda e: e.scalar_tensor_tensor(out=S32[:, h, :], in0=S32[:, h, :], scalar=RC["cd"][:, 0, h:h + 1],
                 in1=psb[3][:, 256:512], op0=ALU.mult, op1=ALU.add), reads=[bds, bS[h], bRC], writes=[bS[h]])
            k.op("dve", lambda e: e.scalar_tensor_tensor(out=C32[:, h, :], in0=C32[:, h, :], scalar=chb[:, 1, h, c:c + 1],
                 in1=psb[6][:, 0:257], op0=ALU.mult, op1=ALU.add), reads=[b5, bC[h], b_chb], writes=[bC[h]])

        def st4(i):
            c, h = divmod(i, 8)
            C, I = CH[c], IT[i]
            yt, ytb = C["yt"], C["ytb"]
            ypa, npa = I["ypa"], I["npa"]
            k.op("act", lambda e: e.activation(out=yt[:, 0, h, :], in_=ypa, func=AF.Copy), reads=[I["yb_"]], writes=[ytb])
            k.op("act", lambda e: e.activation(out=yt[:, 1, h, :], in_=npa[0][:, 0:256], func=AF.Copy), reads=[I["nb_"][0]], writes=[ytb])
            n1c, n1cb, _ = n1r.next()
            k.op("act", lambda e: e.activation(out=n1c, in_=npa[1][:, 0:256], func=AF.Copy), reads=[I["nb_"][1]], writes=[n1cb])
            I["n1c"] = (n1c, n1cb)
            I["s8"] = []
            for d in range(2):
                den, denb = npa[d][:, 256:257], I["nb_"][d]
                s8, s8b, _ = small.next()
                k.op("act", lambda e, s8=s8, den=den: e.activation(out=s8[:, 0:1], in_=den, func=AF.Abs), reads=[denb], writes=[s8b])
                I["s8"].append((s8, s8b))
            k.op("pool", lambda e: e.tensor_copy(out=Sbf[:, h, :], in_=S32[:, h, :]), reads=[bS[h]], writes=[bSb[h]])
            k.op("pool", lambda e: e.tensor_copy(out=Cbf[:, h, :], in_=C32[:, h, :]), reads=[bC[h]], writes=[bCb[h]])

        def st5(i):
            c, h = divmod(i, 8)
            C, I = CH[c], IT[i]
            yt, ytb, hs, hsb = C["yt"], C["ytb"], C["hs"], C["hsb"]
            for d in range(2):
                s8, s8b = I["s8"][d]
                k.op("dve", lambda e, s8=s8, d=d: e.tensor_tensor(out=s8[:, 0:1], in0=s8[:, 0:1],
                     in1=TM[:, d * 3 + 2, c * 8 + h:c * 8 + h + 1], op=ALU.max), reads=[s8b, b_TM], writes=[s8b])
                k.op("dve", lambda e, s8=s8: e.reciprocal(out=s8[:, 1:2], in_=s8[:, 0:1]), reads=[s8b], writes=[s8b])
            k.op("act", lambda e: e.activation(out=junk, in_=yt[:, 0, h, :], func=AF.Square, accum_out=hs[:, 0, h:h + 1]),
                 reads=[ytb], writes=[b_junk, hsb])

        def st6(i):
            c, h = divmod(i, 8)
            C, I = CH[c], IT[i]
            yt, ytb = C["yt"], C["ytb"]
            s80, s80b = I["s8"][0]
            s81, s81b = I["s8"][1]
            n1c, n1cb = I["n1c"]
            k.op("dve", lambda e: e.tensor_scalar(out=yt[:, 1, h, :], in0=yt[:, 1, h, :], scalar1=s80[:, 1:2], scalar2=None, op0=ALU.mult),
                 reads=[ytb, s80b], writes=[ytb])
            k.op("dve", lambda e: e.scalar_tensor_tensor(out=yt[:, 1, h, :], in0=n1c, scalar=s81[:, 1:2], in1=yt[:, 1, h, :],
                 op0=ALU.mult, op1=ALU.add), reads=[n1cb, s81b, ytb], writes=[ytb])

        def st7(i):
            c, h = divmod(i, 8)
            C = CH[c]
            yt, ytb, hs, hsb = C["yt"], C["ytb"], C["hs"], C["hsb"]
            k.op("act", lambda e: e.activation(out=junk, in_=yt[:, 1, h, :], func=AF.Square, accum_out=hs[:, 1, h:h + 1]),
                 reads=[ytb], writes=[b_junk, hsb])
            if h == 7:
                post_body(c, C["yt"], C["ytb"], C["hs"], C["hsb"], C["gi"], C["gb_"])
                del CH[c]
            del IT[i]

        steps = [st0, st1, st2, st3, st4, st5, st6, st7]
        NI = NCH * 8
        for tick in range(NI + len(steps) - 1):
            for s_ in range(len(steps) - 1, -1, -1):
                i = tick - s_
                if 0 <= i < NI:
                    steps[s_](i)
        k.barrier()

    def resid_update(t, ps_list, gi_row, xr, gnt, b_gnt, last_layer, stg2, src_sb=None, rstd_ap=None, rstd_b=None):
        pass

    def phase_outproj(l, last):
        cur[0] = persist_end
        for step in range(2):
            save = cur[0]
            aT, aTb = load_resident(YRT if step == 0 else YMT, 2048)
            wring = Ring("wo", [sb([128, KC, 128], BF16) for _ in range(3)])
            gt = Ring("og", [sb([128, 512], BF16) for _ in range(3)])
            zt = Ring("oz", [sb([128, 512], F32) for _ in range(3)])
            yo = Ring("oy", [sb([128, 512], BF16) for _ in range(3)])
            Wm = w_ro[l] if step == 0 else w_mo[l]

            def ev(ci, c0, pb, si, g0, n, step=step):
                g_, gb_, gsem = gt.next()
                k.dma("sp", g_[:, 0:n], GRM[step * 2048 + c0:step * 2048 + c0 + 128, g0:g0 + n], writes=[gb_], sem=gsem)
                z_, zb, zsem = zt.next()
                if step == 0:
                    k.op("dve", lambda e: e.tensor_tensor(out=z_[:, 0:n], in0=psb[pb][:, 0:n], in1=g_[:, 0:n], op=ALU.mult),
                         reads=[psB[pb], gb_], writes=[zb])
                    k.dma("act", ZR[c0:c0 + 128, g0:g0 + n], z_[:, 0:n], reads=[zb], sem=zsem)
                else:
                    k.dma("sp", z_[:, 0:n], ZR[c0:c0 + 128, g0:g0 + n], writes=[zb], sem=zsem)
                    y_, yb, ysem = yo.next()
                    k.op("dve", lambda e: e.tensor_tensor(out=g_[:, 0:n], in0=psb[pb][:, 0:n], in1=g_[:, 0:n], op=ALU.mult),
                         reads=[psB[pb], gb_], writes=[gb_])
                    k.op("pool", lambda e: e.tensor_tensor(out=y_[:, 0:n], in0=g_[:, 0:n], in1=z_[:, 0:n], op=ALU.add),
                         reads=[gb_, zb], writes=[yb])
                    k.dma("act", YT[c0:c0 + 128, g0:g0 + n], y_[:, 0:n], reads=[yb], sem=ysem)
            proj_fm(aT, aTb, Wm, [i * 128 for i in range(16)], KC, ev, wring=wring)
            k.barrier()
            cur[0] = save
        wres = sb([128, KC, D], BF16)
        b_wres = Buf()
        for g in range(4):
            k.dma("pool", wres[:, :, g * 512:(g + 1) * 512], wslab_src(w_o[l], g * 512, 512), writes=[b_wres], sem="wr%d" % g)
        gn = sb([128, 2, D], F32)
        b_gn = Buf()
        for m in range(2):
            k.dma("sp", gn[:, m, :], GNROW[m:m + 1, :].partition_broadcast(128), writes=[b_gn], sem="gn")
        yin = Ring("pyi", [sb([128, KC, 128], BF16) for _ in range(2)])
        xin = Ring("pxi", [sb([128, D], F32) for _ in range(2)])
        ot = Ring("pot", [sb([128, D], F32) for _ in range(2)])
        junk = sb([128, D], BF16)
        b_junk = Buf()
        st4 = Ring("ps4", [sb([128, 4], F32) for _ in range(4)])
        for t in range(NCH):
            yi, yib, ysem = yin.next()
            k.dma("sp", yi, YT[:, t * 128:(t + 1) * 128].rearrange("(kc p) t -> p kc t", p=128), writes=[yib], sem=ysem)
            xi, xib, xsem = xin.next()
            k.dma("sp", xi, X[t * 128:(t + 1) * 128, :], writes=[xib], sem=xsem)
            o_, ob, osem = ot.next()
            s4, s4b, _ = st4.next()
            base = (t % 2) * 4
            for g in range(4):
                pb = base + g
                for kc in range(KC):
                    k.op("pe", lambda e, pb=pb, kc=kc, g=g, yi=yi: e.matmul(psb[pb], yi[:, kc, :], wres[:, kc, g * 512:(g + 1) * 512],
                         start=(kc == 0), stop=(kc == KC - 1)), reads=[yib, b_wres], writes=[psB[pb]], inc=(kc == KC - 1))
                k.op("act", lambda e, pb=pb, g=g, o_=o_: e.activation(out=o_[:, g * 512:(g + 1) * 512], in_=psb[pb], func=AF.Copy),
                     reads=[psB[pb]], writes=[ob])
            finish_resid(t, o_, ob, xi, xib, gn, b_gn, s4, s4b, junk, b_junk, osem, last=False)
        k.barrier()

    def finish_resid(t, o_, ob, xi, xib, gn, b_gn, s4, s4b, junk, b_junk, osem, last):
        m = 0 if t >= NCC else 1
        k.op("act", lambda e: e.activation(out=junk, in_=o_, func=AF.Square, accum_out=s4[:, 0:1]), reads=[ob], writes=[b_junk, s4b])
        k.op("act", lambda e: e.activation(out=s4[:, 1:2], in_=s4[:, 0:1], func=AF.Sqrt, scale=1.0 / D, bias=EPS), reads=[s4b], writes=[s4b])
        k.op("dve", lambda e: e.reciprocal(out=s4[:, 2:3], in_=s4[:, 1:2]), reads=[s4b], writes=[s4b])
        k.op("dve", lambda e: e.scalar_tensor_tensor(out=o_, in0=o_, scalar=s4[:, 2:3], in1=gn[:, m, :], op0=ALU.mult, op1=ALU.mult),
             reads=[ob, s4b, b_gn], writes=[ob])
        k.op("pool", lambda e: e.tensor_tensor(out=o_, in0=o_, in1=xi, op=ALU.add), reads=[ob, xib], writes=[ob])
        if last and t >= NCC:
            k.dma("pool", y_out[(t - NCC) * 128:(t - NCC + 1) * 128, :], o_, reads=[ob], sem=osem + "s")
        else:
            k.dma("pool", X[t * 128:(t + 1) * 128, :], o_, reads=[ob], sem=osem + "s")

    def phase_ffn_up(l, hT, hTb):
        save = cur[0]
        wring = Ring("wu", [sb([128, KC, 128], BF16) for _ in range(2)])
        cw = sb([128, 4, 2 * FC], F32)
        b_cw = Buf()
        for kk in range(4):
            for half in range(2):
                src = (fconv_w[l, kk, half * FF:(half + 1) * FF] if kk < 3 else fconv_b[l, half * FF:(half + 1) * FF])
                k.dma("sp", cw[:, kk, :].rearrange("p (c two) -> p c two", two=2)[:, :, half], src.rearrange("(c p) -> p c", p=128),
                      writes=[b_cw], sem="fcw", allow_slow_non_contiguous=True)
        cols = []
        for c in range(FC):
            cols += [c * 128, FF + c * 128]
        build.conv_chunks(hT, hTb, w_up[l], cols, cw, wring, None, mode="ffn", cwb=b_cw)
        k.barrier()
        cur[0] = save

    def phase_ffn_down(l, last):
        cur[0] = persist_end
        HALF = 1024
        ssq = sb([128, NCH, 2], F32)
        ssq2 = sb([128, NCH, 2], F32)
        b_ssq = Buf()
        ssq_end = [cur[0]]
        wres = sb([128, FC, HALF], BF16)
        junk = sb([128, 512], BF16)
        b_junk = Buf()
        ain = Ring("dai", [sb([128, FC, 128], BF16) for _ in range(2)])
        ost = Ring("dos", [sb([128, HALF], F32) for _ in range(2)])
        b_wres = Buf()
        for hcol in range(2):
            for g in range(2):
                k.dma("pool", wres[:, :, g * 512:(g + 1) * 512],
                      w_dn[l][:, hcol * HALF + g * 512:hcol * HALF + (g + 1) * 512].rearrange("(kc p) n -> p kc n", p=128),
                      writes=[b_wres], sem="dw%d" % g)
            for t in range(NCH):
                ai, aib, asem = ain.next()
                k.dma("sp", ai, ACTT[:, t * 128:(t + 1) * 128].rearrange("(kc p) t -> p kc t", p=128), writes=[aib], sem=asem)
                o_, ob, osem = ost.next()
                for g in range(2):
                    pb = (t % 2) * 2 + g
                    for kc in range(FC):
                        k.op("pe", lambda e, pb=pb, kc=kc, g=g, ai=ai: e.matmul(psb[pb], ai[:, kc, :], wres[:, kc, g * 512:(g + 1) * 512],
                             start=(kc == 0), stop=(kc == FC - 1)), reads=[aib, b_wres], writes=[psB[pb]], inc=(kc == FC - 1))
                    k.op("act", lambda e, pb=pb, g=g, o_=o_: e.activation(out=o_[:, g * 512:(g + 1) * 512], in_=psb[pb], func=AF.Copy),
                         reads=[psB[pb]], writes=[ob])
                k.op("act", lambda e, o_=o_, t=t, hcol=hcol: e.activation(out=junk, in_=o_[:, 0:512], func=AF.Square,
                     accum_out=ssq[:, t, hcol:hcol + 1]), reads=[ob], writes=[b_junk, b_ssq])
                k.op("act", lambda e, o_=o_, t=t, hcol=hcol: e.activation(out=junk, in_=o_[:, 512:1024], func=AF.Square,
                     accum_out=ssq2[:, t, hcol:hcol + 1]), reads=[ob], writes=[b_junk, b_ssq])
                k.dma("pool", FO[t * 128:(t + 1) * 128, hcol * HALF:(hcol + 1) * HALF], o_, reads=[ob], sem=osem)
        k.barrier()
        cur[0] = ssq_end[0]
        gn = sb([128, 2, D], F32)
        b_gn = Buf()
        for m in range(2):
            k.dma("sp", gn[:, m, :], GNROW[2 + m:3 + m, :].partition_broadcast(128), writes=[b_gn], sem="gn")
        fin = Ring("dfi", [sb([128, D], F32) for _ in range(2)])
        xin = Ring("dxi", [sb([128, D], F32) for _ in range(2)])
        st4 = Ring("ds4", [sb([128, 4], F32) for _ in range(4)])
        for t in range(NCH):
            o_, ob, osem = fin.next()
            k.dma("sp", o_, FO[t * 128:(t + 1) * 128, :], writes=[ob], sem=osem + "l")
            xi, xib, xsem = xin.next()
            k.dma("sp", xi, X[t * 128:(t + 1) * 128, :], writes=[xib], sem=xsem)
            s4, s4b, _ = st4.next()
            m = 0 if t >= NCC else 1
            k.op("dve", lambda e, s4=s4, t=t: e.tensor_tensor(out=s4[:, 0:2], in0=ssq[:, t, :], in1=ssq2[:, t, :], op=ALU.add), reads=[b_ssq], writes=[s4b])
            k.op("dve", lambda e, s4=s4: e.tensor_tensor(out=s4[:, 0:1], in0=s4[:, 0:1], in1=s4[:, 1:2], op=ALU.add), reads=[s4b], writes=[s4b])
            k.op("act", lambda e, s4=s4: e.activation(out=s4[:, 1:2], in_=s4[:, 0:1], func=AF.Sqrt, scale=1.0 / D, bias=EPS), reads=[s4b], writes=[s4b])
            k.op("dve", lambda e, s4=s4: e.reciprocal(out=s4[:, 2:3], in_=s4[:, 1:2]), reads=[s4b], writes=[s4b])
            k.op("dve", lambda e, s4=s4, o_=o_, m=m: e.scalar_tensor_tensor(out=o_, in0=o_, scalar=s4[:, 2:3], in1=gn[:, m, :], op0=ALU.mult, op1=ALU.mult),
                 reads=[ob, s4b, b_gn], writes=[ob])
            k.op("pool", lambda e, o_=o_, xi=xi: e.tensor_tensor(out=o_, in0=o_, in1=xi, op=ALU.add), reads=[ob, xib], writes=[ob])
            if last and t >= NCC:
                k.dma("pool", y_out[(t - NCC) * 128:(t - NCC + 1) * 128, :], o_, reads=[ob], sem=osem)
            else:
                k.dma("pool", X[t * 128:(t + 1) * 128, :], o_, reads=[ob], sem=osem)
        k.barrier()


    for l in range(L):
        last = l == L - 1
        phase_adaln(l)
        if stop == "adaln":
            break
        cur[0] = persist_end
        hT, hTb = norm_to_hT(0)
        if stop == "norm":
            break
        phase_inproj(l, hT, hTb)
        if stop == "inproj":
            break
        TM, b_TM = phase_gates(l)
        if stop == "gates":
            break
        phase_scan(l, TM, b_TM)
        if stop == "scan":
            break
        phase_outproj(l, last)
        if stop == "outproj":
            break
        cur[0] = persist_end
        hT, hTb = norm_to_hT(1)
        phase_ffn_up(l, hT, hTb)
        if stop == "ffnup":
            break
        phase_ffn_down(l, last)
    k.barrier()
    k.emit()
    return nc


def make_in_maps(inputs, cfg):
    n = cfg["NCORES"]
    hc = host_consts(cfg["TL"])
    maps = []
    shared = {kk: np.ascontiguousarray(inputs[kk]) for kk in (
        "w_ada", "b_ada", "norm_w", "w_in", "mlstm_conv_w", "mlstm_conv_b", "mlstm_gate_b", "ret_decay_exp",
        "head_norm_w", "w_ret_out", "w_mlstm_out", "w_o", "w_up", "ffn_conv_w", "ffn_conv_b", "w_down")}
    for b in range(n):
        m = dict(shared)
        m["x"] = np.ascontiguousarray(inputs["x"][b])
        m["ctx"] = np.ascontiguousarray(inputs["ctx"][b])
        m["cvec"] = np.ascontiguousarray(np.stack([inputs["c"][b], inputs["c_ctx"]], axis=0))
        m.update(hc)
        maps.append(m)
    return maps


def kernel(**inputs):
    cfg = dict(CFG)
    nc = build(cfg)
    maps = make_in_maps(inputs, cfg)
    res = run_bass_kernel_spmd(nc, maps, core_ids=list(range(cfg["NCORES"])))
    return np.stack([res.results[b]["y"] for b in range(cfg["NCORES"])], axis=0).astype(np.float32)
```
